# Optimizing a Trainium2 kernel written in Bass

```python
import jax
import jax.numpy as jnp
from jax import lax
import numpy as np

D_MODEL = 1024
BATCH = 2
SEQ = 8192
DEPTH = 4
DEC_BATCH = 32
DEC_SEQ = 8
PAST_LEN = 8192
PAGE_SIZE = 128

N_A_LAYERS = DEPTH // 2
N_B_LAYERS = DEPTH - N_A_LAYERS
HEAD_DIM = 64
N_HEADS = D_MODEL // HEAD_DIM
D_FF = ((8 * D_MODEL // 3 + 127) // 128) * 128
D_DECAY_LORA = max(32, int(round(1.8 * D_MODEL ** 0.5 / 32)) * 32)
D_AAA_LORA = max(32, int(round(1.8 * D_MODEL ** 0.5 / 32)) * 32)
D_MV_LORA = max(32, int(round(1.3 * D_MODEL ** 0.5 / 32)) * 32)
D_GATE_LORA = max(32, int(round(0.6 * D_MODEL ** 0.8 / 32)) * 32)
N_MIX = 6
Q_BLOCK = 128
NORM_EPS = 1e-6
GN_EPS = 64e-5
FORGET_BIAS = 3.0
MACARON_W = 0.5
NEG_INF = -1e30

kernel_name = 'yoco_rwkv7_fox_macaron_step'


def rms_norm(x, g):
    xf = x.astype(jnp.float32)
    y = xf * lax.rsqrt(jnp.mean(xf * xf, axis=-1, keepdims=True) + NORM_EPS)
    return (y * g.astype(jnp.float32)).astype(x.dtype)


def modulate(x, g_pre, m):
    return rms_norm(x, g_pre) * (1 + m[:, None, 1]) + m[:, None, 0]


def gated_post(y, g_post, m):
    return m[:, None, 2] * rms_norm(y, g_post)


def swiglu(h, w_in, w_out):
    gate, up = jnp.split(h @ w_in, 2, axis=-1)
    return (jax.nn.silu(gate) * up) @ w_out


def wkv_step(S, inp):
    r, w, k, v, kk, a = inp
    s_kk = jnp.einsum('bhij,bhj->bhi', S, kk)
    S = S * w[:, :, None, :] - s_kk[..., None] * (a * kk)[:, :, None, :] + v[..., None] * k[:, :, None, :]
    return S, jnp.einsum('bhij,bhj->bhi', S, r)


def rwkv7_time_mix(h, h_last, S0, v_first, i, p):
    B, T, _ = h.shape
    heads = lambda t: t.reshape(B, T, N_HEADS, HEAD_DIM)
    h_prev = jnp.concatenate([h_last[:, None, :].astype(h.dtype), h[:, :-1]], axis=1)
    xx = h_prev - h
    mu = p['mu_a'][i]
    xr, xw, xk, xv, xa, xg = [h + xx * mu[j] for j in range(N_MIX)]
    w_rkv = p['w_rkv_a'][i]
    r = xr @ w_rkv[0]
    k = xk @ w_rkv[1]
    v = xv @ w_rkv[2]
    w_log = -jax.nn.softplus(-(p['w0_a'][i] + jnp.tanh(xw @ p['w1_a'][i]) @ p['w2_a'][i])) - 0.5
    a = jax.nn.sigmoid(p['a0_a'][i] + (xa @ p['a1_a'][i]) @ p['a2_a'][i])
    g = jax.nn.sigmoid(xg @ p['g1_a'][i]) @ p['g2_a'][i]
    if i == 0:
        v_first = v
    else:
        j = i - 1
        v = v + (v_first - v) * jax.nn.sigmoid(p['v0_a'][j] + (xv @ p['v1_a'][j]) @ p['v2_a'][j])
    kk = heads(k * p['kk_a'][i]).astype(jnp.float32)
    kk = kk * lax.rsqrt(jnp.maximum(jnp.sum(kk * kk, axis=-1, keepdims=True), 1e-24))
    k = k * (1 + (a - 1) * p['ka_a'][i])
    r_h = heads(r).astype(jnp.float32)
    k_h = heads(k).astype(jnp.float32)
    v_h = heads(v).astype(jnp.float32)
    a_h = heads(a).astype(jnp.float32)
    w_h = jnp.exp(-jnp.exp(heads(w_log).astype(jnp.float32)))
    xs = tuple(jnp.moveaxis(t, 1, 0) for t in (r_h, w_h, k_h, v_h, kk, a_h))
    S, o = lax.scan(wkv_step, S0.astype(jnp.float32), xs)
    o = jnp.moveaxis(o, 0, 1)
    mean = jnp.mean(o, axis=-1, keepdims=True)
    var = jnp.mean(jnp.square(o - mean), axis=-1, keepdims=True)
    o = ((o - mean) * lax.rsqrt(var + GN_EPS)).reshape(B, T, D_MODEL)
    o = o * p['lnx_w_a'][i].astype(jnp.float32) + p['lnx_b_a'][i].astype(jnp.float32)
    bonus = jnp.sum(r_h * k_h * p['rk_a'][i].astype(jnp.float32), axis=-1, keepdims=True) * v_h
    o = (o + bonus.reshape(B, T, D_MODEL)).astype(h.dtype)
    y = (o * g) @ p['w_o_a'][i]
    return y, h[:, -1], S, v_first


def fox_attend(q, cq, q_pos, k, v, ck, k_pos):
    s = jnp.einsum('bqhd,bkhd->bhqk', q, k).astype(jnp.float32)
    bias = jnp.swapaxes(cq, 1, 2)[..., None] - jnp.swapaxes(ck, 1, 2)[:, :, None, :]
    mask = (k_pos[None, :] <= q_pos[:, None])[None, None]
    s = jnp.where(mask, s + bias, NEG_INF)
    prob = jax.nn.softmax(s, axis=-1)
    return jnp.einsum('bhqk,bkhd->bqhd', prob.astype(v.dtype), v)


def fox_attention(q, cq, q_pos, k, v, ck, k_pos):
    B, T = q.shape[:2]
    if T <= Q_BLOCK:
        return fox_attend(q, cq, q_pos, k, v, ck, k_pos)
    nb = -(-T // Q_BLOCK)
    pad = nb * Q_BLOCK - T
    qp = jnp.pad(q, ((0, 0), (0, pad), (0, 0), (0, 0)))
    cqp = jnp.pad(cq, ((0, 0), (0, pad), (0, 0)))
    posp = jnp.pad(q_pos, (0, pad), constant_values=-1)
    qb = jnp.swapaxes(qp.reshape(B, nb, Q_BLOCK, N_HEADS, HEAD_DIM), 0, 1)
    cqb = jnp.swapaxes(cqp.reshape(B, nb, Q_BLOCK, N_HEADS), 0, 1)
    pb = posp.reshape(nb, Q_BLOCK)
    out = lax.map(lambda blk: fox_attend(blk[0], blk[1], blk[2], k, v, ck, k_pos), (qb, cqb, pb))
    return jnp.swapaxes(out, 0, 1).reshape(B, nb * Q_BLOCK, N_HEADS, HEAD_DIM)[:, :T]


def shared_kv(x, past, p):
    B, T, _ = x.shape
    s = rms_norm(x, p['g_kv'])
    k_new, v_new = jnp.split(s @ p['w_kv'], 2, axis=-1)
    k_new = k_new.reshape(B, T, N_HEADS, HEAD_DIM)
    v_new = v_new.reshape(B, T, N_HEADS, HEAD_DIM)
    logf_new = jax.nn.log_sigmoid((s @ p['w_f'] + p['b_f']).astype(jnp.float32))
    if past is None:
        k_all, v_all, logf_all = k_new, v_new, logf_new
    else:
        k_past, v_past, logf_past = past
        k_all = jnp.concatenate([k_past.astype(k_new.dtype), k_new], axis=1)
        v_all = jnp.concatenate([v_past.astype(v_new.dtype), v_new], axis=1)
        logf_all = jnp.concatenate([logf_past.astype(jnp.float32), logf_new], axis=1)
    n_past = k_all.shape[1] - T
    ck = jnp.cumsum(logf_all, axis=1)
    return dict(k_new=k_new, v_new=v_new, logf_new=logf_new.astype(x.dtype), k_all=k_all, v_all=v_all,
                ck=ck, cq=ck[:, n_past:], q_pos=n_past + jnp.arange(T), k_pos=jnp.arange(n_past + T))


def fox_layer(h, kv, j, p):
    B, T, _ = h.shape
    q = (h @ p['w_q_b'][j]).reshape(B, T, N_HEADS, HEAD_DIM) * (HEAD_DIM ** -0.5)
    o = fox_attention(q, kv['cq'], kv['q_pos'], kv['k_all'], kv['v_all'], kv['ck'], kv['k_pos'])
    return o.reshape(B, T, D_MODEL) @ p['w_o_b'][j]


def trunk(x, c, wkv0, shift0, past, p):
    B = x.shape[0]
    c_act = jax.nn.silu(c)
    v_first = None
    kv = None
    wkv_new, shift_new = [], []
    for l in range(DEPTH):
        if l == N_A_LAYERS:
            kv = shared_kv(x, past, p)
        mods = (c_act @ p['w_mod'][l] + p['b_mod'][l]).reshape(B, 3, 3, D_MODEL)
        gn = p['g_norm'][l]
        m = mods[:, 0]
        x = x + MACARON_W * gated_post(swiglu(modulate(x, gn[0, 0], m), p['w_ffn_in'][l, 0], p['w_ffn_out'][l, 0]), gn[0, 1], m)
        m = mods[:, 1]
        h = modulate(x, gn[1, 0], m)
        if l < N_A_LAYERS:
            y, h_last, S, v_first = rwkv7_time_mix(h, shift0[l], wkv0[l], v_first, l, p)
            wkv_new.append(S)
            shift_new.append(h_last)
        else:
            y = fox_layer(h, kv, l - N_A_LAYERS, p)
        x = x + gated_post(y, gn[1, 1], m)
        m = mods[:, 2]
        x = x + MACARON_W * gated_post(swiglu(modulate(x, gn[2, 0], m), p['w_ffn_in'][l, 1], p['w_ffn_out'][l, 1]), gn[2, 1], m)
    return x, jnp.stack(wkv_new), jnp.stack(shift_new), kv['k_new'], kv['v_new'], kv['logf_new']


def setup_inputs(seed: int = 0) -> dict:
    key = jax.random.key(seed)
    keys = iter(jax.random.split(key, 64))

    def nrm(shape, scale):
        return jax.random.normal(next(keys), shape, jnp.float32) * scale

    D = D_MODEL
    sd = D ** -0.5
    n_pages = PAST_LEN // PAGE_SIZE
    n_used = DEC_BATCH * n_pages
    n_pool = n_used + (n_used + 3) // 4
    page_table = jax.random.permutation(next(keys), n_pool)[:n_used].reshape(DEC_BATCH, n_pages).astype(jnp.int32)
    return {
        'x_prompt': nrm((BATCH, SEQ, D), 1.0),
        'x_sample': nrm((DEC_BATCH, DEC_SEQ, D), 1.0),
        'state_wkv': nrm((N_A_LAYERS, DEC_BATCH, N_HEADS, HEAD_DIM, HEAD_DIM), 0.3),
        'state_shift': nrm((N_A_LAYERS, DEC_BATCH, D), 1.0),
        'cache_k': nrm((n_pool, PAGE_SIZE, N_HEADS, HEAD_DIM), 1.0),
        'cache_v': nrm((n_pool, PAGE_SIZE, N_HEADS, HEAD_DIM), 1.0),
        'cache_logf': jax.nn.log_sigmoid(FORGET_BIAS + nrm((n_pool, PAGE_SIZE, N_HEADS), 0.5)),
        'page_table': page_table,
        'c_prompt': nrm((BATCH, D), 1.0),
        'c_sample': nrm((DEC_BATCH, D), 1.0),
        'w_mod': nrm((DEPTH, D, 9 * D), 0.5 * sd),
        'b_mod': nrm((DEPTH, 9 * D), 0.01),
        'g_norm': 1.0 + nrm((DEPTH, 3, 2, D), 0.02),
        'w_ffn_in': nrm((DEPTH, 2, D, 2 * D_FF), sd),
        'w_ffn_out': nrm((DEPTH, 2, D_FF, D), D_FF ** -0.5),
        'mu_a': jax.random.uniform(next(keys), (N_A_LAYERS, N_MIX, D), jnp.float32),
        'w_rkv_a': nrm((N_A_LAYERS, 3, D, D), sd),
        'w_o_a': nrm((N_A_LAYERS, D, D), sd),
        'w0_a': nrm((N_A_LAYERS, D), 0.5),
        'w1_a': nrm((N_A_LAYERS, D, D_DECAY_LORA), 0.1 * sd),
        'w2_a': nrm((N_A_LAYERS, D_DECAY_LORA, D), 0.1 * D_DECAY_LORA ** -0.5),
        'a0_a': nrm((N_A_LAYERS, D), 0.1),
        'a1_a': nrm((N_A_LAYERS, D, D_AAA_LORA), 0.5 * sd),
        'a2_a': nrm((N_A_LAYERS, D_AAA_LORA, D), 0.5 * D_AAA_LORA ** -0.5),
        'v0_a': 1.0 + nrm((N_A_LAYERS - 1, D), 0.1),
        'v1_a': nrm((N_A_LAYERS - 1, D, D_MV_LORA), 0.5 * sd),
        'v2_a': nrm((N_A_LAYERS - 1, D_MV_LORA, D), 0.5 * D_MV_LORA ** -0.5),
        'g1_a': nrm((N_A_LAYERS, D, D_GATE_LORA), sd),
        'g2_a': nrm((N_A_LAYERS, D_GATE_LORA, D), D_GATE_LORA ** -0.5),
        'kk_a': 0.85 + nrm((N_A_LAYERS, D), 0.05),
        'ka_a': 1.0 + nrm((N_A_LAYERS, D), 0.05),
        'rk_a': nrm((N_A_LAYERS, N_HEADS, HEAD_DIM), 0.1),
        'lnx_w_a': 1.0 + nrm((N_A_LAYERS, D), 0.02),
        'lnx_b_a': nrm((N_A_LAYERS, D), 0.01),
        'g_kv': 1.0 + nrm((D,), 0.02),
        'w_kv': nrm((D, 2 * D), sd),
        'w_f': nrm((D, N_HEADS), 0.1 * sd),
        'b_f': FORGET_BIAS + nrm((N_HEADS,), 0.1),
        'w_q_b': nrm((N_B_LAYERS, D, D), sd),
        'w_o_b': nrm((N_B_LAYERS, D, D), sd),
    }


def reference(x_prompt, x_sample, state_wkv, state_shift, cache_k, cache_v, cache_logf, page_table, c_prompt, c_sample,
              w_mod, b_mod, g_norm, w_ffn_in, w_ffn_out, mu_a, w_rkv_a, w_o_a, w0_a, w1_a, w2_a, a0_a, a1_a, a2_a,
              v0_a, v1_a, v2_a, g1_a, g2_a, kk_a, ka_a, rk_a, lnx_w_a, lnx_b_a, g_kv, w_kv, w_f, b_f, w_q_b, w_o_b):
    p = dict(w_mod=w_mod, b_mod=b_mod, g_norm=g_norm, w_ffn_in=w_ffn_in, w_ffn_out=w_ffn_out, mu_a=mu_a,
             w_rkv_a=w_rkv_a, w_o_a=w_o_a, w0_a=w0_a, w1_a=w1_a, w2_a=w2_a, a0_a=a0_a, a1_a=a1_a, a2_a=a2_a,
             v0_a=v0_a, v1_a=v1_a, v2_a=v2_a, g1_a=g1_a, g2_a=g2_a, kk_a=kk_a, ka_a=ka_a, rk_a=rk_a,
             lnx_w_a=lnx_w_a, lnx_b_a=lnx_b_a, g_kv=g_kv, w_kv=w_kv, w_f=w_f, b_f=b_f, w_q_b=w_q_b, w_o_b=w_o_b)
    B = x_prompt.shape[0]
    wkv0 = jnp.zeros((N_A_LAYERS, B, N_HEADS, HEAD_DIM, HEAD_DIM), jnp.float32)
    shift0 = jnp.zeros((N_A_LAYERS, B, D_MODEL), x_prompt.dtype)
    y_prompt, wkv_p, shift_p, k_p, v_p, logf_p = trunk(x_prompt, c_prompt, wkv0, shift0, None, p)
    DB, n_pages = page_table.shape
    past_len = n_pages * PAGE_SIZE
    past = (cache_k[page_table].reshape(DB, past_len, N_HEADS, HEAD_DIM),
            cache_v[page_table].reshape(DB, past_len, N_HEADS, HEAD_DIM),
            cache_logf[page_table].reshape(DB, past_len, N_HEADS))
    y_sample, wkv_s, shift_s, k_s, v_s, logf_s = trunk(x_sample, c_sample, state_wkv, state_shift, past, p)
    return (y_prompt, y_sample, wkv_p.astype(x_prompt.dtype), shift_p, k_p, v_p, logf_p,
            wkv_s.astype(state_wkv.dtype), shift_s, k_s, v_s, logf_s)
```

```python
import contextlib
import numpy as np
import concourse.bass as bass
import concourse.mybir as mybir
from concourse.bass_utils import run_bass_kernel_spmd

F32 = mybir.dt.float32
BF16 = mybir.dt.bfloat16
I32 = mybir.dt.int32
AF = mybir.ActivationFunctionType
ALU = mybir.AluOpType

P = 128
D = 1024
KC = 8
DFF = 2816
FC = 22
NH = 16
HD = 64
NSEQ = 5
NS = 4
SLOT = 32
DEC = 8
NORM_EPS = 1e-6
GN_EPS = 64e-5
PAGE = 128


class Res:
    __slots__ = ("w", "r", "name")

    def __init__(self, name=""):
        self.w = {}
        self.r = {}
        self.name = name


class KB:
    def __init__(self, nc, nring=20, same_sync=True):
        self.nc = nc
        self.es = contextlib.ExitStack()
        self.same = same_sync
        self.eng = {"pe": nc.tensor, "act": nc.scalar, "dve": nc.vector, "pool": nc.gpsimd, "sp": nc.sync}
        self.csem = {e: self.es.enter_context(nc.semaphore(f"c_{e}")) for e in ("pe", "act", "dve", "pool")}
        self.cnt = {e: 0 for e in self.csem}
        self.ring = {q: [self.es.enter_context(nc.semaphore(f"r_{q}{i}")) for i in range(nring)] for q in ("sp", "pool")}
        self.ruse = {q: [0] * nring for q in self.ring}
        self.rnext = {q: 0 for q in self.ring}
        self.seen = {e: {} for e in self.eng}
        self.nbuf = 0
        self.psum = []
        self.psn = 0
        self.psrot = list(range(8))

    def sb(self, shape, dt, name=None):
        self.nbuf += 1
        return self.es.enter_context(self.nc.sbuf_tensor(f"{name or 'sb'}_{self.nbuf}", list(shape), dt))

    def init_psum(self):
        for i in range(8):
            t = self.es.enter_context(self.nc.psum_tensor(f"ps{i}", [P, 512], F32))
            self.psum.append((t, Res(f"ps{i}")))

    def ps(self):
        t = self.psum[self.psrot[self.psn % len(self.psrot)]]
        self.psn += 1
        return t

    def dram(self, name, shape, dt):
        return self.nc.dram_tensor(name, list(shape), dt).ap()

    def _collect(self, e, reads, writes):
        waits = {}

        def add(d):
            for k, (sem, v) in d.items():
                if k == e and (e == "pe" or not self.same):
                    continue
                if k not in waits or waits[k][1] < v:
                    waits[k] = (sem, v)
        for r in reads:
            add(r.w)
        for w in writes:
            add(w.w)
            add(w.r)
        return waits

    def _wait(self, e, waits):
        sn = self.seen[e]
        for k, (sem, v) in waits.items():
            if sn.get(k, 0) >= v:
                continue
            self.eng[e].wait_ge(sem, v)
            sn[k] = v

    @staticmethod
    def _register(key, ev, reads, writes):
        for r in reads:
            r.r[key] = ev
        for w in writes:
            w.w = {key: ev}
            w.r = {}

    def op(self, e, fn, reads=(), writes=()):
        self._wait(e, self._collect(e, reads, writes))
        ins = fn(self.eng[e])
        self.cnt[e] += 1
        ins.then_inc(self.csem[e], 1)
        self._register(e, (self.csem[e], self.cnt[e]), reads, writes)

    def mm(self, out_ap, out_res, items, transpose=False, start=True, stop=True):
        n = len(items)
        allreads = []
        ins = None
        for i, (l, r, rd) in enumerate(items):
            self._wait("pe", self._collect("pe", rd, [out_res] if i == 0 else []))
            if transpose:
                ins = self.nc.tensor.transpose(out_ap, l, r)
            else:
                ins = self.nc.tensor.matmul(out_ap, lhsT=l, rhs=r, start=(start and i == 0), stop=(stop and i == n - 1))
            allreads.extend(rd)
        self.cnt["pe"] += 1
        ins.then_inc(self.csem["pe"], 1)
        self._register("pe", (self.csem["pe"], self.cnt["pe"]), allreads, [out_res])

    def dma(self, q, out_ap=None, in_ap=None, reads=(), writes=(), fn=None):
        waits = self._collect(q, reads, writes)
        ring = self.ring[q]
        i = self.rnext[q]
        self.rnext[q] = (i + 1) % len(ring)
        key = f"{q}{i}"
        if self.ruse[q][i] > 0:
            v = 16 * self.ruse[q][i]
            if key not in waits or waits[key][1] < v:
                waits[key] = (ring[i], v)
        self._wait(q, waits)
        if fn is None:
            ins = self.eng[q].dma_start(out=out_ap, in_=in_ap)
        else:
            ins = fn(self.eng[q])
        self.ruse[q][i] += 1
        ins.then_inc(ring[i], 16)
        self._register(key, (ring[i], 16 * self.ruse[q][i]), reads, writes)

    def finish(self):
        for q in self.ring:
            for i, sem in enumerate(self.ring[q]):
                if self.ruse[q][i] > 0:
                    self.nc.sync.wait_ge(sem, 16 * self.ruse[q][i])
        for e in self.csem:
            if self.cnt[e] > 0:
                self.nc.sync.wait_ge(self.csem[e], self.cnt[e])


class Rot:
    def __init__(self, kb, n, shape, dt, name, nres=1):
        self.tiles = [(kb.sb(shape, dt, name), [Res(f"{name}{i}_{j}") for j in range(nres)]) for i in range(n)]
        self.i = 0

    def next(self):
        t = self.tiles[self.i % len(self.tiles)]
        self.i += 1
        return t


class Cfg:
    def __init__(self, seq=8192, npages=64, npool=2560, debug=None, depth=4):
        self.seq = seq
        self.npages = npages
        self.npool = npool
        self.ntok = seq + NS * SLOT
        self.debug = debug
        self.depth = depth
        self.n_a = depth // 2
        self.tiles = []
        for t in range(seq // 512):
            self.tiles.append((t * 512, 512, [(0, 512, 0)], t))
        self.tiles.append((seq, NS * SLOT, [(s * SLOT, SLOT, 1 + s) for s in range(NS)], seq // 512))


class Prog:
    def __init__(self, cfg):
        self.cfg = cfg
        nc = self.nc = bass.Bass("TRN2", target_bir_lowering=False)
        self.kb = KB(nc)
        self.inputs = {}
        self.outputs = {}
        self.scope_stack = None

    def inp(self, name, shape, dt=F32):
        ap = self.nc.dram_tensor(name, list(shape), dt, kind="ExternalInput").ap()
        self.inputs[name] = ap
        return ap

    def outp(self, name, shape, dt=F32):
        ap = self.nc.dram_tensor(name, list(shape), dt, kind="ExternalOutput").ap()
        self.outputs[name] = ap
        return ap

    @contextlib.contextmanager
    def scope(self):
        kb = self.kb
        outer = kb.es
        inner = contextlib.ExitStack()
        kb.es = inner
        try:
            yield
            self.barrier()
        finally:
            kb.es = outer
            inner.close()

    def barrier(self):
        kb = self.kb
        evs = {}
        for e in kb.csem:
            if kb.cnt[e] > 0:
                evs[e] = (kb.csem[e], kb.cnt[e])
        for q in kb.ring:
            for i, sem in enumerate(kb.ring[q]):
                if kb.ruse[q][i] > 0:
                    evs[f"{q}{i}"] = (sem, 16 * kb.ruse[q][i])
        for e in kb.eng:
            for k, (sem, v) in evs.items():
                if kb.seen[e].get(k, 0) >= v:
                    continue
                kb.eng[e].wait_ge(sem, v)
                kb.seen[e][k] = v

    def load_w(self, dst_tile, dst_res, src_ap, nsplit):
        X = src_ap.shape[1]
        step = (X + nsplit - 1) // nsplit
        for x0 in range(0, X, step):
            x1 = min(X, x0 + step)
            self.kb.dma("pool", dst_tile[:, x0:x1, :], src_ap[:, x0:x1, :], writes=[dst_res])

    def consts(self):
        kb = self.kb
        self.ones_bf = kb.sb([P, P], BF16, "ones")
        self.r_const = Res("const")
        kb.op("dve", lambda e: e.memset(self.ones_bf[:], 1.0), writes=[self.r_const])
        self.eps_t = kb.sb([P, 1], F32, "eps")
        kb.op("dve", lambda e: e.memset(self.eps_t[:], NORM_EPS), writes=[self.r_const])

    def sumsq_rstd(self, src_chunks, src_res, w, sq_pool, rstd_tile, rstd_res, nchunk=KC, eps_ap=None, scale=1.0 / D):
        kb = self.kb
        sq, sqres = sq_pool.next()
        for c in range(nchunk):
            kb.op("act", lambda e, c=c: e.activation(sq[:, c, 0:w], src_chunks(c), AF.Square),
                  reads=[src_res(c)], writes=[sqres[c]])
        pt, pr = kb.ps()
        kb.mm(pt[:, 0:w], pr, [(self.ones_bf[:], sq[:, c, 0:w], [sqres[c], self.r_const]) for c in range(nchunk)])
        kb.op("act", lambda e: e.activation(rstd_tile[:, 0:w], pt[:, 0:w], AF.Sqrt,
                                            bias=(eps_ap if eps_ap is not None else self.eps_t[:, 0:1]), scale=scale),
              reads=[pr, self.r_const], writes=[rstd_res])
        kb.op("dve", lambda e: e.reciprocal(rstd_tile[:, 0:w], rstd_tile[:, 0:w]), reads=[rstd_res], writes=[rstd_res])

    def setup(self):
        cfg, kb, nc = self.cfg, self.kb, self.nc
        L = cfg.depth
        NT = cfg.ntok
        kb.init_psum()
        self.consts()
        self.xin = self.inp("xin", [P, KC, NT])
        self.cT = self.inp("cT", [P, KC * NSEQ])
        self.w_mod = self.inp("w_mod", [L, P, KC, 9 * D])
        self.b_mod = self.inp("b_mod", [P, L * 72])
        self.g_norm = self.inp("g_norm", [P, L * 6 * KC])
        self.w_in = self.inp("w_in", [L, 2, P, KC, 2 * DFF])
        self.w_out = self.inp("w_out", [L, 2, P, FC, D])
        self.xs = kb.dram("xs", [P, KC, NT], F32)
        self.hid = kb.dram("hid", [P, FC, NT], BF16)
        self.xs_res = [Res(f"xs{i}") for i in range(len(cfg.tiles))]
        self.hid_res = [Res(f"hid{i}") for i in range(len(cfg.tiles))]
        self.modv = kb.sb([P, L, 72, NSEQ], F32, "modv")
        self.AV = kb.sb([P, L * 3, KC, NSEQ], F32, "AV")
        self.CV = kb.sb([P, L * 3, KC, NSEQ], F32, "CV")
        self.gn = kb.sb([P, L * 6, KC], F32, "gn")
        self.r_mod = Res("mod")
        self.first_x = True
        self.compute_mods()

    def compute_mods(self):
        cfg, kb = self.cfg, self.kb
        L = cfg.depth
        with self.scope():
            cf = kb.sb([P, KC * NSEQ], F32, "cf")
            cb = kb.sb([P, KC, NSEQ], BF16, "cb")
            bm = kb.sb([P, L * 72], F32, "bm")
            rc = Res("c")
            kb.dma("sp", cf[:], self.cT[:, :], writes=[rc])
            kb.dma("sp", bm[:], self.b_mod[:, :], writes=[rc])
            kb.dma("sp", self.gn[:].rearrange("p a c -> p (a c)"), self.g_norm[:, :], writes=[self.r_mod])
            kb.op("act", lambda e: e.activation(cb[:].rearrange("p c s -> p (c s)"), cf[:], AF.Silu), reads=[rc], writes=[rc])
            wpool = Rot(kb, 2, [P, KC, 1152], BF16, "wm")
            for l in range(L):
                pt, pr = kb.ps()
                for blk in range(8):
                    wt, wr = wpool.next()
                    self.load_w(wt, wr[0], self.w_mod[l, :, :, blk * 1152:(blk + 1) * 1152], 2)
                    for o in range(9):
                        oc = blk * 9 + o
                        kb.mm(pt[:, oc * NSEQ:(oc + 1) * NSEQ], pr,
                              [(wt[:, kc, o * 128:(o + 1) * 128], cb[:, kc, :], [wr[0], rc]) for kc in range(KC)])
                kb.op("dve", lambda e, l=l, pt=pt: e.tensor_tensor(
                    out=self.modv[:, l, :, :], in0=pt[:, 0:72 * NSEQ].rearrange("p (a s) -> p a s", s=NSEQ),
                    in1=bm[:, l * 72:(l + 1) * 72].unsqueeze(2).to_broadcast([P, 72, NSEQ]), op=ALU.add),
                    reads=[pr, rc], writes=[self.r_mod])
                for s in range(3):
                    wres = 1.0 if s == 1 else 0.5
                    gpre = self.gn[:, (l * 3 + s) * 2 + 0, :].unsqueeze(2).to_broadcast([P, KC, NSEQ])
                    gpost = self.gn[:, (l * 3 + s) * 2 + 1, :].unsqueeze(2).to_broadcast([P, KC, NSEQ])
                    sc = self.modv[:, l, (s * 3 + 1) * KC:(s * 3 + 2) * KC, :]
                    gt = self.modv[:, l, (s * 3 + 2) * KC:(s * 3 + 3) * KC, :]
                    kb.op("dve", lambda e, sc=sc, gpre=gpre, l=l, s=s: e.scalar_tensor_tensor(
                        out=self.AV[:, l * 3 + s, :, :], in0=sc, scalar=1.0, in1=gpre, op0=ALU.add, op1=ALU.mult),
                        reads=[self.r_mod], writes=[self.r_mod])
                    kb.op("dve", lambda e, gt=gt, gpost=gpost, l=l, s=s, wres=wres: e.scalar_tensor_tensor(
                        out=self.CV[:, l * 3 + s, :, :], in0=gt, scalar=wres, in1=gpost, op0=ALU.mult, op1=ALU.mult),
                        reads=[self.r_mod], writes=[self.r_mod])

    def vA(self, l, s, c, q):
        return self.AV[:, l * 3 + s, c, q:q + 1]

    def vB(self, l, s, c, q):
        return self.modv[:, l, (s * 3) * KC + c, q:q + 1]

    def vC(self, l, s, c, q):
        return self.CV[:, l * 3 + s, c, q:q + 1]

    def x_src(self):
        if self.first_x:
            return self.xin
        return self.xs

    def load_x(self, pool, tile):
        c0, w, segs, ti = tile
        xt, xr = pool.next()
        self.kb.dma("sp", xt[:, :, 0:w], self.x_src()[:, :, c0:c0 + w], reads=[self.xs_res[ti]], writes=xr)
        return xt, xr

    def modulate(self, l, s, tile, xt, xr, out_fn, out_res, pools, ident=None):
        kb = self.kb
        c0, w, segs, ti = tile
        rstd, rres = pools["rstd"].next()
        self.sumsq_rstd(lambda c: xt[:, c, 0:w], lambda c: xr[c], w, pools["sq"], rstd, rres[0])
        for c in range(KC):
            for (s0, sw, q) in segs:
                tmp, tr = pools["tmp"].next()
                a_ap = self.vA(l, s, c, q) if ident is None else ident(c)
                kb.op("dve", lambda e, c=c, s0=s0, sw=sw, tmp=tmp, a_ap=a_ap: e.scalar_tensor_tensor(
                    out=tmp[:, 0:sw], in0=xt[:, c, s0:s0 + sw], scalar=a_ap, in1=rstd[:, s0:s0 + sw],
                    op0=ALU.mult, op1=ALU.mult), reads=[xr[c], rres[0], self.r_mod], writes=[tr[0]])
                if ident is None:
                    b_ap = self.vB(l, s, c, q)
                    kb.op("act", lambda e, c=c, s0=s0, sw=sw, tmp=tmp, b_ap=b_ap: e.activation(
                        out_fn(c, s0, sw), tmp[:, 0:sw], AF.Identity, bias=b_ap, scale=1.0),
                        reads=[tr[0], self.r_mod], writes=[out_res[c]])
                else:
                    kb.op("act", lambda e, c=c, s0=s0, sw=sw, tmp=tmp: e.activation(
                        out_fn(c, s0, sw), tmp[:, 0:sw], AF.Copy), reads=[tr[0]], writes=[out_res[c]])

    def post_residual(self, l, s, tile, y_fn, y_res, xt, xr, pools):
        kb = self.kb
        c0, w, segs, ti = tile
        rstd, rres = pools["rstd"].next()
        self.sumsq_rstd(y_fn, y_res, w, pools["sq"], rstd, rres[0])
        for c in range(KC):
            for (s0, sw, q) in segs:
                tmp, tr = pools["tmp"].next()
                kb.op("dve", lambda e, c=c, s0=s0, sw=sw, tmp=tmp, q=q: e.scalar_tensor_tensor(
                    out=tmp[:, 0:sw], in0=y_fn(c)[:, s0:s0 + sw], scalar=self.vC(l, s, c, q), in1=rstd[:, s0:s0 + sw],
                    op0=ALU.mult, op1=ALU.mult), reads=[y_res(c), rres[0], self.r_mod], writes=[tr[0]])
                kb.op("pool", lambda e, c=c, s0=s0, sw=sw, tmp=tmp: e.tensor_tensor(
                    out=xt[:, c, s0:s0 + sw], in0=xt[:, c, s0:s0 + sw], in1=tmp[:, 0:sw], op=ALU.add),
                    reads=[tr[0], xr[c]], writes=[xr[c]])
        kb.dma("pool", self.xs[:, :, c0:c0 + w], xt[:, :, 0:w], reads=xr, writes=[self.xs_res[ti]])

    def std_pools(self):
        kb = self.kb
        return {
            "rstd": Rot(kb, 2, [P, 512], F32, "rstd"),
            "sq": Rot(kb, 1, [P, KC, 512], BF16, "sq", nres=KC),
            "tmp": Rot(kb, 3, [P, 512], F32, "tmp"),
        }

    def ffn(self, l, s):
        cfg, kb = self.cfg, self.kb
        fi = 0 if s == 0 else 1
        ntile = len(cfg.tiles)
        with self.scope():
            win = kb.sb([P, KC, 2 * DFF], BF16, "win")
            rw = Res("win")
            self.load_w(win, rw, self.w_in[l, fi], 8)
            pools = self.std_pools()
            xpool = Rot(kb, 2, [P, KC, 512], F32, "xt", nres=KC)
            hpool = Rot(kb, 2, [P, KC, 512], BF16, "ht", nres=KC)
            spool = Rot(kb, 2, [P, 11, 512], BF16, "hs", nres=1)
            gpool = Rot(kb, 3, [P, 512], F32, "sg")
            for tile in cfg.tiles:
                c0, w, segs, ti = tile
                xt, xr = self.load_x(xpool, tile)
                ht, hr = hpool.next()
                self.modulate(l, s, tile, xt, xr, lambda c, s0, sw: ht[:, c, s0:s0 + sw], hr, pools)
                for half in range(2):
                    st, sr = spool.next()
                    for jj in range(11):
                        j = half * 11 + jj
                        pg, rg = kb.ps()
                        kb.mm(pg[:, 0:w], rg, [(win[:, kc, j * 128:(j + 1) * 128], ht[:, kc, 0:w], [rw, hr[kc]]) for kc in range(KC)])
                        pu, ru = kb.ps()
                        kb.mm(pu[:, 0:w], ru, [(win[:, kc, DFF + j * 128:DFF + (j + 1) * 128], ht[:, kc, 0:w], [rw, hr[kc]]) for kc in range(KC)])
                        sg, sgr = gpool.next()
                        kb.op("act", lambda e, sg=sg, pg=pg: e.activation(sg[:, 0:w], pg[:, 0:w], AF.Silu), reads=[rg], writes=[sgr[0]])
                        kb.op("dve", lambda e, sg=sg, pu=pu, st=st, jj=jj: e.tensor_tensor(
                            out=st[:, jj, 0:w], in0=sg[:, 0:w], in1=pu[:, 0:w], op=ALU.mult), reads=[sgr[0], ru], writes=[sr[0]])
                    kb.dma("pool", self.hid[:, half * 11:(half + 1) * 11, c0:c0 + w], st[:, :, 0:w], reads=[sr[0]],
                           writes=[self.hid_res[ti]])
        with self.scope():
            wout = kb.sb([P, FC, D], BF16, "wout")
            rw = Res("wout")
            self.load_w(wout, rw, self.w_out[l, fi], 4)
            pools = self.std_pools()
            xpool = Rot(kb, 2, [P, KC, 512], F32, "xt", nres=KC)
            ipool = Rot(kb, 2, [P, FC, 512], BF16, "hin", nres=1)
            ypool = Rot(kb, 1, [P, KC, 512], F32, "y", nres=KC)
            for tile in cfg.tiles:
                c0, w, segs, ti = tile
                hin, hir = ipool.next()
                kb.dma("sp", hin[:, :, 0:w], self.hid[:, :, c0:c0 + w], reads=[self.hid_res[ti]], writes=hir)
                xt, xr = self.load_x(xpool, tile)
                yt, yr = ypool.next()
                for oc in range(KC):
                    py, ry = kb.ps()
                    kb.mm(py[:, 0:w], ry, [(wout[:, kc, oc * 128:(oc + 1) * 128], hin[:, kc, 0:w], [rw, hir[0]]) for kc in range(FC)])
                    kb.op("act", lambda e, oc=oc, py=py, yt=yt: e.copy(yt[:, oc, 0:w], py[:, 0:w]), reads=[ry], writes=[yr[oc]])
                self.post_residual(l, s, tile, lambda c: yt[:, c, 0:w], lambda c: yr[c], xt, xr, pools)
        self.first_x = False

    def setup_rwkv(self):
        cfg, kb = self.cfg, self.kb
        na, NT = cfg.n_a, cfg.ntok
        self.mu = self.inp("mu", [P, na * 6 * KC])
        self.rvec = self.inp("rvec", [P, na * 5 * KC])
        self.v0_64 = self.inp("v0_64", [HD, max(1, na - 1) * NH])
        self.lnx64 = self.inp("lnx64", [HD, na * 2 * NH])
        self.w_rkv = self.inp("w_rkv", [na, 3, P, KC, D])
        self.w_o64 = self.inp("w_o64", [na, HD, NH, D])
        self.w1 = self.inp("w1", [na, P, KC, 64])
        self.a1 = self.inp("a1", [na, P, KC, 64])
        self.v1 = self.inp("v1", [max(1, na - 1), P, KC, 32])
        self.g1 = self.inp("g1", [na, P, KC, 160])
        self.w2 = self.inp("w2", [na, 64, 1, D])
        self.a2 = self.inp("a2", [na, 64, 1, D])
        self.v2 = self.inp("v2", [max(1, na - 1), 32, 1, D])
        self.g2 = self.inp("g2", [na, 160, 1, D])
        self.sh0 = self.inp("sh0", [P, na * NS * KC])
        self.hst0 = self.inp("hst0", [na, NS, P, KC * HD])
        self.o_shift = self.outp("o_shift", [P, na * NSEQ * KC])
        self.o_wkv = self.outp("o_wkv", [na, NSEQ, P, KC * HD])
        names = ["rt", "kt", "at", "kkt", "eg"]
        self.wk_d = {n: kb.dram("wk_" + n, [P, KC, NT], F32) for n in names}
        for n in ["v64", "g64", "bn64", "o64", "vf64"]:
            self.wk_d[n] = kb.dram("wk_" + n, [HD, NH, NT], F32)
        self.wk_res = {n: Res("wk_" + n) for n in self.wk_d}
        self.tiles256 = []
        for t in range(cfg.seq // 256):
            self.tiles256.append((t * 256, 256, [(0, 256, 0)], t // 2))
        self.tiles256.append((cfg.seq, NS * SLOT, [(s * SLOT, SLOT, 1 + s) for s in range(NS)], len(cfg.tiles) - 1))

    def rwkv_proj(self, l):
        cfg, kb = self.cfg, self.kb
        na = cfg.n_a
        W = 256
        with self.scope():
            rW = Res("rw")
            wr = kb.sb([P, KC, D], BF16, "wr")
            wk = kb.sb([P, KC, D], BF16, "wk")
            wv = kb.sb([P, KC, D], BF16, "wv")
            for t, i in ((wr, 0), (wk, 1), (wv, 2)):
                self.load_w(t, rW, self.w_rkv[l, i], 2)
            w1 = kb.sb([P, KC, 64], BF16, "w1")
            a1 = kb.sb([P, KC, 64], BF16, "a1")
            g1 = kb.sb([P, KC, 160], BF16, "g1")
            self.load_w(w1, rW, self.w1[l], 1)
            self.load_w(a1, rW, self.a1[l], 1)
            self.load_w(g1, rW, self.g1[l], 1)
            w2 = kb.sb([64, 1, D], BF16, "w2")
            a2 = kb.sb([64, 1, D], BF16, "a2")
            g2a = kb.sb([P, 1, D], BF16, "g2a")
            g2b = kb.sb([32, 1, D], BF16, "g2b")
            self.load_w(w2, rW, self.w2[l], 1)
            self.load_w(a2, rW, self.a2[l], 1)
            self.load_w(g2a, rW, self.g2[l, 0:128], 1)
            self.load_w(g2b, rW, self.g2[l, 128:160], 1)
            if l > 0:
                v1 = kb.sb([P, KC, 32], BF16, "v1")
                v2 = kb.sb([32, 1, D], BF16, "v2")
                self.load_w(v1, rW, self.v1[l - 1], 1)
                self.load_w(v2, rW, self.v2[l - 1], 1)
                v0 = kb.sb([HD, NH], F32, "v0")
                kb.dma("sp", v0[:], self.v0_64[:, (l - 1) * NH:l * NH], writes=[rW])
            mu = kb.sb([P, 6, KC], F32, "mu")
            kb.dma("sp", mu[:].rearrange("p a c -> p (a c)"), self.mu[:, l * 6 * KC:(l + 1) * 6 * KC], writes=[rW])
            rv = kb.sb([P, 5, KC], F32, "rv")
            kb.dma("sp", rv[:].rearrange("p a c -> p (a c)"), self.rvec[:, l * 5 * KC:(l + 1) * 5 * KC], writes=[rW])
            sh0 = kb.sb([P, NS, KC], F32, "sh0")
            kb.dma("sp", sh0[:].rearrange("p a c -> p (a c)"), self.sh0[:, l * NS * KC:(l + 1) * NS * KC], writes=[rW])
            bones = kb.sb([P, P], BF16, "bones")
            kb.op("pool", lambda e: e.memset(bones[:], 0.0), writes=[rW])
            kb.op("pool", lambda e: e.memset(bones[0:64, 0:64], 1.0), writes=[rW])
            kb.op("pool", lambda e: e.memset(bones[64:128, 64:128], 1.0), writes=[rW])
            mask64 = kb.sb([P, W], F32, "m64")
            mask32 = kb.sb([P, NS * SLOT], F32, "m32")
            kb.op("pool", lambda e: e.memset(mask64[:], 1.0), writes=[rW])
            kb.op("pool", lambda e: e.memset(mask64[:].rearrange("p (a b) -> p a b", b=64)[:, :, 0:1], 0.0), writes=[rW])
            kb.op("pool", lambda e: e.memset(mask32[:], 1.0), writes=[rW])
            kb.op("pool", lambda e: e.memset(mask32[:].rearrange("p (a b) -> p a b", b=SLOT)[:, :, 0:1], 0.0), writes=[rW])
            carry = kb.sb([P, KC, 2], F32, "carry")
            rcar = Res("carry")
            kb.op("pool", lambda e: e.memset(carry[:], 0.0), writes=[rcar])
            tiny = kb.sb([P, 1], F32, "tiny")
            kb.op("pool", lambda e: e.memset(tiny[:], 1e-24), writes=[rW])
            osh = kb.sb([P, NSEQ, KC], F32, "osh")
            rosh = Res("osh")

            pools = {"rstd": Rot(kb, 2, [P, W], F32, "rstd"), "sq": Rot(kb, 1, [P, KC, W], BF16, "sq", nres=KC),
                     "tmp": Rot(kb, 3, [P, W], F32, "tmp")}
            xpool = Rot(kb, 1, [P, KC, W], F32, "xt", nres=KC)
            hfp = Rot(kb, 1, [P, KC, W], F32, "hf", nres=KC)
            xxp = Rot(kb, 1, [P, KC, W], F32, "xx", nres=1)
            mixp = [Rot(kb, 1, [P, KC, W], BF16, f"mix{m}", nres=1) for m in range(6)]
            lorp = {n: Rot(kb, 1, [sz, W], BF16, n) for n, sz in (("tw", 64), ("ta", 64), ("tv", 32), ("tg0", 128), ("tg1", 32))}
            t32 = Rot(kb, 14, [P, W], F32, "t32")
            tb16 = Rot(kb, 3, [P, W], BF16, "tb16")
            stg = Rot(kb, 2, [P, 5, W], F32, "stg")
            v64p = Rot(kb, 1, [HD, NH, W], F32, "v64t")
            b64p = Rot(kb, 1, [HD, NH, W], F32, "b64t")
            t64 = Rot(kb, 8, [HD, W], F32, "t64")
            MI = {"r": 0, "w": 1, "k": 2, "v": 3, "a": 4, "g": 5}

            x512 = {}
            for (c0, w, segs, ti5) in self.tiles256:
                is_samp = (c0 >= cfg.seq)
                xt_full, xr = xpool.next()
                kb.dma("sp", xt_full[:, :, 0:w], self.x_src()[:, :, c0:c0 + w], reads=[self.xs_res[ti5]], writes=xr)
                off = 0
                hf, hr = hfp.next()
                self._mod256(l, ti5, off, w, segs, xt_full, xr, hf, hr, pools)
                xx, xxr = xxp.next()
                for (s0, sw, q) in segs:
                    kb.op("pool", lambda e, s0=s0, sw=sw: e.tensor_tensor(
                        out=xx[:, :, s0 + 1:s0 + sw], in0=hf[:, :, s0:s0 + sw - 1], in1=hf[:, :, s0 + 1:s0 + sw], op=ALU.subtract),
                        reads=hr, writes=[xxr[0]])
                    prev = carry[:, :, 0:1] if not is_samp else sh0[:, q - 1, :].unsqueeze(2)
                    kb.op("pool", lambda e, s0=s0, prev=prev: e.tensor_tensor(
                        out=xx[:, :, s0:s0 + 1], in0=prev, in1=hf[:, :, s0:s0 + 1], op=ALU.subtract),
                        reads=hr + [rcar, rW], writes=[xxr[0]])
                    last = s0 + sw - 1 if not is_samp else s0 + DEC - 1
                    if not is_samp:
                        kb.op("pool", lambda e, last=last: e.tensor_copy(out=carry[:, :, 0:1], in_=hf[:, :, last:last + 1]),
                              reads=hr, writes=[rcar])
                    if is_samp or c0 + w == cfg.seq:
                        kb.op("pool", lambda e, last=last, q=q: e.tensor_copy(out=osh[:, q, :].unsqueeze(2), in_=hf[:, :, last:last + 1]),
                              reads=hr, writes=[rosh])
                mixes = []
                for m in range(6):
                    if m == MI["v"] and False:
                        pass
                    mt, mr = mixp[m].next()
                    for c in range(KC):
                        kb.op("dve", lambda e, m=m, c=c, mt=mt: e.scalar_tensor_tensor(
                            out=mt[:, c, 0:w], in0=xx[:, c, 0:w], scalar=mu[:, m, c:c + 1], in1=hf[:, c, 0:w],
                            op0=ALU.mult, op1=ALU.add), reads=[xxr[0], hr[c], rW], writes=[mr[0]])
                    mixes.append((mt, mr[0]))
                xr_, xw_, xk_, xv_, xa_, xg_ = mixes

                def proj(out_ap, out_res, wt, col0, ncol, mix, kparts=P):
                    kb.mm(out_ap, out_res, [(wt[:, kc, col0:col0 + ncol], mix[0][:, kc, 0:w], [rW, mix[1]]) for kc in range(KC)])

                tw, twr = lorp["tw"].next()
                pt, pr = kb.ps()
                proj(pt[0:64, 0:w], pr, w1, 0, 64, xw_)
                kb.op("act", lambda e: e.activation(tw[:, 0:w], pt[0:64, 0:w], AF.Tanh), reads=[pr], writes=[twr[0]])
                ta, tar = lorp["ta"].next()
                pt, pr = kb.ps()
                proj(pt[0:64, 0:w], pr, a1, 0, 64, xa_)
                kb.op("act", lambda e, pt=pt: e.copy(ta[:, 0:w], pt[0:64, 0:w]), reads=[pr], writes=[tar[0]])
                tg0, tg0r = lorp["tg0"].next()
                pt, pr = kb.ps()
                proj(pt[:, 0:w], pr, g1, 0, 128, xg_)
                kb.op("act", lambda e, pt=pt: e.activation(tg0[:, 0:w], pt[:, 0:w], AF.Sigmoid), reads=[pr], writes=[tg0r[0]])
                tg1, tg1r = lorp["tg1"].next()
                pt, pr = kb.ps()
                proj(pt[0:32, 0:w], pr, g1, 128, 32, xg_)
                kb.op("act", lambda e, pt=pt: e.activation(tg1[:, 0:w], pt[0:32, 0:w], AF.Sigmoid), reads=[pr], writes=[tg1r[0]])
                if l > 0:
                    tv, tvr = lorp["tv"].next()
                    pt, pr = kb.ps()
                    proj(pt[0:32, 0:w], pr, v1, 0, 32, xv_)
                    kb.op("act", lambda e, pt=pt: e.copy(tv[:, 0:w], pt[0:32, 0:w]), reads=[pr], writes=[tvr[0]])

                v64t, v64r = v64p.next()
                for h in range(NH):
                    pv, prv = kb.ps()
                    proj(pv[0:64, 0:w], prv, wv, h * 64, 64, xv_)
                    if l == 0:
                        kb.op("act", lambda e, h=h, pv=pv: e.copy(v64t[:, h, 0:w], pv[0:64, 0:w]), reads=[prv], writes=[v64r[0]])
                    else:
                        pl, prl = kb.ps()
                        kb.mm(pl[0:64, 0:w], prl, [(v2[:, 0, h * 64:(h + 1) * 64], tv[:, 0:w], [rW, tvr[0]])])
                        vg, vgr = t64.next()
                        kb.op("act", lambda e, h=h, pl=pl, vg=vg: e.activation(vg[:, 0:w], pl[0:64, 0:w], AF.Sigmoid, bias=v0[:, h:h + 1], scale=1.0),
                              reads=[prl, rW], writes=[vgr[0]])
                        dd, ddr = t64.next()
                        kb.dma("sp", dd[:, 0:w], self.wk_d["vf64"][:, h, c0:c0 + w], reads=[self.wk_res["vf64"]], writes=[ddr[0]])
                        kb.op("dve", lambda e, h=h, pv=pv, dd=dd: e.tensor_tensor(out=dd[:, 0:w], in0=dd[:, 0:w], in1=pv[0:64, 0:w], op=ALU.subtract),
                              reads=[prv, ddr[0]], writes=[ddr[0]])
                        kb.op("pool", lambda e, dd=dd, vg=vg: e.tensor_tensor(out=dd[:, 0:w], in0=dd[:, 0:w], in1=vg[:, 0:w], op=ALU.mult),
                              reads=[ddr[0], vgr[0]], writes=[ddr[0]])
                        kb.op("dve", lambda e, h=h, pv=pv, dd=dd: e.tensor_tensor(out=v64t[:, h, 0:w], in0=dd[:, 0:w], in1=pv[0:64, 0:w], op=ALU.add),
                              reads=[prv, ddr[0]], writes=[v64r[0]])
                    pg, prg = kb.ps()
                    kb.mm(pg[0:64, 0:w], prg, [(g2a[:, 0, h * 64:(h + 1) * 64], tg0[:, 0:w], [rW, tg0r[0]]),
                                               (g2b[:, 0, h * 64:(h + 1) * 64], tg1[:, 0:w], [rW, tg1r[0]])])
                    gg, ggr = t64.next()
                    kb.op("act", lambda e, h=h, pg=pg, gg=gg: e.copy(gg[:, 0:w], pg[0:64, 0:w]), reads=[prg], writes=[ggr[0]])
                    kb.dma("pool", self.wk_d["g64"][:, h, c0:c0 + w], gg[:, 0:w], reads=[ggr[0]], writes=[self.wk_res["g64"]])
                kb.dma("pool", self.wk_d["v64"][:, :, c0:c0 + w], v64t[:, :, 0:w], reads=v64r, writes=[self.wk_res["v64"]])
                if l == 0:
                    kb.dma("pool", self.wk_d["vf64"][:, :, c0:c0 + w], v64t[:, :, 0:w], reads=v64r, writes=[self.wk_res["vf64"]])

                b64t, b64r = b64p.next()
                msk = mask32 if is_samp else mask64
                for c in range(KC):
                    cs = slice(c * 128, (c + 1) * 128)
                    p_r, r_r = kb.ps()
                    proj(p_r[:, 0:w], r_r, wr, c * 128, 128, xr_)
                    p_k, r_k = kb.ps()
                    proj(p_k[:, 0:w], r_k, wk, c * 128, 128, xk_)
                    p_w, r_w = kb.ps()
                    kb.mm(p_w[:, 0:w], r_w, [(w2[:, 0, cs], tw[:, 0:w], [rW, twr[0]])])
                    p_a, r_a = kb.ps()
                    kb.mm(p_a[:, 0:w], r_a, [(a2[:, 0, cs], ta[:, 0:w], [rW, tar[0]])])
                    T = {}

                    def tmp(name):
                        T[name] = t32.next()
                        return T[name][0]
                    a_ = tmp("a")
                    kb.op("act", lambda e: e.activation(a_[:, 0:w], p_a[:, 0:w], AF.Sigmoid, bias=rv[:, 1, c:c + 1], scale=1.0),
                          reads=[r_a, rW], writes=[T["a"][1][0]])
                    lw = tmp("lw")
                    kb.op("act", lambda e: e.activation(lw[:, 0:w], p_w[:, 0:w], AF.Sigmoid, bias=rv[:, 0, c:c + 1], scale=1.0),
                          reads=[r_w, rW], writes=[T["lw"][1][0]])
                    kb.op("dve", lambda e: e.tensor_scalar(out=lw[:, 0:w], in0=lw[:, 0:w], scalar1=-0.6065306597126334, scalar2=None, op0=ALU.mult),
                          reads=[T["lw"][1][0]], writes=[T["lw"][1][0]])
                    gc = tmp("gc")
                    kb.op("dve", lambda e: e.tensor_tensor_scan(out=gc[:, 0:w], data0=msk[:, 0:w], data1=lw[:, 0:w], initial=0.0,
                                                                 op0=ALU.mult, op1=ALU.add),
                          reads=[T["lw"][1][0], rW], writes=[T["gc"][1][0]])
                    st, sr = stg.next()
                    kb.op("act", lambda e: e.activation(st[:, 4, 0:w], gc[:, 0:w], AF.Exp), reads=[T["gc"][1][0]], writes=[sr[0]])
                    eng_ = tmp("eng")
                    kb.op("act", lambda e: e.activation(eng_[:, 0:w], gc[:, 0:w], AF.Exp, scale=-1.0), reads=[T["gc"][1][0]], writes=[T["eng"][1][0]])
                    egm = tmp("egm")
                    kb.op("pool", lambda e: e.tensor_tensor(out=egm[:, 0:w], in0=gc[:, 0:w], in1=lw[:, 0:w], op=ALU.subtract),
                          reads=[T["gc"][1][0], T["lw"][1][0]], writes=[T["egm"][1][0]])
                    kb.op("act", lambda e: e.activation(egm[:, 0:w], egm[:, 0:w], AF.Exp), reads=[T["egm"][1][0]], writes=[T["egm"][1][0]])
                    kkn = tmp("kkn")
                    kb.op("dve", lambda e: e.tensor_scalar(out=kkn[:, 0:w], in0=p_k[:, 0:w], scalar1=rv[:, 2, c:c + 1], scalar2=None, op0=ALU.mult),
                          reads=[r_k, rW], writes=[T["kkn"][1][0]])
                    k2, k2r = tb16.next()
                    kb.op("act", lambda e: e.activation(k2[:, 0:w], kkn[:, 0:w], AF.Square), reads=[T["kkn"][1][0]], writes=[k2r[0]])
                    p_s, r_s = kb.ps()
                    kb.mm(p_s[:, 0:w], r_s, [(bones[:], k2[:, 0:w], [rW, k2r[0]])])
                    rn = tmp("rn")
                    kb.op("dve", lambda e: e.tensor_scalar(out=rn[:, 0:w], in0=p_s[:, 0:w], scalar1=1e-24, scalar2=None, op0=ALU.max),
                          reads=[r_s], writes=[T["rn"][1][0]])
                    kb.op("act", lambda e: e.activation(rn[:, 0:w], rn[:, 0:w], AF.Sqrt), reads=[T["rn"][1][0]], writes=[T["rn"][1][0]])
                    kb.op("dve", lambda e: e.reciprocal(rn[:, 0:w], rn[:, 0:w]), reads=[T["rn"][1][0]], writes=[T["rn"][1][0]])
                    kb.op("pool", lambda e: e.tensor_tensor(out=kkn[:, 0:w], in0=kkn[:, 0:w], in1=rn[:, 0:w], op=ALU.mult),
                          reads=[T["rn"][1][0], T["kkn"][1][0]], writes=[T["kkn"][1][0]])
                    kp = tmp("kp")
                    kb.op("dve", lambda e: e.tensor_scalar(out=kp[:, 0:w], in0=a_[:, 0:w], scalar1=-1.0, scalar2=rv[:, 3, c:c + 1], op0=ALU.add, op1=ALU.mult),
                          reads=[T["a"][1][0], rW], writes=[T["kp"][1][0]])
                    kb.op("dve", lambda e: e.scalar_tensor_tensor(out=kp[:, 0:w], in0=kp[:, 0:w], scalar=1.0, in1=p_k[:, 0:w], op0=ALU.add, op1=ALU.mult),
                          reads=[T["kp"][1][0], r_k], writes=[T["kp"][1][0]])
                    kb.op("pool", lambda e: e.tensor_tensor(out=a_[:, 0:w], in0=a_[:, 0:w], in1=kkn[:, 0:w], op=ALU.mult),
                          reads=[T["a"][1][0], T["kkn"][1][0]], writes=[T["a"][1][0]])
                    pb_, pbr = tb16.next()
                    kb.op("dve", lambda e: e.scalar_tensor_tensor(out=pb_[:, 0:w], in0=p_r[:, 0:w], scalar=rv[:, 4, c:c + 1], in1=kp[:, 0:w], op0=ALU.mult, op1=ALU.mult),
                          reads=[r_r, T["kp"][1][0], rW], writes=[pbr[0]])
                    for half in range(2):
                        h = 2 * c + half
                        p_b, r_b = kb.ps()
                        kb.mm(p_b[0:64, 0:w], r_b, [(bones[:, half * 64:(half + 1) * 64], pb_[:, 0:w], [rW, pbr[0]])])
                        kb.op("dve", lambda e, h=h, p_b=p_b: e.tensor_tensor(out=b64t[:, h, 0:w], in0=v64t[:, h, 0:w], in1=p_b[0:64, 0:w], op=ALU.mult),
                              reads=[r_b, v64r[0]], writes=[b64r[0]])
                    kb.op("dve", lambda e: e.tensor_tensor(out=st[:, 0, 0:w], in0=st[:, 4, 0:w], in1=p_r[:, 0:w], op=ALU.mult),
                          reads=[r_r, sr[0]], writes=[sr[0]])
                    kb.op("pool", lambda e: e.tensor_tensor(out=st[:, 1, 0:w], in0=kp[:, 0:w], in1=eng_[:, 0:w], op=ALU.mult),
                          reads=[T["kp"][1][0], T["eng"][1][0]], writes=[sr[0]])
                    kb.op("pool", lambda e: e.tensor_tensor(out=st[:, 2, 0:w], in0=a_[:, 0:w], in1=eng_[:, 0:w], op=ALU.mult),
                          reads=[T["a"][1][0], T["eng"][1][0]], writes=[sr[0]])
                    kb.op("pool", lambda e: e.tensor_tensor(out=st[:, 3, 0:w], in0=kkn[:, 0:w], in1=egm[:, 0:w], op=ALU.mult),
                          reads=[T["kkn"][1][0], T["egm"][1][0]], writes=[sr[0]])
                    for i, n in enumerate(["rt", "kt", "at", "kkt", "eg"]):
                        kb.dma("sp" if i % 2 else "pool", self.wk_d[n][:, c, c0:c0 + w], st[:, i, 0:w], reads=[sr[0]], writes=[self.wk_res[n]])
                kb.dma("pool", self.wk_d["bn64"][:, :, c0:c0 + w], b64t[:, :, 0:w], reads=b64r, writes=[self.wk_res["bn64"]])
            kb.dma("pool", self.o_shift[:, l * NSEQ * KC:(l + 1) * NSEQ * KC], osh[:].rearrange("p a c -> p (a c)"), reads=[rosh])

    def _mod256(self, l, ti5, off, w, segs, xt, xr, hf, hr, pools):
        kb = self.kb
        rstd, rres = pools["rstd"].next()
        self.sumsq_rstd(lambda c: xt[:, c, off:off + w], lambda c: xr[c], w, pools["sq"], rstd, rres[0])
        for c in range(KC):
            for (s0, sw, q) in segs:
                tmp, tr = pools["tmp"].next()
                kb.op("dve", lambda e, c=c, s0=s0, sw=sw, tmp=tmp, q=q: e.scalar_tensor_tensor(
                    out=tmp[:, 0:sw], in0=xt[:, c, off + s0:off + s0 + sw], scalar=self.vA(l, 1, c, q), in1=rstd[:, s0:s0 + sw],
                    op0=ALU.mult, op1=ALU.mult), reads=[xr[c], rres[0], self.r_mod], writes=[tr[0]])
                kb.op("act", lambda e, c=c, s0=s0, sw=sw, tmp=tmp, q=q: e.activation(
                    hf[:, c, s0:s0 + sw], tmp[:, 0:sw], AF.Identity, bias=self.vB(l, 1, c, q), scale=1.0),
                    reads=[tr[0], self.r_mod], writes=[hr[c]])

    def rwkv_wkv(self, l):
        cfg, kb = self.cfg, self.kb
        HG = 8
        dbg = cfg.debug or ""
        with self.scope():
            rC = Res("wkvc")
            ones = kb.sb([P, P], F32, "ones32")
            ident = kb.sb([P, P], F32, "ident")
            mask = kb.sb([P, 5, P], F32, "mask")
            kb.op("pool", lambda e: e.memset(ones[:], 1.0), writes=[rC])
            kb.op("pool", lambda e: e.affine_select(out=ident[:], in_=ones[:], pattern=[[1, P]], compare_op=ALU.is_equal,
                                                    fill=0.0, base=0, channel_multiplier=-1), reads=[rC], writes=[rC])
            for i, (op_, cm, pat) in enumerate([(ALU.is_gt, -1, 1), (ALU.is_gt, 1, -1), (ALU.is_gt, -1, 1), (ALU.is_ge, -1, 1), (ALU.is_ge, -1, 1)]):
                kb.op("pool", lambda e, i=i, op_=op_, cm=cm, pat=pat: e.affine_select(
                    out=mask[:, i, :], in_=ones[:], pattern=[[pat, P]], compare_op=op_, fill=0.0, base=0, channel_multiplier=cm),
                    reads=[rC], writes=[rC])
            kb.op("pool", lambda e: e.memset(mask[0:64, :, 64:128], 0.0), reads=[rC], writes=[rC])
            kb.op("pool", lambda e: e.memset(mask[64:128, :, 0:64], 0.0), reads=[rC], writes=[rC])
            kb.op("pool", lambda e: e.tensor_scalar(out=mask[:, 4, :], in0=mask[:, 4, :], scalar1=-1.0, scalar2=None, op0=ALU.mult),
                  reads=[rC], writes=[rC])
            Hst = [kb.sb([P, KC, HD], F32, f"H{q}") for q in range(NSEQ)]
            Hres = [[Res(f"H{q}_{h}") for h in range(NH)] for q in range(NSEQ)]
            kb.op("pool", lambda e: e.memset(Hst[0][:], 0.0), writes=Hres[0])
            for q in range(1, NSEQ):
                kb.dma("sp", Hst[q][:].rearrange("p c i -> p (c i)"), self.hst0[l, q - 1], writes=Hres[q])

            def zrot(n, shape, name):
                r = Rot(kb, n, shape, F32, name)
                for t, rr in r.tiles:
                    kb.op("pool", lambda e, t=t: e.memset(t[:], 0.0), writes=rr)
                return r
            fpool = {n: Rot(kb, 2, [P, KC, P], F32, "f" + n) for n in ("kt", "at", "eg")}
            fmask = {(n, hf): zrot(2, [P, KC, P], f"fm{n}{hf}") for n in ("rt", "kkt") for hf in range(2)}
            vpool = Rot(kb, 1, [HD, NH, P], F32, "v64b")
            tkp = {(n, ch): zrot(1, [P, KC, P], f"{n}{ch}") for n in ("kTk", "naTk") for ch in range(2)}
            vtp = Rot(kb, 2, [P, NH, HD], F32, "vTk")
            osp = Rot(kb, 1, [HD, NH, P], F32, "ost")
            Gp = Rot(kb, HG + 2, [P, 5, P], F32, "G")
            XYp = Rot(kb, 4 * HG + 2, [P, P], F32, "XY")
            Pp = Rot(kb, HG + 2, [P, P], F32, "Pm")
            WUp = zrot(2 * (HG + 2), [P, HD], "WU")
            hgp = Rot(kb, HG + 2, [P, HD], F32, "hg")

            kb.psrot = list(range(6))
            zt, zr = osp.next()
            kb.op("pool", lambda e: e.memset(zt[:], 0.0), writes=zr)
            kb.dma("pool", self.wk_d["o64"][:, :, cfg.seq:cfg.seq + NS * SLOT], zt[:, :, 0:NS * SLOT], reads=zr, writes=[self.wk_res["o64"]])
            blocks = [(0, b * 128, 128, 64) for b in range(cfg.seq // 128)]
            blocks += [(q, cfg.seq + (q - 1) * SLOT, DEC, DEC) for q in range(1, NSEQ)]
            if "nosamp" in dbg:
                blocks = blocks[:cfg.seq // 128]
            if "noprompt" in dbg:
                blocks = blocks[cfg.seq // 128:]
            for (q, col0, BS, CL) in blocks:
                H = Hst[q]
                nch = BS // CL
                es = [2, 4, 8, 16, 32] if CL == 64 else [2, 4]
                F = {}
                for n in fpool:
                    t, r = fpool[n].next()
                    kb.dma("sp", t[:, :, 0:BS], self.wk_d[n][:, :, col0:col0 + BS], reads=[self.wk_res[n]], writes=r)
                    F[n] = (t, r[0])
                FM = {}
                for (n, hf) in fmask:
                    t, r = fmask[(n, hf)].next()
                    hs = slice(hf * 64, hf * 64 + 64)
                    kb.dma("sp", t[hs, :, 0:BS], self.wk_d[n][hs, :, col0:col0 + BS], reads=[self.wk_res[n]], writes=r)
                    FM[(n, hf)] = (t, r[0])
                vb, vbr = vpool.next()
                kb.dma("sp", vb[:, :, 0:BS], self.wk_d["v64"][:, :, col0:col0 + BS], reads=[self.wk_res["v64"]], writes=vbr)
                TK = {k_: tkp[k_].next() for k_ in tkp}
                vTk, vTr = vtp.next()
                for half4 in range(2):
                    for (src, nm, neg) in ((F["kt"], "kTk", False), (F["at"], "naTk", True)):
                        pt, pr = kb.ps()
                        for cc in range(4):
                            c = half4 * 4 + cc
                            kb.mm(pt[0:BS, cc * 128:(cc + 1) * 128], pr, [(src[0][:, c, 0:BS], ident[:], [src[1], rC])], transpose=True)
                        for ch in range(nch):
                            rows = slice(ch * CL, (ch + 1) * CL)
                            dst, dres = TK[(nm, ch)]
                            dview = dst[rows, half4 * 4:half4 * 4 + 4, :].rearrange("p a b -> p (a b)")
                            if neg:
                                kb.op("act", lambda e, pt=pt, dview=dview, rows=rows: e.mul(dview, pt[rows, :], -1.0), reads=[pr], writes=[dres[0]])
                            else:
                                kb.op("dve", lambda e, pt=pt, dview=dview, rows=rows: e.tensor_copy(out=dview, in_=pt[rows, :]), reads=[pr], writes=[dres[0]])
                    pt, pr = kb.ps()
                    for hh in range(8):
                        h = half4 * 8 + hh
                        kb.mm(pt[0:BS, hh * 64:(hh + 1) * 64], pr, [(vb[:, h, 0:BS], ident[0:64, 0:64], [vbr[0], rC])], transpose=True)
                    kb.op("act", lambda e, pt=pt: e.copy(vTk[0:BS, half4 * 8:half4 * 8 + 8, :].rearrange("p a b -> p (a b)"), pt[0:BS, :]),
                          reads=[pr], writes=[vTr[0]])
                ost, ostr = osp.next()
                if "tr" in dbg:
                    continue
                for hg0 in range(0, NH, HG):
                    heads = list(range(hg0, hg0 + HG))
                    S = {}
                    for h in heads:
                        c, hf = h // 2, h % 2
                        fk_, fa_ = F["kt"][0][:, c, 0:BS], F["at"][0][:, c, 0:BS]
                        fr_, fkk_ = FM[("rt", hf)][0][:, c, 0:BS], FM[("kkt", hf)][0][:, c, 0:BS]
                        rds = [F["kt"][1], F["at"][1], FM[("rt", hf)][1], FM[("kkt", hf)][1]]
                        pa, pra = kb.ps()
                        pb2, prb = kb.ps()
                        kb.mm(pa[0:BS, 0:BS], pra, [(fa_, fkk_, rds)])
                        kb.mm(pa[0:BS, 128:128 + BS], pra, [(fkk_, fa_, rds)])
                        kb.mm(pa[0:BS, 256:256 + BS], pra, [(fk_, fkk_, rds)])
                        kb.mm(pa[0:BS, 384:384 + BS], pra, [(fk_, fr_, rds)])
                        kb.mm(pb2[0:BS, 0:BS], prb, [(fa_, fr_, rds)])
                        G, Gr = Gp.next()
                        kb.op("dve", lambda e, G=G, pa=pa: e.tensor_tensor(
                            out=G[0:BS, 0:4, 0:BS], in0=pa[0:BS, :].rearrange("p (a b) -> p a b", b=128)[:, :, 0:BS],
                            in1=mask[0:BS, 0:4, 0:BS], op=ALU.mult), reads=[pra, rC], writes=[Gr[0]])
                        kb.op("dve", lambda e, G=G, pb2=pb2: e.tensor_tensor(
                            out=G[0:BS, 4, 0:BS], in0=pb2[0:BS, 0:BS], in1=mask[0:BS, 4, 0:BS], op=ALU.mult),
                            reads=[prb, rC], writes=[Gr[0]])
                        Pm, Pr = Pp.next()
                        kb.op("pool", lambda e, G=G, Pm=Pm: e.tensor_tensor(out=Pm[0:BS, 0:BS], in0=ident[0:BS, 0:BS], in1=G[0:BS, 0, 0:BS], op=ALU.subtract),
                              reads=[Gr[0], rC], writes=[Pr[0]])
                        S[h] = dict(G=G, Gr=Gr[0], Pm=Pm, Pr=Pr[0], X=G[0:BS, 0, 0:BS], Y=G[0:BS, 1, 0:BS], Xr=Gr[0], Yr=Gr[0])
                    if "gram" in dbg:
                        continue
                    for li, e_ in enumerate(es):
                        last = (li == len(es) - 1)
                        for h in heads:
                            s = S[h]
                            py, pyr = kb.ps()
                            kb.mm(py[0:BS, 0:BS], pyr, [(s["X"], s["Y"], [s["Xr"], s["Yr"]])])
                            Y2, Y2r = XYp.next()
                            kb.op("act", lambda e, py=py, Y2=Y2: e.copy(Y2[0:BS, 0:BS], py[0:BS, 0:BS]), reads=[pyr], writes=[Y2r[0]])
                            if not last:
                                px, pxr = kb.ps()
                                kb.mm(px[0:BS, 0:BS], pxr, [(s["Y"], s["X"], [s["Xr"], s["Yr"]])])
                                X2, X2r = XYp.next()
                                kb.op("dve", lambda e, px=px, X2=X2: e.tensor_copy(out=X2[0:BS, 0:BS], in_=px[0:BS, 0:BS]), reads=[pxr], writes=[X2r[0]])
                                s["X"], s["Xr"] = X2[0:BS, 0:BS], X2r[0]
                            s["Y"], s["Yr"] = Y2[0:BS, 0:BS], Y2r[0]
                        for h in heads:
                            s = S[h]
                            pp_, ppr = kb.ps()
                            kb.mm(pp_[0:BS, 0:BS], ppr, [(s["Y"], s["Pm"][0:BS, 0:BS], [s["Yr"], s["Pr"]])])
                            kb.op("dve", lambda e, s=s, pp_=pp_: e.tensor_tensor(out=s["Pm"][0:BS, 0:BS], in0=s["Pm"][0:BS, 0:BS], in1=pp_[0:BS, 0:BS], op=ALU.add),
                                  reads=[ppr, s["Pr"]], writes=[s["Pr"]])
                    if "inv" in dbg:
                        continue
                    po = {}
                    for hi, h in enumerate(heads):
                        bt, br = kb.psum[6 + hi // 4]
                        po[h] = (bt[:, (hi % 4) * 128:(hi % 4 + 1) * 128], br)
                        Wt, Wr_ = WUp.next()
                        Ut, Ur_ = WUp.next()
                        S[h].update(W=Wt, Wr=Wr_[0], U=Ut, Ur=Ur_[0])
                    for ch in range(nch):
                        rows = slice(ch * CL, (ch + 1) * CL)
                        for h in heads:
                            s = S[h]
                            c, hf = h // 2, h % 2
                            ps_ = slice(hf * 64, hf * 64 + 64)
                            gam = F["eg"][0][ps_, c, ch * CL + CL - 1:ch * CL + CL]
                            hg, hgr = hgp.next()
                            kb.op("pool", lambda e, hg=hg, gam=gam, c=c, ps_=ps_: e.tensor_scalar(out=hg[ps_, :], in0=H[ps_, c, :], scalar1=gam, scalar2=None, op0=ALU.mult),
                                  reads=[Hres[q][h], F["eg"][1]], writes=[hgr[0]])
                            s["hg"], s["hgr"], s["gam"] = hg, hgr[0], gam
                            pw, pwr = kb.ps()
                            kb.mm(pw[0:BS, 0:64], pwr, [(FM[("kkt", hf)][0][:, c, 0:BS], H[:, c, :], [FM[("kkt", hf)][1], Hres[q][h]]),
                                                          (s["G"][0:BS, 2, 0:BS], vTk[0:BS, h, :], [s["Gr"], vTr[0]])])
                            kb.op("act", lambda e, pw=pw, s=s: e.copy(s["W"][rows, :], pw[rows, 0:64]), reads=[pwr], writes=[s["Wr"]])
                        for h in heads:
                            s = S[h]
                            pu, pur = kb.ps()
                            kb.mm(pu[0:BS, 0:64], pur, [(s["Pm"][0:BS, 0:BS], s["W"][0:BS, :], [s["Pr"], s["Wr"]])])
                            kb.op("dve", lambda e, pu=pu, s=s: e.tensor_copy(out=s["U"][rows, :], in_=pu[rows, 0:64]), reads=[pur], writes=[s["Ur"]])
                        for h in heads:
                            s = S[h]
                            c, hf = h // 2, h % 2
                            ps_ = slice(hf * 64, hf * 64 + 64)
                            pt, pr = po[h]
                            kb.mm(pt[0:64, ch * CL:(ch + 1) * CL], pr,
                                  [(H[:, c, :], FM[("rt", hf)][0][:, c, rows], [Hres[q][h], FM[("rt", hf)][1]]),
                                   (vTk[0:BS, h, :], s["G"][0:BS, 3, rows], [vTr[0], s["Gr"]]),
                                   (s["U"][0:BS, :], s["G"][0:BS, 4, rows], [s["Ur"], s["Gr"]])])
                            ph, phr = kb.ps()
                            kT, kTr = TK[("kTk", ch)]
                            nT, nTr = TK[("naTk", ch)]
                            kb.mm(ph[:, 0:64], phr, [(kT[0:BS, c, :], vTk[0:BS, h, :], [kTr[0], vTr[0]]),
                                                      (nT[0:BS, c, :], s["U"][0:BS, :], [nTr[0], s["Ur"]])])
                            kb.op("dve", lambda e, ph=ph, s=s, c=c, ps_=ps_: e.scalar_tensor_tensor(
                                out=H[ps_, c, :], in0=ph[ps_, 0:64], scalar=s["gam"], in1=s["hg"][ps_, :], op0=ALU.mult, op1=ALU.add),
                                reads=[phr, s["hgr"], F["eg"][1]], writes=[Hres[q][h]])
                    for h in heads:
                        pt, pr = po[h]
                        kb.op("act", lambda e, pt=pt, h=h: e.copy(ost[:, h, 0:BS], pt[0:64, 0:BS]), reads=[pr], writes=[ostr[0]])
                if not dbg:
                    kb.dma("pool", self.wk_d["o64"][:, :, col0:col0 + BS], ost[:, :, 0:BS], reads=ostr, writes=[self.wk_res["o64"]])
            for q in range(NSEQ):
                kb.dma("pool", self.o_wkv[l, q], Hst[q][:].rearrange("p c i -> p (c i)"), reads=Hres[q])
            kb.psrot = list(range(8))

    def rwkv_out(self, l):
        cfg, kb = self.cfg, self.kb
        W = 256
        with self.scope():
            rW = Res("r3w")
            wo = kb.sb([HD, NH, D], BF16, "wo64")
            self.load_w(wo, rW, self.w_o64[l], 4)
            ln = kb.sb([HD, 2, NH], F32, "ln")
            kb.dma("sp", ln[:].rearrange("p a h -> p (a h)"), self.lnx64[:, l * 2 * NH:(l + 1) * 2 * NH], writes=[rW])
            ones = kb.sb([HD, HD], F32, "ones64")
            kb.op("pool", lambda e: e.memset(ones[:], 1.0), writes=[rW])
            geps = kb.sb([HD, 1], F32, "geps")
            kb.op("pool", lambda e: e.memset(geps[:], GN_EPS), writes=[rW])
            pools = {"rstd": Rot(kb, 2, [P, W], F32, "rstd"), "sq": Rot(kb, 1, [P, KC, W], BF16, "sq", nres=KC),
                     "tmp": Rot(kb, 3, [P, W], F32, "tmp")}
            xpool = Rot(kb, 2, [P, KC, W], F32, "xt", nres=KC)
            ypool = Rot(kb, 1, [P, KC, W], F32, "y", nres=KC)
            inp_ = {n: Rot(kb, 1, [HD, NH, W], F32, "i" + n) for n in ("o64", "g64", "bn64")}
            ogp = Rot(kb, 2, [HD, NH, W], BF16, "og")
            t64 = Rot(kb, 10, [HD, W], F32, "t64")
            for tile in self.tiles256:
                c0, w, segs, ti = tile
                I = {}
                for n in inp_:
                    t, r = inp_[n].next()
                    kb.dma("sp", t[:, :, 0:w], self.wk_d[n][:, :, c0:c0 + w], reads=[self.wk_res[n]], writes=r)
                    I[n] = (t, r[0])
                og, ogr = ogp.next()
                for h in range(NH):
                    o_ = I["o64"][0][:, h, 0:w]
                    osq, osqr = t64.next()
                    kb.op("act", lambda e: e.activation(osq[:, 0:w], o_, AF.Square), reads=[I["o64"][1]], writes=[osqr[0]])
                    p1, r1 = kb.ps()
                    kb.mm(p1[0:64, 0:w], r1, [(ones[:], o_, [rW, I["o64"][1]])])
                    p2, r2 = kb.ps()
                    kb.mm(p2[0:64, 0:w], r2, [(ones[:], osq[:, 0:w], [rW, osqr[0]])])
                    nm, nmr = t64.next()
                    kb.op("act", lambda e: e.mul(nm[:, 0:w], p1[0:64, 0:w], -1.0 / HD), reads=[r1], writes=[nmr[0]])
                    msq, msqr = t64.next()
                    kb.op("pool", lambda e: e.tensor_tensor(out=msq[:, 0:w], in0=nm[:, 0:w], in1=nm[:, 0:w], op=ALU.mult), reads=[nmr[0]], writes=[msqr[0]])
                    var, varr = t64.next()
                    kb.op("dve", lambda e: e.scalar_tensor_tensor(out=var[:, 0:w], in0=p2[0:64, 0:w], scalar=1.0 / HD, in1=msq[:, 0:w], op0=ALU.mult, op1=ALU.subtract),
                          reads=[r2, msqr[0]], writes=[varr[0]])
                    kb.op("act", lambda e: e.activation(var[:, 0:w], var[:, 0:w], AF.Sqrt, bias=geps[:, 0:1], scale=1.0), reads=[varr[0], rW], writes=[varr[0]])
                    kb.op("dve", lambda e: e.reciprocal(var[:, 0:w], var[:, 0:w]), reads=[varr[0]], writes=[varr[0]])
                    d, dr = t64.next()
                    kb.op("pool", lambda e: e.tensor_tensor(out=d[:, 0:w], in0=o_, in1=nm[:, 0:w], op=ALU.add), reads=[I["o64"][1], nmr[0]], writes=[dr[0]])
                    kb.op("pool", lambda e: e.tensor_tensor(out=d[:, 0:w], in0=d[:, 0:w], in1=var[:, 0:w], op=ALU.mult), reads=[dr[0], varr[0]], writes=[dr[0]])
                    kb.op("act", lambda e: e.activation(d[:, 0:w], d[:, 0:w], AF.Identity, bias=ln[:, 1, h:h + 1], scale=ln[:, 0, h:h + 1]),
                          reads=[dr[0], rW], writes=[dr[0]])
                    kb.op("pool", lambda e: e.tensor_tensor(out=d[:, 0:w], in0=d[:, 0:w], in1=I["bn64"][0][:, h, 0:w], op=ALU.add),
                          reads=[dr[0], I["bn64"][1]], writes=[dr[0]])
                    kb.op("dve", lambda e: e.tensor_tensor(out=og[:, h, 0:w], in0=d[:, 0:w], in1=I["g64"][0][:, h, 0:w], op=ALU.mult),
                          reads=[dr[0], I["g64"][1]], writes=[ogr[0]])
                xt, xr = self.load_x(xpool, tile)
                yt, yr = ypool.next()
                for oc in range(KC):
                    py, ry = kb.ps()
                    kb.mm(py[:, 0:w], ry, [(wo[:, h, oc * 128:(oc + 1) * 128], og[:, h, 0:w], [rW, ogr[0]]) for h in range(NH)])
                    kb.op("act", lambda e, oc=oc, py=py: e.copy(yt[:, oc, 0:w], py[:, 0:w]), reads=[ry], writes=[yr[oc]])
                self.post_residual(l, 1, tile, lambda c: yt[:, c, 0:w], lambda c: yr[c], xt, xr, pools)

    def rwkv_layer(self, l):
        self.rwkv_proj(l)
        self.rwkv_wkv(l)
        self.rwkv_out(l)


def _kmaj(w):
    Kd, N = w.shape
    return np.ascontiguousarray(w.reshape(Kd // P, P, N).transpose(1, 0, 2))


def _vec(v):
    sh = v.shape[:-1]
    return np.ascontiguousarray(np.moveaxis(v.reshape(sh + (KC, P)), -1, 0))


def _v64(v):
    sh = v.shape[:-1]
    return np.ascontiguousarray(np.moveaxis(v.reshape(sh + (NH, HD)), -1, 0))


def pack_shared(cfg, I):
    L, na = cfg.depth, cfg.n_a
    f = lambda a: np.ascontiguousarray(a, dtype=np.float32)
    out = {
        "w_mod": np.stack([_kmaj(f(I["w_mod"][l])) for l in range(L)]),
        "b_mod": f(I["b_mod"]).reshape(L, 72, P).transpose(2, 0, 1).reshape(P, -1).copy(),
        "g_norm": f(I["g_norm"]).reshape(L * 6, KC, P).transpose(2, 0, 1).reshape(P, -1).copy(),
        "w_in": np.stack([np.stack([_kmaj(f(I["w_ffn_in"][l, s])) for s in range(2)]) for l in range(L)]),
        "w_out": np.stack([np.stack([_kmaj(f(I["w_ffn_out"][l, s])) for s in range(2)]) for l in range(L)]),
        "mu": _vec(f(I["mu_a"])).reshape(P, -1).copy(),
        "rvec": _vec(np.stack([f(I["w0_a"]), f(I["a0_a"]), f(I["kk_a"]), f(I["ka_a"]),
                               f(I["rk_a"]).reshape(na, D)], axis=1)).reshape(P, -1).copy(),
        "v0_64": _v64(f(I["v0_a"])).reshape(HD, -1).copy(),
        "lnx64": _v64(np.stack([f(I["lnx_w_a"]), f(I["lnx_b_a"])], axis=1)).reshape(HD, -1).copy(),
        "w_rkv": np.stack([np.stack([_kmaj(f(I["w_rkv_a"][l, i])) for i in range(3)]) for l in range(na)]),
        "w_o64": np.stack([f(I["w_o_a"][l]).reshape(NH, HD, D).transpose(1, 0, 2).copy() for l in range(na)]),
        "w1": np.stack([_kmaj(f(I["w1_a"][l])) for l in range(na)]),
        "a1": np.stack([_kmaj(f(I["a1_a"][l])) for l in range(na)]),
        "v1": np.stack([_kmaj(f(I["v1_a"][l])) for l in range(na - 1)]),
        "g1": np.stack([_kmaj(f(I["g1_a"][l])) for l in range(na)]),
        "w2": f(I["w2_a"])[:, :, None, :].copy(),
        "a2": f(I["a2_a"])[:, :, None, :].copy(),
        "v2": f(I["v2_a"])[:, :, None, :].copy(),
        "g2": f(I["g2_a"])[:, :, None, :].copy(),
    }
    return out


def pack_core(cfg, I, core):
    na = cfg.n_a
    f = lambda a: np.ascontiguousarray(a, dtype=np.float32)
    b = core % I["x_prompt"].shape[0]
    sl = slice(core * NS, (core + 1) * NS)
    xall = np.zeros((cfg.ntok, D), np.float32)
    xall[:cfg.seq] = I["x_prompt"][b]
    for s in range(NS):
        xall[cfg.seq + s * SLOT: cfg.seq + s * SLOT + DEC] = I["x_sample"][core * NS + s]
    call = np.concatenate([f(I["c_prompt"][b:b + 1]), f(I["c_sample"][sl])], axis=0)
    out = {
        "xin": np.ascontiguousarray(xall.reshape(cfg.ntok, KC, P).transpose(2, 1, 0)),
        "cT": np.ascontiguousarray(call.reshape(NSEQ, KC, P).transpose(2, 1, 0)).reshape(P, -1),
        "sh0": _vec(f(I["state_shift"][:, sl])).reshape(P, -1).copy(),
        "hst0": np.ascontiguousarray(f(I["state_wkv"][:, sl]).reshape(na, NS, KC, 2, HD, HD).transpose(0, 1, 3, 5, 2, 4)).reshape(na, NS, P, KC * HD),
    }
    return out


def unpack_wkv(o_wkv):
    na = o_wkv.shape[0]
    a = o_wkv.reshape(na, NSEQ, 2, HD, KC, HD)
    return np.ascontiguousarray(a.transpose(0, 1, 4, 2, 5, 3)).reshape(na, NSEQ, NH, HD, HD)


def unpack_vec(o):
    return np.ascontiguousarray(np.moveaxis(o, 0, -1)).reshape(o.shape[1], D)


def _fox_setup(self):
    cfg, kb = self.cfg, self.kb
    NT = cfg.ntok
    nb = cfg.depth - cfg.n_a
    self.g_kv = self.inp("g_kv", [P, KC])
    self.w_kv = self.inp("w_kv", [P, KC, 2 * D])
    self.w_f = self.inp("w_f", [P, KC, NH])
    self.b_f = self.inp("b_f", [P, NH])
    self.w_q = self.inp("w_q", [nb, P, KC, D])
    self.w_o = self.inp("w_o", [nb, P, KC, D])
    self.cache_k = self.inp("cache_k", [cfg.npool * PAGE, D])
    self.cache_v = self.inp("cache_v", [cfg.npool * PAGE, D])
    self.cache_lf = self.inp("cache_lf", [cfg.npool * PAGE, NH])
    self.ptab = self.inp("ptab", [1, NS * cfg.npages], I32)
    self.o_k = self.outp("o_k", [P, KC, NT])
    self.o_v = self.outp("o_v", [NT, D])
    self.o_lf = self.outp("o_lf", [NT, NH])
    self.kbf = kb.dram("kbf", [P, KC, NT], BF16)
    self.vaug = kb.dram("vaug", [NT, NH * P], BF16)
    self.ckd = kb.dram("ckd", [NT, NH], F32)
    self.bnp = kb.dram("bnp", [NS, P, cfg.npages * NH], F32)
    self.r_kv = Res("kv")
    self.pidx = kb.sb([P, NS * cfg.npages], I32, "pidx")
    self.r_pidx = Res("pidx")
    with self.scope():
        iot = kb.sb([P, 1], I32, "iota")
        iof = kb.sb([P, 1], F32, "iotaf")
        idf = kb.sb([P, NS * cfg.npages], F32, "idf")
        kb.dma("sp", self.pidx[:], self.ptab[0:1, :].partition_broadcast(P), writes=[self.r_pidx])
        kb.op("pool", lambda e: e.iota(iot[:], pattern=[[0, 1]], base=0, channel_multiplier=1), writes=[self.r_pidx])
        kb.op("dve", lambda e: e.tensor_copy(out=iof[:], in_=iot[:]), reads=[self.r_pidx], writes=[self.r_pidx])
        kb.op("dve", lambda e: e.tensor_copy(out=idf[:], in_=self.pidx[:]), reads=[self.r_pidx], writes=[self.r_pidx])
        kb.op("dve", lambda e: e.tensor_scalar(out=idf[:], in0=idf[:], scalar1=float(PAGE), scalar2=iof[:, 0:1], op0=ALU.mult, op1=ALU.add),
              reads=[self.r_pidx], writes=[self.r_pidx])
        kb.op("dve", lambda e: e.tensor_copy(out=self.pidx[:], in_=idf[:]), reads=[self.r_pidx], writes=[self.r_pidx])


def _tri_consts(self, rC):
    kb = self.kb
    ones = kb.sb([P, P], F32, "ones32")
    ltri = kb.sb([P, P], F32, "ltri")
    e127 = kb.sb([P, P], F32, "e127")
    ltri32 = kb.sb([P, P], F32, "ltri32")
    kb.op("pool", lambda e: e.memset(ones[:], 1.0), writes=[rC])
    kb.op("pool", lambda e: e.affine_select(out=ltri[:], in_=ones[:], pattern=[[1, P]], compare_op=ALU.is_ge, fill=0.0, base=0, channel_multiplier=-1),
          reads=[rC], writes=[rC])
    kb.op("pool", lambda e: e.affine_select(out=e127[:], in_=ones[:], pattern=[[0, P]], compare_op=ALU.is_equal, fill=0.0, base=-127, channel_multiplier=1),
          reads=[rC], writes=[rC])
    kb.op("pool", lambda e: e.tensor_copy(out=ltri32[:], in_=ltri[:]), reads=[rC], writes=[rC])
    for a in range(4):
        for b in range(4):
            if a != b:
                kb.op("pool", lambda e, a=a, b=b: e.memset(ltri32[a * 32:(a + 1) * 32, b * 32:(b + 1) * 32], 0.0), reads=[rC], writes=[rC])
    return ones, ltri, e127, ltri32


def _shared_kv(self):
    cfg, kb = self.cfg, self.kb
    with self.scope():
        rW = Res("kvw")
        wkv = kb.sb([P, KC, 2 * D], BF16, "wkv")
        self.load_w(wkv, rW, self.w_kv, 4)
        wf = kb.sb([P, KC, NH], BF16, "wf")
        self.load_w(wf, rW, self.w_f, 1)
        gk = kb.sb([P, KC], F32, "gk")
        bf = kb.sb([P, NH], F32, "bf")
        kb.dma("sp", gk[:], self.g_kv[:, :], writes=[rW])
        kb.dma("sp", bf[:], self.b_f[:, :], writes=[rW])
        ones, ltri, e127, ltri32 = _tri_consts(self, rW)
        pools = self.std_pools()
        xpool = Rot(kb, 2, [P, KC, 512], F32, "xt", nres=KC)
        spool = Rot(kb, 2, [P, KC, 512], BF16, "st", nres=KC)
        kfp = Rot(kb, 2, [P, 512], F32, "kf")
        kbp = Rot(kb, 2, [P, 512], BF16, "kb")
        vfp = Rot(kb, 2, [P, D], F32, "vf")
        vap = Rot(kb, 2, [P, NH, P], BF16, "va")
        for t, r in vap.tiles:
            kb.op("pool", lambda e, t=t: e.memset(t[:], 0.0), writes=r)
            v4 = t[:].rearrange("p (a two) c -> p a two c", two=2)
            kb.op("pool", lambda e, v4=v4: e.memset(v4[:, :, 0, 64:65], 1.0), reads=r, writes=r)
            kb.op("pool", lambda e, v4=v4: e.memset(v4[:, :, 1, 0:1], 1.0), reads=r, writes=r)
        lfp = Rot(kb, 3, [P, NH], F32, "lf")
        ckp = Rot(kb, 3, [P, NH], F32, "ck")
        ck_prev = None
        for tile in cfg.tiles:
            c0, w, segs, ti = tile
            is_samp = c0 >= cfg.seq
            xt, xr = self.load_x(xpool, tile)
            st, sr = spool.next()
            self.modulate(0, 0, tile, xt, xr, lambda c, s0, sw: st[:, c, s0:s0 + sw], sr, pools, ident=lambda c: gk[:, c:c + 1])
            dbg = cfg.debug or ""
            if "kcut1" in dbg:
                continue
            for oc in range(KC):
                pk, rk = kb.ps()
                kb.mm(pk[:, 0:w], rk, [(wkv[:, kc, oc * 128:(oc + 1) * 128], st[:, kc, 0:w], [rW, sr[kc]]) for kc in range(KC)])
                kf, kfr = kfp.next()
                kb.op("act", lambda e, pk=pk, kf=kf: e.copy(kf[:, 0:w], pk[:, 0:w]), reads=[rk], writes=[kfr[0]])
                kbt, kbr = kbp.next()
                kb.op("dve", lambda e, kf=kf, kbt=kbt: e.tensor_copy(out=kbt[:, 0:w], in_=kf[:, 0:w]), reads=[kfr[0]], writes=[kbr[0]])
                kb.dma("pool", self.o_k[:, oc, c0:c0 + w], kf[:, 0:w], reads=[kfr[0]])
                kb.dma("pool", self.kbf[:, oc, c0:c0 + w], kbt[:, 0:w], reads=[kbr[0]], writes=[self.r_kv])
            if "kcut2" in dbg:
                continue
            for tb in range(w // 128):
                t0 = c0 + tb * 128
                ts_ = slice(tb * 128, (tb + 1) * 128)
                vf, vfr = vfp.next()
                va, var_ = vap.next()
                va4 = va[:].rearrange("p (a two) c -> p a two c", two=2)
                for half in range(2):
                    pv, rv_ = kb.ps()
                    kb.mm(pv[:, :], rv_, [(st[:, kc, ts_], wkv[:, kc, D + half * 512:D + (half + 1) * 512], [rW, sr[kc]]) for kc in range(KC)])
                    kb.op("act", lambda e, pv=pv, half=half: e.copy(vf[:, half * 512:(half + 1) * 512], pv[:, :]), reads=[rv_], writes=[vfr[0]])
                    pv4 = vf[:, half * 512:(half + 1) * 512].rearrange("p (a two c) -> p a two c", two=2, c=64)
                    kb.op("dve", lambda e, pv4=pv4, half=half: e.tensor_copy(out=va4[:, half * 4:half * 4 + 4, 0, 0:64], in_=pv4[:, :, 0, :]),
                          reads=[vfr[0]], writes=[var_[0]])
                    kb.op("pool", lambda e, pv4=pv4, half=half: e.tensor_copy(out=va4[:, half * 4:half * 4 + 4, 1, 64:128], in_=pv4[:, :, 1, :]),
                          reads=[vfr[0]], writes=[var_[0]])
                kb.dma("pool", self.o_v[t0:t0 + 128, :], vf[:], reads=[vfr[0]])
                kb.dma("pool", self.vaug[t0:t0 + 128, :], va[:].rearrange("p a c -> p (a c)"), reads=[var_[0]], writes=[self.r_kv])
                if "kcut3" in dbg:
                    continue
                pl, rl = kb.ps()
                kb.mm(pl[:, 0:NH], rl, [(st[:, kc, ts_], wf[:, kc, :], [rW, sr[kc]]) for kc in range(KC)])
                lf, lfr = lfp.next()
                kb.op("dve", lambda e, pl=pl, lf=lf: e.tensor_tensor(out=lf[:], in0=pl[:, 0:NH], in1=bf[:], op=ALU.add), reads=[rl, rW], writes=[lfr[0]])
                kb.op("act", lambda e, lf=lf: e.activation(lf[:], lf[:], AF.Exp, scale=-1.0), reads=[lfr[0]], writes=[lfr[0]])
                kb.op("act", lambda e, lf=lf: e.activation(lf[:], lf[:], AF.Ln, bias=1.0, scale=1.0), reads=[lfr[0]], writes=[lfr[0]])
                kb.op("dve", lambda e, lf=lf: e.tensor_scalar(out=lf[:], in0=lf[:], scalar1=-1.0, scalar2=None, op0=ALU.mult), reads=[lfr[0]], writes=[lfr[0]])
                kb.dma("pool", self.o_lf[t0:t0 + 128, :], lf[:], reads=[lfr[0]])
                if "kcut4" in dbg:
                    continue
                pc, rc = kb.ps()
                ck, ckr = ckp.next()
                if not is_samp:
                    items = [(ltri[:], lf[:], [rW, lfr[0]])]
                    if ck_prev is not None:
                        items.append((e127[:], ck_prev[0][:], [rW, ck_prev[1]]))
                    kb.mm(pc[:, 0:NH], rc, items)
                    kb.op("act", lambda e, pc=pc, ck=ck: e.copy(ck[:], pc[:, 0:NH]), reads=[rc], writes=[ckr[0]])
                    ck_prev = (ck, ckr[0])
                else:
                    kb.mm(pc[:, 0:NH], rc, [(ltri32[:], lf[:], [rW, lfr[0]])])
                    kb.op("act", lambda e, pc=pc, ck=ck: e.mul(ck[:], pc[:, 0:NH], -1.0), reads=[rc], writes=[ckr[0]])
                kb.dma("pool", self.ckd[t0:t0 + 128, :], ck[:], reads=[ckr[0]], writes=[self.r_kv])
        if "nopast" in (cfg.debug or ""):
            return
        npg = cfg.npages
        lpp = Rot(kb, 1, [P, npg, NH], F32, "lpast")
        cpp = Rot(kb, 2, [P, npg, NH], F32, "cpast")
        for s in range(NS):
            lp, lpr = lpp.next()
            for pg in range(npg):
                col = s * npg + pg
                kb.dma("pool", reads=[self.r_pidx], writes=[lpr[0]], fn=lambda e, pg=pg, col=col: e.indirect_dma_start(
                    out=lp[:, pg, :], out_offset=None, in_=self.cache_lf[:, :],
                    in_offset=bass.IndirectOffsetOnAxis(ap=self.pidx[:, col:col + 1], axis=0)))
            cp, cpr = cpp.next()
            for pg in range(npg):
                pc, rc = kb.ps()
                items = [(ltri[:], lp[:, pg, :], [rW, lpr[0]])]
                if pg > 0:
                    items.append((e127[:], cp[:, pg - 1, :], [rW, cpr[0]]))
                kb.mm(pc[:, 0:NH], rc, items)
                kb.op("act", lambda e, pc=pc, pg=pg: e.copy(cp[:, pg, :], pc[:, 0:NH]), reads=[rc], writes=[cpr[0]])
            pt_, rt_ = kb.ps()
            kb.mm(pt_[:, 0:NH], rt_, [(e127[:], cp[:, npg - 1, :], [rW, cpr[0]])])
            tot, totr = ckp.next()
            kb.op("act", lambda e, pt_=pt_, tot=tot: e.copy(tot[:], pt_[:, 0:NH]), reads=[rt_], writes=[totr[0]])
            kb.op("dve", lambda e, tot=tot: e.tensor_tensor(out=cp[:], in0=tot[:].unsqueeze(1).to_broadcast([P, npg, NH]), in1=cp[:], op=ALU.subtract),
                  reads=[totr[0], cpr[0]], writes=[cpr[0]])
            kb.dma("pool", self.bnp[s], cp[:].rearrange("p a h -> p (a h)"), reads=[cpr[0]], writes=[self.r_kv])


Prog.fox_setup = _fox_setup
Prog.shared_kv = _shared_kv


def _fox_layer(self, j):
    cfg, kb = self.cfg, self.kb
    l = cfg.n_a + j
    npg = cfg.npages
    with self.scope():
        rW = Res("aw")
        wq = kb.sb([P, KC, D], BF16, "wq")
        wo = kb.sb([P, KC, D], BF16, "wo")
        self.load_w(wq, rW, self.w_q[j], 2)
        self.load_w(wo, rW, self.w_o[j], 2)
        ones = kb.sb([P, P], F32, "ones32")
        onesb = kb.sb([P, 512], BF16, "onesb")
        sel = [kb.sb([P, P], F32, f"sel{i}") for i in range(2)]
        identb = kb.sb([P, P], BF16, "identb")
        kb.op("pool", lambda e: e.memset(ones[:], 1.0), writes=[rW])
        kb.op("pool", lambda e: e.memset(onesb[:], 1.0), writes=[rW])
        kb.op("pool", lambda e: e.affine_select(out=sel[0][:], in_=ones[:], pattern=[[0, P]], compare_op=ALU.is_equal, fill=0.0, base=-64, channel_multiplier=1),
              reads=[rW], writes=[rW])
        kb.op("pool", lambda e: e.affine_select(out=sel[1][:], in_=ones[:], pattern=[[0, P]], compare_op=ALU.is_equal, fill=0.0, base=0, channel_multiplier=1),
              reads=[rW], writes=[rW])
        kb.op("pool", lambda e: e.affine_select(out=identb[:], in_=onesb[:, 0:P], pattern=[[1, P]], compare_op=ALU.is_equal, fill=0.0, base=0, channel_multiplier=-1),
              reads=[rW], writes=[rW])
        dmask = kb.sb([P, 4, 512], BF16, "dmask")
        for dd in range(4):
            kb.op("pool", lambda e, dd=dd: e.affine_select(out=dmask[:, dd, :], in_=onesb[:], pattern=[[1, 512]], compare_op=ALU.is_ge, fill=0.0,
                                                           base=-dd * 128, channel_multiplier=-1), reads=[rW], writes=[rW])
        mask8 = kb.sb([DEC, DEC], BF16, "mask8")
        kb.op("pool", lambda e: e.affine_select(out=mask8[:], in_=onesb[0:DEC, 0:DEC], pattern=[[1, DEC]], compare_op=ALU.is_ge, fill=0.0,
                                                base=0, channel_multiplier=-1), reads=[rW], writes=[rW])
        pools = self.std_pools()
        xpool = Rot(kb, 1, [P, KC, 512], F32, "xt", nres=KC)
        hpool = Rot(kb, 1, [P, KC, 512], BF16, "ht", nres=KC)
        ypool = Rot(kb, 1, [P, KC, 512], F32, "y", nres=KC)
        qtm = kb.sb([P, NH, 512], BF16, "qtm")
        rq = Res("qtm")
        kb.op("pool", lambda e: e.memset(qtm[:], 0.0), writes=[rq])
        attnT = kb.sb([P, KC, 512], BF16, "attnT")
        ra = Res("attnT")
        kb.op("pool", lambda e: e.memset(attnT[:], 0.0), writes=[ra])
        rzp = [Rot(kb, 2, [P, 512], F32, f"rz{i}") for i in range(2)]
        for rp in rzp:
            for t, r in rp.tiles:
                kb.op("pool", lambda e, t=t: e.memset(t[:], 0.0), writes=r)
        osbp = Rot(kb, 2, [P, 512], F32, "osb")
        ktp = Rot(kb, 2, [P, 2, 512], BF16, "kt")
        vap = Rot(kb, 2, [P, 4, 512], BF16, "va")
        ckbp = Rot(kb, 2, [P, 4, NH], F32, "ckb")
        bnp_ = Rot(kb, 2, [P, 4, NH], F32, "bn")
        crp = Rot(kb, 2, [P, NH], F32, "cref")
        ptp = Rot(kb, 4, [P, 512], BF16, "pt")
        kb.psrot = [0, 1, 2, 3]

        def qproj(ht, hr, w):
            for oc in range(KC):
                pq, rq_ = kb.ps()
                kb.mm(pq[:, 0:w], rq_, [(wq[:, kc, oc * 128:(oc + 1) * 128], ht[:, kc, 0:w], [rW, hr[kc]]) for kc in range(KC)])
                kb.op("act", lambda e, pq=pq, oc=oc: e.mul(qtm[0:64, 2 * oc, 0:w], pq[0:64, 0:w], HD ** -0.5), reads=[rq_], writes=[rq])
                kb.op("act", lambda e, pq=pq, oc=oc: e.mul(qtm[64:128, 2 * oc + 1, 0:w], pq[64:128, 0:w], HD ** -0.5), reads=[rq_], writes=[rq])

        def epilogue(h, acc_ap, acc_res, w, out_cols):
            par = h % 2
            lr = 64 if par == 0 else 0
            rows = slice(0, 64) if par == 0 else slice(64, 128)
            rz, rzr = rzp[par].next()
            osb, osr = osbp.next()
            kb.op("act", lambda e: e.copy(osb[:, 0:w], acc_ap[:, 0:w]), reads=[acc_res], writes=[osr[0]])
            kb.op("dve", lambda e: e.reciprocal(rz[lr:lr + 1, 0:w], osb[lr:lr + 1, 0:w]), reads=[osr[0]], writes=[rzr[0]])
            pb_, pbr = kb.ps()
            kb.mm(pb_[:, 0:w], pbr, [(sel[par][:], rz[:, 0:w], [rW, rzr[0]])])
            kb.op("dve", lambda e: e.tensor_tensor(out=attnT[rows, h // 2, out_cols], in0=osb[rows, 0:w], in1=pb_[rows, 0:w], op=ALU.mult),
                  reads=[osr[0], pbr], writes=[ra])

        def out_proj(tile, xt, xr):
            c0, w, segs, ti = tile
            yt, yr = ypool.next()
            for oc in range(KC):
                py, ry = kb.ps()
                kb.mm(py[:, 0:w], ry, [(wo[:, kc, oc * 128:(oc + 1) * 128], attnT[:, kc, 0:w], [rW, ra]) for kc in range(KC)])
                kb.op("act", lambda e, oc=oc, py=py: e.copy(yt[:, oc, 0:w], py[:, 0:w]), reads=[ry], writes=[yr[oc]])
            self.post_residual(l, 1, tile, lambda c: yt[:, c, 0:w], lambda c: yr[c], xt, xr, pools)

        for qi, tile in enumerate(cfg.tiles[:-1]):
            c0, w, segs, ti = tile
            xt, xr = self.load_x(xpool, tile)
            ht, hr = hpool.next()
            self.modulate(l, 1, tile, xt, xr, lambda c, s0, sw: ht[:, c, s0:s0 + sw], hr, pools)
            qproj(ht, hr, w)
            cref, crr = crp.next()
            kb.dma("sp", cref[:], self.ckd[c0:c0 + 1, :].partition_broadcast(P), reads=[self.r_kv], writes=[crr[0]])
            nkb = 4 * (qi + 1)
            for hg in range(4):
                accs = [(kb.psum[4 + hi][0], kb.psum[4 + hi][1]) for hi in range(4)]
                for kg in range(qi + 1):
                    kt, ktr = ktp.next()
                    kb.dma("sp", kt[:], self.kbf[:, 2 * hg:2 * hg + 2, kg * 512:(kg + 1) * 512], reads=[self.r_kv], writes=[ktr[0]])
                    va, var_ = vap.next()
                    kb.dma("sp", va[:], self.vaug[kg * 512:(kg + 1) * 512, hg * 512:(hg + 1) * 512].rearrange("(b p) c -> p b c", p=P),
                           reads=[self.r_kv], writes=[var_[0]])
                    ckb, ckr = ckbp.next()
                    kb.dma("sp", ckb[:], self.ckd[kg * 512:(kg + 1) * 512, :].rearrange("(b p) h -> p b h", p=P), reads=[self.r_kv], writes=[ckr[0]])
                    bn, bnr = bnp_.next()
                    kb.op("dve", lambda e: e.tensor_tensor(out=bn[:], in0=cref[:].unsqueeze(1).to_broadcast([P, 4, NH]), in1=ckb[:], op=ALU.subtract),
                          reads=[crr[0], ckr[0]], writes=[bnr[0]])
                    for b in range(4):
                        kbi = kg * 4 + b
                        for hi in range(4):
                            h = 4 * hg + hi
                            ps_, pr_ = kb.ps()
                            kb.mm(ps_[:, 0:w], pr_, [(kt[:, hi // 2, b * 128:(b + 1) * 128], qtm[:, h, 0:w], [ktr[0], rq])])
                            pt, ptr = ptp.next()
                            kb.op("act", lambda e: e.activation(pt[:, 0:w], ps_[:, 0:w], AF.Exp, bias=bn[:, b, h:h + 1], scale=1.0),
                                  reads=[pr_, bnr[0]], writes=[ptr[0]])
                            if kbi >= 4 * qi:
                                dd = kbi - 4 * qi
                                kb.op("pool", lambda e: e.tensor_tensor(out=pt[:, 0:w], in0=pt[:, 0:w], in1=dmask[:, dd, 0:w], op=ALU.mult),
                                      reads=[ptr[0], rW], writes=[ptr[0]])
                            kb.mm(accs[hi][0][:, 0:w], accs[hi][1], [(va[:, b, hi * 128:(hi + 1) * 128], pt[:, 0:w], [var_[0], ptr[0]])],
                                  start=(kbi == 0), stop=(kbi == nkb - 1))
                for hi in range(4):
                    epilogue(4 * hg + hi, accs[hi][0], accs[hi][1], w, slice(0, w))
            out_proj(tile, xt, xr)

        tile = cfg.tiles[-1]
        c0, w, segs, ti = tile
        xt, xr = self.load_x(xpool, tile)
        ht, hr = hpool.next()
        self.modulate(l, 1, tile, xt, xr, lambda c, s0, sw: ht[:, c, s0:s0 + sw], hr, pools)
        qproj(ht, hr, w)
        kpp = Rot(kb, 2, [P, D], F32, "kpage")
        vpp = Rot(kb, 2, [P, D], F32, "vpage")
        k16p = Rot(kb, 2, [P, D], BF16, "k16")
        kTp = Rot(kb, 2, [P, KC, P], BF16, "kpT")
        vAp = Rot(kb, 2, [P, NH, P], BF16, "vaP")
        for t, r in vAp.tiles:
            kb.op("pool", lambda e, t=t: e.memset(t[:], 0.0), writes=r)
            v4 = t[:].rearrange("p (a two) c -> p a two c", two=2)
            kb.op("pool", lambda e, v4=v4: e.memset(v4[:, :, 0, 64:65], 1.0), reads=r, writes=r)
            kb.op("pool", lambda e, v4=v4: e.memset(v4[:, :, 1, 0:1], 1.0), reads=r, writes=r)
        bpp = Rot(kb, 1, [P, npg, NH], F32, "bpast")
        bnew = Rot(kb, 2, [DEC, NH], F32, "bnew")
        sbp = Rot(kb, 2, [P, NH, DEC], F32, "sb")
        pTp = Rot(kb, 2, [P, NH, DEC], BF16, "pT")
        acc_t, acc_r = kb.psum[4]
        for s in range(NS):
            qc = slice(s * SLOT, s * SLOT + DEC)
            bp, bpr = bpp.next()
            kb.dma("sp", bp[:].rearrange("p a h -> p (a h)"), self.bnp[s], reads=[self.r_kv], writes=[bpr[0]])
            for pg in range(npg + 1):
                new = (pg == npg)
                kT, kTr = kTp.next()
                vA, vAr = vAp.next()
                if not new:
                    col = s * npg + pg
                    kp, kpr = kpp.next()
                    vp, vpr = vpp.next()
                    kb.dma("pool", reads=[self.r_pidx], writes=[kpr[0]], fn=lambda e: e.indirect_dma_start(
                        out=kp[:], out_offset=None, in_=self.cache_k[:, :], in_offset=bass.IndirectOffsetOnAxis(ap=self.pidx[:, col:col + 1], axis=0)))
                    kb.dma("pool", reads=[self.r_pidx], writes=[vpr[0]], fn=lambda e: e.indirect_dma_start(
                        out=vp[:], out_offset=None, in_=self.cache_v[:, :], in_offset=bass.IndirectOffsetOnAxis(ap=self.pidx[:, col:col + 1], axis=0)))
                    k16, k16r = k16p.next()
                    kb.op("pool", lambda e: e.tensor_copy(out=k16[:], in_=kp[:]), reads=[kpr[0]], writes=[k16r[0]])
                    ptb, ptbr = kb.ps()
                    ptb16 = ptb[:, :].bitcast(BF16)
                    for c in range(KC):
                        kb.mm(ptb16[:, c * 128:(c + 1) * 128], ptbr, [(k16[:, c * 128:(c + 1) * 128], identb[:], [k16r[0], rW])], transpose=True)
                    kb.op("act", lambda e: e.copy(kT[:].rearrange("p c k -> p (c k)"), ptb16[:, :]), reads=[ptbr], writes=[kTr[0]])
                    vA4 = vA[:].rearrange("p (a two) c -> p a two c", two=2)
                    vp4 = vp[:].rearrange("p (a two c) -> p a two c", two=2, c=64)
                    kb.op("dve", lambda e: e.tensor_copy(out=vA4[:, :, 0, 0:64], in_=vp4[:, :, 0, :]), reads=[vpr[0]], writes=[vAr[0]])
                    kb.op("pool", lambda e: e.tensor_copy(out=vA4[:, :, 1, 64:128], in_=vp4[:, :, 1, :]), reads=[vpr[0]], writes=[vAr[0]])
                    nk = P
                    bias_ap = bp[:, pg, :]
                    bias_res = bpr[0]
                else:
                    t0 = c0 + s * SLOT
                    kb.dma("sp", kT[:, :, 0:DEC], self.kbf[:, :, t0:t0 + DEC], reads=[self.r_kv], writes=[kTr[0]])
                    kb.dma("sp", vA[0:DEC].rearrange("p a c -> p (a c)"), self.vaug[t0:t0 + DEC, :], reads=[self.r_kv], writes=[vAr[0]])
                    bnw, bnwr = bnew.next()
                    kb.dma("sp", bnw[:], self.ckd[t0:t0 + DEC, :], reads=[self.r_kv], writes=[bnwr[0]])
                    nk = DEC
                    bias_ap = bnw[:, :]
                    bias_res = bnwr[0]
                ps_, pr_ = kb.ps()
                for h in range(NH):
                    kb.mm(ps_[0:nk, h * DEC:(h + 1) * DEC], pr_, [(kT[:, h // 2, 0:nk], qtm[:, h, qc], [kTr[0], rq])])
                sb, sbr = sbp.next()
                kb.op("dve", lambda e: e.tensor_tensor(out=sb[0:nk], in0=ps_[0:nk, 0:NH * DEC].rearrange("p (h q) -> p h q", q=DEC),
                                                       in1=bias_ap.unsqueeze(2).to_broadcast([nk, NH, DEC]), op=ALU.add),
                      reads=[pr_, bias_res], writes=[sbr[0]])
                pT, pTr = pTp.next()
                kb.op("act", lambda e: e.activation(pT[0:nk].rearrange("p h q -> p (h q)"), sb[0:nk].rearrange("p h q -> p (h q)"), AF.Exp),
                      reads=[sbr[0]], writes=[pTr[0]])
                if new:
                    kb.op("pool", lambda e: e.tensor_tensor(out=pT[0:nk], in0=pT[0:nk], in1=mask8[:].unsqueeze(1).to_broadcast([DEC, NH, DEC]), op=ALU.mult),
                          reads=[pTr[0], rW], writes=[pTr[0]])
                for h in range(NH):
                    kb.mm(acc_t[:, h * DEC:(h + 1) * DEC], acc_r, [(vA[0:nk, h, :], pT[0:nk, h, :], [vAr[0], pTr[0]])],
                          start=(pg == 0), stop=(pg == npg))
            for h in range(NH):
                epilogue(h, acc_t[:, h * DEC:(h + 1) * DEC], acc_r, DEC, qc)
        out_proj(tile, xt, xr)
        kb.psrot = list(range(8))


def _final_out(self):
    kb = self.kb
    self.o_y = self.outp("o_y", [P, KC, self.cfg.ntok])
    for tile in self.cfg.tiles:
        c0, w, segs, ti = tile
        kb.dma("sp", self.o_y[:, :, c0:c0 + w], self.xs[:, :, c0:c0 + w], reads=[self.xs_res[ti]])


Prog.fox_layer = _fox_layer
Prog.final_out = _final_out


def build_program(cfg, stop=None):
    pg = Prog(cfg)
    pg.setup()
    pg.setup_rwkv()
    steps = []
    for l in range(cfg.depth):
        if l == cfg.n_a:
            steps.append(("fox_setup", pg.fox_setup))
            steps.append(("kv", pg.shared_kv))
        steps.append((f"ffn{l}a", lambda l=l: pg.ffn(l, 0)))
        if l < cfg.n_a:
            steps.append((f"rwkv{l}", lambda l=l: pg.rwkv_layer(l)))
        else:
            steps.append((f"fox{l}", lambda l=l: pg.fox_layer(l - cfg.n_a)))
        steps.append((f"ffn{l}b", lambda l=l: pg.ffn(l, 2)))
    for i, (name, fn) in enumerate(steps):
        if stop is not None and not isinstance(stop, int):
            if name not in stop:
                continue
        elif stop is not None and i >= stop:
            break
        fn()
    pg.final_out()
    pg.kb.finish()
    return pg


def pack_fox(cfg, I, core):
    f = lambda a: np.ascontiguousarray(a, dtype=np.float32)
    nb = cfg.depth - cfg.n_a
    shared = {
        "g_kv": _vec(f(I["g_kv"])[None])[:, 0, :].copy(),
        "w_kv": _kmaj(f(I["w_kv"])),
        "w_f": _kmaj(f(I["w_f"])),
        "b_f": np.ascontiguousarray(np.broadcast_to(f(I["b_f"])[None, :], (P, NH))),
        "w_q": np.stack([_kmaj(f(I["w_q_b"][j])) for j in range(nb)]),
        "w_o": np.stack([_kmaj(f(I["w_o_b"][j])) for j in range(nb)]),
        "cache_k": f(I["cache_k"]).reshape(-1, D),
        "cache_v": f(I["cache_v"]).reshape(-1, D),
        "cache_lf": f(I["cache_logf"]).reshape(-1, NH),
    }
    return shared


_STOP = None
_DEBUG = None


def kernel(**I):
    n_cores = 8
    seq = I["x_prompt"].shape[1]
    npages = I["page_table"].shape[1]
    npool = I["cache_k"].shape[0]
    depth = I["w_mod"].shape[0]
    cfg = Cfg(seq=seq, npages=npages, npool=npool, depth=depth, debug=_DEBUG)
    pg = build_program(cfg, stop=_STOP)
    shared = pack_shared(cfg, I)
    shared.update(pack_fox(cfg, I, 0))
    in_maps = []
    for core in range(n_cores):
        d = dict(shared)
        d.update(pack_core(cfg, I, core))
        d["ptab"] = np.ascontiguousarray(I["page_table"][core * NS:(core + 1) * NS].reshape(1, -1).astype(np.int32))
        in_maps.append({k: d[k] for k in pg.inputs})
    res = run_bass_kernel_spmd(pg.nc, in_maps, core_ids=list(range(n_cores)))
    R = res.results
    if _STOP is not None:
        for r in R:
            for k, shp in (("o_k", (P, KC, cfg.ntok)), ("o_v", (cfg.ntok, D)), ("o_lf", (cfg.ntok, NH))):
                r.setdefault(k, np.zeros(shp, np.float32))
    B = I["x_prompt"].shape[0]
    DB = I["x_sample"].shape[0]
    na = cfg.n_a

    def tokmaj(a):
        return np.ascontiguousarray(a.transpose(2, 1, 0)).reshape(a.shape[2], D)

    def samp_rows(a):
        return np.stack([a[seq + s * SLOT: seq + s * SLOT + DEC] for s in range(NS)])
    y_p = np.stack([tokmaj(R[b]["o_y"][:, :, :seq]) for b in range(B)])
    y_s = np.concatenate([samp_rows(tokmaj(R[c]["o_y"])) for c in range(n_cores)])[:DB]
    wk = [unpack_wkv(R[c]["o_wkv"]) for c in range(n_cores)]
    wkv_p = np.stack([wk[b][:, 0] for b in range(B)], axis=1)
    wkv_s = np.concatenate([wk[c][:, 1:] for c in range(n_cores)], axis=1)[:, :DB]
    sh = [np.stack([unpack_vec(R[c]["o_shift"].reshape(P, na, NSEQ, KC)[:, l]) for l in range(na)]) for c in range(n_cores)]
    sh_p = np.stack([sh[b][:, 0] for b in range(B)], axis=1)
    sh_s = np.concatenate([sh[c][:, 1:] for c in range(n_cores)], axis=1)[:, :DB]
    k_all = [tokmaj(R[c]["o_k"]) for c in range(n_cores)]
    k_p = np.stack([k_all[b][:seq] for b in range(B)]).reshape(B, seq, NH, HD)
    k_s = np.concatenate([samp_rows(k_all[c]) for c in range(n_cores)])[:DB].reshape(DB, DEC, NH, HD)
    v_p = np.stack([R[b]["o_v"][:seq] for b in range(B)]).reshape(B, seq, NH, HD)
    v_s = np.concatenate([samp_rows(R[c]["o_v"]) for c in range(n_cores)])[:DB].reshape(DB, DEC, NH, HD)
    lf_p = np.stack([R[b]["o_lf"][:seq] for b in range(B)])
    lf_s = np.concatenate([samp_rows(R[c]["o_lf"]) for c in range(n_cores)])[:DB]
    f32 = lambda a: np.ascontiguousarray(a, dtype=np.float32)
    return (f32(y_p), f32(y_s), f32(wkv_p), f32(sh_p), f32(k_p), f32(v_p), f32(lf_p),
            f32(wkv_s), f32(sh_s), f32(k_s), f32(v_s), f32(lf_s))
```

```python
import contextlib
import numpy as np
import concourse.bass as bass
import concourse.mybir as mybir
from concourse.bass_utils import run_bass_kernel_spmd

F32 = mybir.dt.float32
BF16 = mybir.dt.bfloat16
I32 = mybir.dt.int32
AF = mybir.ActivationFunctionType
ALU = mybir.AluOpType

P = 128
D = 1024
KC = 8
DFF = 2816
FC = 22
NH = 16
HD = 64
NSEQ = 5
NS = 4
SLOT = 32
DEC = 8
NORM_EPS = 1e-6
GN_EPS = 64e-5
PAGE = 128


class Res:
    __slots__ = ("w", "r", "name")

    def __init__(self, name=""):
        self.w = {}
        self.r = {}
        self.name = name


class KB:
    def __init__(self, nc, nring=20, same_sync=True):
        self.nc = nc
        self.es = contextlib.ExitStack()
        self.same = same_sync
        self.eng = {"pe": nc.tensor, "act": nc.scalar, "dve": nc.vector, "pool": nc.gpsimd, "sp": nc.sync}
        self.csem = {e: self.es.enter_context(nc.semaphore(f"c_{e}")) for e in ("pe", "act", "dve", "pool")}
        self.cnt = {e: 0 for e in self.csem}
        self.ring = {q: [self.es.enter_context(nc.semaphore(f"r_{q}{i}")) for i in range(nring)] for q in ("sp", "pool")}
        self.ruse = {q: [0] * nring for q in self.ring}
        self.rnext = {q: 0 for q in self.ring}
        self.seen = {e: {} for e in self.eng}
        self.nbuf = 0
        self.psum = []
        self.psn = 0
        self.psrot = list(range(8))

    def sb(self, shape, dt, name=None):
        self.nbuf += 1
        return self.es.enter_context(self.nc.sbuf_tensor(f"{name or 'sb'}_{self.nbuf}", list(shape), dt))

    def init_psum(self):
        for i in range(8):
            t = self.es.enter_context(self.nc.psum_tensor(f"ps{i}", [P, 512], F32))
            self.psum.append((t, Res(f"ps{i}")))

    def ps(self):
        t = self.psum[self.psrot[self.psn % len(self.psrot)]]
        self.psn += 1
        return t

    def dram(self, name, shape, dt):
        return self.nc.dram_tensor(name, list(shape), dt).ap()

    def _collect(self, e, reads, writes):
        waits = {}

        def add(d):
            for k, (sem, v) in d.items():
                if k == e and (e == "pe" or not self.same):
                    continue
                if k not in waits or waits[k][1] < v:
                    waits[k] = (sem, v)
        for r in reads:
            add(r.w)
        for w in writes:
            add(w.w)
            add(w.r)
        return waits

    def _wait(self, e, waits):
        sn = self.seen[e]
        for k, (sem, v) in waits.items():
            if sn.get(k, 0) >= v:
                continue
            self.eng[e].wait_ge(sem, v)
            sn[k] = v

    @staticmethod
    def _register(key, ev, reads, writes):
        for r in reads:
            r.r[key] = ev
        for w in writes:
            w.w = {key: ev}
            w.r = {}

    def op(self, e, fn, reads=(), writes=()):
        self._wait(e, self._collect(e, reads, writes))
        ins = fn(self.eng[e])
        self.cnt[e] += 1
        ins.then_inc(self.csem[e], 1)
        self._register(e, (self.csem[e], self.cnt[e]), reads, writes)

    def mm(self, out_ap, out_res, items, transpose=False, start=True, stop=True):
        n = len(items)
        allreads = []
        ins = None
        for i, (l, r, rd) in enumerate(items):
            self._wait("pe", self._collect("pe", rd, [out_res] if i == 0 else []))
            if transpose:
                ins = self.nc.tensor.transpose(out_ap, l, r)
            else:
                ins = self.nc.tensor.matmul(out_ap, lhsT=l, rhs=r, start=(start and i == 0), stop=(stop and i == n - 1))
            allreads.extend(rd)
        self.cnt["pe"] += 1
        ins.then_inc(self.csem["pe"], 1)
        self._register("pe", (self.csem["pe"], self.cnt["pe"]), allreads, [out_res])

    def dma(self, q, out_ap=None, in_ap=None, reads=(), writes=(), fn=None):
        waits = self._collect(q, reads, writes)
        ring = self.ring[q]
        i = self.rnext[q]
        self.rnext[q] = (i + 1) % len(ring)
        key = f"{q}{i}"
        if self.ruse[q][i] > 0:
            v = 16 * self.ruse[q][i]
            if key not in waits or waits[key][1] < v:
                waits[key] = (ring[i], v)
        self._wait(q, waits)
        if fn is None:
            ins = self.eng[q].dma_start(out=out_ap, in_=in_ap)
        else:
            ins = fn(self.eng[q])
        self.ruse[q][i] += 1
        ins.then_inc(ring[i], 16)
        self._register(key, (ring[i], 16 * self.ruse[q][i]), reads, writes)

    def finish(self):
        for q in self.ring:
            for i, sem in enumerate(self.ring[q]):
                if self.ruse[q][i] > 0:
                    self.nc.sync.wait_ge(sem, 16 * self.ruse[q][i])
        for e in self.csem:
            if self.cnt[e] > 0:
                self.nc.sync.wait_ge(self.csem[e], self.cnt[e])


class Rot:
    def __init__(self, kb, n, shape, dt, name, nres=1):
        self.tiles = [(kb.sb(shape, dt, name), [Res(f"{name}{i}_{j}") for j in range(nres)]) for i in range(n)]
        self.i = 0

    def next(self):
        t = self.tiles[self.i % len(self.tiles)]
        self.i += 1
        return t


def interleave(gens, width):
    gens = list(gens)
    active = []
    while gens or active:
        while gens and len(active) < width:
            active.append(gens.pop(0))
        for g in list(active):
            try:
                next(g)
            except StopIteration:
                active.remove(g)


class Cfg:
    def __init__(self, seq=8192, npages=64, npool=2560, debug=None, depth=4):
        self.seq = seq
        self.npages = npages
        self.npool = npool
        self.ntok = seq + NS * SLOT
        self.debug = debug
        self.depth = depth
        self.n_a = depth // 2
        self.tiles = []
        for t in range(seq // 512):
            self.tiles.append((t * 512, 512, [(0, 512, 0)], t))
        self.tiles.append((seq, NS * SLOT, [(s * SLOT, SLOT, 1 + s) for s in range(NS)], seq // 512))


class Prog:
    def __init__(self, cfg):
        self.cfg = cfg
        nc = self.nc = bass.Bass("TRN2", target_bir_lowering=False)
        self.kb = KB(nc)
        self.inputs = {}
        self.outputs = {}
        self.scope_stack = None

    def inp(self, name, shape, dt=F32):
        ap = self.nc.dram_tensor(name, list(shape), dt, kind="ExternalInput").ap()
        self.inputs[name] = ap
        return ap

    def outp(self, name, shape, dt=F32):
        ap = self.nc.dram_tensor(name, list(shape), dt, kind="ExternalOutput").ap()
        self.outputs[name] = ap
        return ap

    @contextlib.contextmanager
    def scope(self):
        kb = self.kb
        outer = kb.es
        inner = contextlib.ExitStack()
        kb.es = inner
        try:
            yield
            self.barrier()
        finally:
            kb.es = outer
            inner.close()

    def barrier(self):
        kb = self.kb
        evs = {}
        for e in kb.csem:
            if kb.cnt[e] > 0:
                evs[e] = (kb.csem[e], kb.cnt[e])
        for q in kb.ring:
            for i, sem in enumerate(kb.ring[q]):
                if kb.ruse[q][i] > 0:
                    evs[f"{q}{i}"] = (sem, 16 * kb.ruse[q][i])
        for e in kb.eng:
            for k, (sem, v) in evs.items():
                if kb.seen[e].get(k, 0) >= v:
                    continue
                kb.eng[e].wait_ge(sem, v)
                kb.seen[e][k] = v

    def load_w(self, dst_tile, dst_res, src_ap, nsplit):
        X = src_ap.shape[1]
        step = (X + nsplit - 1) // nsplit
        for x0 in range(0, X, step):
            x1 = min(X, x0 + step)
            self.kb.dma("pool", dst_tile[:, x0:x1, :], src_ap[:, x0:x1, :], writes=[dst_res])

    def consts(self):
        kb = self.kb
        self.ones_bf = kb.sb([P, P], BF16, "ones")
        self.r_const = Res("const")
        kb.op("dve", lambda e: e.memset(self.ones_bf[:], 1.0), writes=[self.r_const])
        self.eps_t = kb.sb([P, 1], F32, "eps")
        kb.op("dve", lambda e: e.memset(self.eps_t[:], NORM_EPS), writes=[self.r_const])

    def sumsq_rstd(self, src_chunks, src_res, w, sq_pool, rstd_tile, rstd_res, nchunk=KC, eps_ap=None, scale=1.0 / D):
        kb = self.kb
        sq, sqres = sq_pool.next()
        for c in range(nchunk):
            kb.op("act", lambda e, c=c: e.activation(sq[:, c, 0:w], src_chunks(c), AF.Square),
                  reads=[src_res(c)], writes=[sqres[c]])
        pt, pr = kb.ps()
        kb.mm(pt[:, 0:w], pr, [(self.ones_bf[:], sq[:, c, 0:w], [sqres[c], self.r_const]) for c in range(nchunk)])
        kb.op("act", lambda e: e.activation(rstd_tile[:, 0:w], pt[:, 0:w], AF.Sqrt,
                                            bias=(eps_ap if eps_ap is not None else self.eps_t[:, 0:1]), scale=scale),
              reads=[pr, self.r_const], writes=[rstd_res])
        kb.op("dve", lambda e: e.reciprocal(rstd_tile[:, 0:w], rstd_tile[:, 0:w]), reads=[rstd_res], writes=[rstd_res])

    def setup(self):
        cfg, kb, nc = self.cfg, self.kb, self.nc
        L = cfg.depth
        NT = cfg.ntok
        kb.init_psum()
        self.consts()
        self.xin = self.inp("xin", [P, KC, NT])
        self.cT = self.inp("cT", [P, KC * NSEQ])
        self.w_mod = self.inp("w_mod", [L, P, KC, 9 * D])
        self.b_mod = self.inp("b_mod", [P, L * 72])
        self.g_norm = self.inp("g_norm", [P, L * 6 * KC])
        self.w_in = self.inp("w_in", [L, 2, P, KC, 2 * DFF])
        self.w_out = self.inp("w_out", [L, 2, P, FC, D])
        self.xs = kb.dram("xs", [P, KC, NT], F32)
        self.hid = kb.dram("hid", [P, FC, NT], BF16)
        self.xs_res = [Res(f"xs{i}") for i in range(len(cfg.tiles))]
        self.hid_res = [Res(f"hid{i}") for i in range(len(cfg.tiles))]
        self.modv = kb.sb([P, L, 72, NSEQ], F32, "modv")
        self.AV = kb.sb([P, L * 3, KC, NSEQ], F32, "AV")
        self.CV = kb.sb([P, L * 3, KC, NSEQ], F32, "CV")
        self.gn = kb.sb([P, L * 6, KC], F32, "gn")
        self.r_mod = Res("mod")
        self.first_x = True
        self.compute_mods()

    def compute_mods(self):
        cfg, kb = self.cfg, self.kb
        L = cfg.depth
        with self.scope():
            cf = kb.sb([P, KC * NSEQ], F32, "cf")
            cb = kb.sb([P, KC, NSEQ], BF16, "cb")
            bm = kb.sb([P, L * 72], F32, "bm")
            rc = Res("c")
            kb.dma("sp", cf[:], self.cT[:, :], writes=[rc])
            kb.dma("sp", bm[:], self.b_mod[:, :], writes=[rc])
            kb.dma("sp", self.gn[:].rearrange("p a c -> p (a c)"), self.g_norm[:, :], writes=[self.r_mod])
            kb.op("act", lambda e: e.activation(cb[:].rearrange("p c s -> p (c s)"), cf[:], AF.Silu), reads=[rc], writes=[rc])
            wpool = Rot(kb, 2, [P, KC, 1152], BF16, "wm")
            for l in range(L):
                pt, pr = kb.ps()
                for blk in range(8):
                    wt, wr = wpool.next()
                    self.load_w(wt, wr[0], self.w_mod[l, :, :, blk * 1152:(blk + 1) * 1152], 2)
                    for o in range(9):
                        oc = blk * 9 + o
                        kb.mm(pt[:, oc * NSEQ:(oc + 1) * NSEQ], pr,
                              [(wt[:, kc, o * 128:(o + 1) * 128], cb[:, kc, :], [wr[0], rc]) for kc in range(KC)])
                kb.op("dve", lambda e, l=l, pt=pt: e.tensor_tensor(
                    out=self.modv[:, l, :, :], in0=pt[:, 0:72 * NSEQ].rearrange("p (a s) -> p a s", s=NSEQ),
                    in1=bm[:, l * 72:(l + 1) * 72].unsqueeze(2).to_broadcast([P, 72, NSEQ]), op=ALU.add),
                    reads=[pr, rc], writes=[self.r_mod])
                for s in range(3):
                    wres = 1.0 if s == 1 else 0.5
                    gpre = self.gn[:, (l * 3 + s) * 2 + 0, :].unsqueeze(2).to_broadcast([P, KC, NSEQ])
                    gpost = self.gn[:, (l * 3 + s) * 2 + 1, :].unsqueeze(2).to_broadcast([P, KC, NSEQ])
                    sc = self.modv[:, l, (s * 3 + 1) * KC:(s * 3 + 2) * KC, :]
                    gt = self.modv[:, l, (s * 3 + 2) * KC:(s * 3 + 3) * KC, :]
                    kb.op("dve", lambda e, sc=sc, gpre=gpre, l=l, s=s: e.scalar_tensor_tensor(
                        out=self.AV[:, l * 3 + s, :, :], in0=sc, scalar=1.0, in1=gpre, op0=ALU.add, op1=ALU.mult),
                        reads=[self.r_mod], writes=[self.r_mod])
                    kb.op("dve", lambda e, gt=gt, gpost=gpost, l=l, s=s, wres=wres: e.scalar_tensor_tensor(
                        out=self.CV[:, l * 3 + s, :, :], in0=gt, scalar=wres, in1=gpost, op0=ALU.mult, op1=ALU.mult),
                        reads=[self.r_mod], writes=[self.r_mod])

    def vA(self, l, s, c, q):
        return self.AV[:, l * 3 + s, c, q:q + 1]

    def vB(self, l, s, c, q):
        return self.modv[:, l, (s * 3) * KC + c, q:q + 1]

    def vC(self, l, s, c, q):
        return self.CV[:, l * 3 + s, c, q:q + 1]

    def x_src(self):
        if self.first_x:
            return self.xin
        return self.xs

    def load_x(self, pool, tile):
        c0, w, segs, ti = tile
        xt, xr = pool.next()
        self.kb.dma("sp", xt[:, :, 0:w], self.x_src()[:, :, c0:c0 + w], reads=[self.xs_res[ti]], writes=xr)
        return xt, xr

    def modulate(self, l, s, tile, xt, xr, out_fn, out_res, pools, ident=None):
        kb = self.kb
        c0, w, segs, ti = tile
        rstd, rres = pools["rstd"].next()
        self.sumsq_rstd(lambda c: xt[:, c, 0:w], lambda c: xr[c], w, pools["sq"], rstd, rres[0])
        for c in range(KC):
            for (s0, sw, q) in segs:
                tmp, tr = pools["tmp"].next()
                a_ap = self.vA(l, s, c, q) if ident is None else ident(c)
                kb.op("dve", lambda e, c=c, s0=s0, sw=sw, tmp=tmp, a_ap=a_ap: e.scalar_tensor_tensor(
                    out=tmp[:, 0:sw], in0=xt[:, c, s0:s0 + sw], scalar=a_ap, in1=rstd[:, s0:s0 + sw],
                    op0=ALU.mult, op1=ALU.mult), reads=[xr[c], rres[0], self.r_mod], writes=[tr[0]])
                if ident is None:
                    b_ap = self.vB(l, s, c, q)
                    kb.op("act", lambda e, c=c, s0=s0, sw=sw, tmp=tmp, b_ap=b_ap: e.activation(
                        out_fn(c, s0, sw), tmp[:, 0:sw], AF.Identity, bias=b_ap, scale=1.0),
                        reads=[tr[0], self.r_mod], writes=[out_res[c]])
                else:
                    kb.op("act", lambda e, c=c, s0=s0, sw=sw, tmp=tmp: e.activation(
                        out_fn(c, s0, sw), tmp[:, 0:sw], AF.Copy), reads=[tr[0]], writes=[out_res[c]])

    def post_residual(self, l, s, tile, y_fn, y_res, xt, xr, pools):
        kb = self.kb
        c0, w, segs, ti = tile
        rstd, rres = pools["rstd"].next()
        self.sumsq_rstd(y_fn, y_res, w, pools["sq"], rstd, rres[0])
        for c in range(KC):
            for (s0, sw, q) in segs:
                tmp, tr = pools["tmp"].next()
                kb.op("dve", lambda e, c=c, s0=s0, sw=sw, tmp=tmp, q=q: e.scalar_tensor_tensor(
                    out=tmp[:, 0:sw], in0=y_fn(c)[:, s0:s0 + sw], scalar=self.vC(l, s, c, q), in1=rstd[:, s0:s0 + sw],
                    op0=ALU.mult, op1=ALU.mult), reads=[y_res(c), rres[0], self.r_mod], writes=[tr[0]])
                kb.op("pool", lambda e, c=c, s0=s0, sw=sw, tmp=tmp: e.tensor_tensor(
                    out=xt[:, c, s0:s0 + sw], in0=xt[:, c, s0:s0 + sw], in1=tmp[:, 0:sw], op=ALU.add),
                    reads=[tr[0], xr[c]], writes=[xr[c]])
        kb.dma("pool", self.xs[:, :, c0:c0 + w], xt[:, :, 0:w], reads=xr, writes=[self.xs_res[ti]])

    def std_pools(self):
        kb = self.kb
        return {
            "rstd": Rot(kb, 2, [P, 512], F32, "rstd"),
            "sq": Rot(kb, 1, [P, KC, 512], BF16, "sq", nres=KC),
            "tmp": Rot(kb, 3, [P, 512], F32, "tmp"),
        }

    def ffn(self, l, s):
        cfg, kb = self.cfg, self.kb
        fi = 0 if s == 0 else 1
        ntile = len(cfg.tiles)
        with self.scope():
            win = kb.sb([P, KC, 2 * DFF], BF16, "win")
            rw = Res("win")
            self.load_w(win, rw, self.w_in[l, fi], 8)
            pools = self.std_pools()
            xpool = Rot(kb, 2, [P, KC, 512], F32, "xt", nres=KC)
            hpool = Rot(kb, 2, [P, KC, 512], BF16, "ht", nres=KC)
            spool = Rot(kb, 2, [P, 11, 512], BF16, "hs", nres=1)
            pools["tmp"] = Rot(kb, 4, [P, 512], F32, "tmpA")
            gpool = Rot(kb, 3, [P, 512], F32, "sg")
            def prep(tile):
                xt, xr = self.load_x(xpool, tile)
                ht, hr = hpool.next()
                self.modulate(l, s, tile, xt, xr, lambda c, s0, sw: ht[:, c, s0:s0 + sw], hr, pools)
                return ht, hr
            nxt = prep(cfg.tiles[0])
            for i, tile in enumerate(cfg.tiles):
                c0, w, segs, ti = tile
                ht, hr = nxt
                if i + 1 < ntile:
                    nxt = prep(cfg.tiles[i + 1])
                for half in range(2):
                    st, sr = spool.next()
                    for jj in range(11):
                        j = half * 11 + jj
                        pg, rg = kb.ps()
                        kb.mm(pg[:, 0:w], rg, [(win[:, kc, j * 128:(j + 1) * 128], ht[:, kc, 0:w], [rw, hr[kc]]) for kc in range(KC)])
                        pu, ru = kb.ps()
                        kb.mm(pu[:, 0:w], ru, [(win[:, kc, DFF + j * 128:DFF + (j + 1) * 128], ht[:, kc, 0:w], [rw, hr[kc]]) for kc in range(KC)])
                        sg, sgr = gpool.next()
                        kb.op("act", lambda e, sg=sg, pg=pg: e.activation(sg[:, 0:w], pg[:, 0:w], AF.Silu), reads=[rg], writes=[sgr[0]])
                        kb.op("dve", lambda e, sg=sg, pu=pu, st=st, jj=jj: e.tensor_tensor(
                            out=st[:, jj, 0:w], in0=sg[:, 0:w], in1=pu[:, 0:w], op=ALU.mult), reads=[sgr[0], ru], writes=[sr[0]])
                    kb.dma("pool", self.hid[:, half * 11:(half + 1) * 11, c0:c0 + w], st[:, :, 0:w], reads=[sr[0]],
                           writes=[self.hid_res[ti]])
        with self.scope():
            wout = kb.sb([P, FC, D], BF16, "wout")
            rw = Res("wout")
            self.load_w(wout, rw, self.w_out[l, fi], 4)
            pools = self.std_pools()
            xpool = Rot(kb, 2, [P, KC, 512], F32, "xt", nres=KC)
            ipool = Rot(kb, 2, [P, FC, 512], BF16, "hin", nres=1)
            ypool = Rot(kb, 2, [P, KC, 512], F32, "y", nres=KC)
            pend = None
            for tile in cfg.tiles:
                c0, w, segs, ti = tile
                hin, hir = ipool.next()
                kb.dma("sp", hin[:, :, 0:w], self.hid[:, :, c0:c0 + w], reads=[self.hid_res[ti]], writes=hir)
                xt, xr = self.load_x(xpool, tile)
                yt, yr = ypool.next()
                for oc in range(KC):
                    py, ry = kb.ps()
                    kb.mm(py[:, 0:w], ry, [(wout[:, kc, oc * 128:(oc + 1) * 128], hin[:, kc, 0:w], [rw, hir[0]]) for kc in range(FC)])
                    kb.op("act", lambda e, oc=oc, py=py, yt=yt: e.copy(yt[:, oc, 0:w], py[:, 0:w]), reads=[ry], writes=[yr[oc]])
                if pend is not None:
                    pend()
                pend = (lambda tile=tile, yt=yt, yr=yr, xt=xt, xr=xr, w=w: self.post_residual(
                    l, s, tile, lambda c: yt[:, c, 0:w], lambda c: yr[c], xt, xr, pools))
            pend()
        self.first_x = False

    def setup_rwkv(self):
        cfg, kb = self.cfg, self.kb
        na, NT = cfg.n_a, cfg.ntok
        self.mu = self.inp("mu", [P, na * 6 * KC])
        self.rvec = self.inp("rvec", [P, na * 5 * KC])
        self.v0_64 = self.inp("v0_64", [HD, max(1, na - 1) * NH])
        self.lnx64 = self.inp("lnx64", [HD, na * 2 * NH])
        self.w_rkv = self.inp("w_rkv", [na, 3, P, KC, D])
        self.w_o64 = self.inp("w_o64", [na, HD, NH, D])
        self.w1 = self.inp("w1", [na, P, KC, 64])
        self.a1 = self.inp("a1", [na, P, KC, 64])
        self.v1 = self.inp("v1", [max(1, na - 1), P, KC, 32])
        self.g1 = self.inp("g1", [na, P, KC, 160])
        self.w2 = self.inp("w2", [na, 64, 1, D])
        self.a2 = self.inp("a2", [na, 64, 1, D])
        self.v2 = self.inp("v2", [max(1, na - 1), 32, 1, D])
        self.g2 = self.inp("g2", [na, 160, 1, D])
        self.sh0 = self.inp("sh0", [P, na * NS * KC])
        self.hst0 = self.inp("hst0", [na, NS, P, KC * HD])
        self.o_shift = self.outp("o_shift", [P, na * NSEQ * KC])
        self.o_wkv = self.outp("o_wkv", [na, NSEQ, P, KC * HD])
        names = ["rt", "kt", "at", "kkt", "eg"]
        self.wk_d = {n: kb.dram("wk_" + n, [P, KC, NT], F32) for n in names}
        for n in ["v64", "g64", "bn64", "o64", "vf64"]:
            self.wk_d[n] = kb.dram("wk_" + n, [HD, NH, NT], F32)
        self.wk_res = {n: Res("wk_" + n) for n in self.wk_d}
        self.tiles256 = []
        for t in range(cfg.seq // 256):
            self.tiles256.append((t * 256, 256, [(0, 256, 0)], t // 2))
        self.tiles256.append((cfg.seq, NS * SLOT, [(s * SLOT, SLOT, 1 + s) for s in range(NS)], len(cfg.tiles) - 1))

    def rwkv_proj(self, l):
        cfg, kb = self.cfg, self.kb
        na = cfg.n_a
        W = 256
        with self.scope():
            rW = Res("rw")
            wr = kb.sb([P, KC, D], BF16, "wr")
            wk = kb.sb([P, KC, D], BF16, "wk")
            wv = kb.sb([P, KC, D], BF16, "wv")
            for t, i in ((wr, 0), (wk, 1), (wv, 2)):
                self.load_w(t, rW, self.w_rkv[l, i], 2)
            w1 = kb.sb([P, KC, 64], BF16, "w1")
            a1 = kb.sb([P, KC, 64], BF16, "a1")
            g1 = kb.sb([P, KC, 160], BF16, "g1")
            self.load_w(w1, rW, self.w1[l], 1)
            self.load_w(a1, rW, self.a1[l], 1)
            self.load_w(g1, rW, self.g1[l], 1)
            w2 = kb.sb([64, 1, D], BF16, "w2")
            a2 = kb.sb([64, 1, D], BF16, "a2")
            g2a = kb.sb([P, 1, D], BF16, "g2a")
            g2b = kb.sb([32, 1, D], BF16, "g2b")
            self.load_w(w2, rW, self.w2[l], 1)
            self.load_w(a2, rW, self.a2[l], 1)
            self.load_w(g2a, rW, self.g2[l, 0:128], 1)
            self.load_w(g2b, rW, self.g2[l, 128:160], 1)
            if l > 0:
                v1 = kb.sb([P, KC, 32], BF16, "v1")
                v2 = kb.sb([32, 1, D], BF16, "v2")
                self.load_w(v1, rW, self.v1[l - 1], 1)
                self.load_w(v2, rW, self.v2[l - 1], 1)
                v0 = kb.sb([HD, NH], F32, "v0")
                kb.dma("sp", v0[:], self.v0_64[:, (l - 1) * NH:l * NH], writes=[rW])
            mu = kb.sb([P, 6, KC], F32, "mu")
            kb.dma("sp", mu[:].rearrange("p a c -> p (a c)"), self.mu[:, l * 6 * KC:(l + 1) * 6 * KC], writes=[rW])
            rv = kb.sb([P, 5, KC], F32, "rv")
            kb.dma("sp", rv[:].rearrange("p a c -> p (a c)"), self.rvec[:, l * 5 * KC:(l + 1) * 5 * KC], writes=[rW])
            sh0 = kb.sb([P, NS, KC], F32, "sh0")
            kb.dma("sp", sh0[:].rearrange("p a c -> p (a c)"), self.sh0[:, l * NS * KC:(l + 1) * NS * KC], writes=[rW])
            bones = kb.sb([P, P], BF16, "bones")
            kb.op("pool", lambda e: e.memset(bones[:], 0.0), writes=[rW])
            kb.op("pool", lambda e: e.memset(bones[0:64, 0:64], 1.0), writes=[rW])
            kb.op("pool", lambda e: e.memset(bones[64:128, 64:128], 1.0), writes=[rW])
            mask64 = kb.sb([P, W], F32, "m64")
            mask32 = kb.sb([P, NS * SLOT], F32, "m32")
            kb.op("pool", lambda e: e.memset(mask64[:], 1.0), writes=[rW])
            kb.op("pool", lambda e: e.memset(mask64[:].rearrange("p (a b) -> p a b", b=64)[:, :, 0:1], 0.0), writes=[rW])
            kb.op("pool", lambda e: e.memset(mask32[:], 1.0), writes=[rW])
            kb.op("pool", lambda e: e.memset(mask32[:].rearrange("p (a b) -> p a b", b=SLOT)[:, :, 0:1], 0.0), writes=[rW])
            carry = kb.sb([P, KC, 2], F32, "carry")
            rcar = Res("carry")
            kb.op("pool", lambda e: e.memset(carry[:], 0.0), writes=[rcar])
            tiny = kb.sb([P, 1], F32, "tiny")
            kb.op("pool", lambda e: e.memset(tiny[:], 1e-24), writes=[rW])
            osh = kb.sb([P, NSEQ, KC], F32, "osh")
            rosh = Res("osh")

            pools = {"rstd": Rot(kb, 2, [P, W], F32, "rstd"), "sq": Rot(kb, 1, [P, KC, W], BF16, "sq", nres=KC),
                     "tmp": Rot(kb, 3, [P, W], F32, "tmp")}
            xpool = Rot(kb, 1, [P, KC, W], F32, "xt", nres=KC)
            hfp = Rot(kb, 1, [P, KC, W], F32, "hf", nres=KC)
            xxp = Rot(kb, 1, [P, KC, W], F32, "xx", nres=1)
            mixp = [Rot(kb, 1, [P, KC, W], BF16, f"mix{m}", nres=1) for m in range(6)]
            lorp = {n: Rot(kb, 1, [sz, W], BF16, n) for n, sz in (("tw", 64), ("ta", 64), ("tv", 32), ("tg0", 128), ("tg1", 32))}
            t32 = Rot(kb, 14, [P, W], F32, "t32")
            tb16 = Rot(kb, 3, [P, W], BF16, "tb16")
            stg = Rot(kb, 2, [P, 5, W], F32, "stg")
            v64p = Rot(kb, 1, [HD, NH, W], F32, "v64t")
            b64p = Rot(kb, 1, [HD, NH, W], F32, "b64t")
            t64 = Rot(kb, 8, [HD, W], F32, "t64")
            MI = {"r": 0, "w": 1, "k": 2, "v": 3, "a": 4, "g": 5}

            x512 = {}
            for (c0, w, segs, ti5) in self.tiles256:
                is_samp = (c0 >= cfg.seq)
                xt_full, xr = xpool.next()
                kb.dma("sp", xt_full[:, :, 0:w], self.x_src()[:, :, c0:c0 + w], reads=[self.xs_res[ti5]], writes=xr)
                off = 0
                hf, hr = hfp.next()
                self._mod256(l, ti5, off, w, segs, xt_full, xr, hf, hr, pools)
                xx, xxr = xxp.next()
                for (s0, sw, q) in segs:
                    kb.op("pool", lambda e, s0=s0, sw=sw: e.tensor_tensor(
                        out=xx[:, :, s0 + 1:s0 + sw], in0=hf[:, :, s0:s0 + sw - 1], in1=hf[:, :, s0 + 1:s0 + sw], op=ALU.subtract),
                        reads=hr, writes=[xxr[0]])
                    prev = carry[:, :, 0:1] if not is_samp else sh0[:, q - 1, :].unsqueeze(2)
                    kb.op("pool", lambda e, s0=s0, prev=prev: e.tensor_tensor(
                        out=xx[:, :, s0:s0 + 1], in0=prev, in1=hf[:, :, s0:s0 + 1], op=ALU.subtract),
                        reads=hr + [rcar, rW], writes=[xxr[0]])
                    last = s0 + sw - 1 if not is_samp else s0 + DEC - 1
                    if not is_samp:
                        kb.op("pool", lambda e, last=last: e.tensor_copy(out=carry[:, :, 0:1], in_=hf[:, :, last:last + 1]),
                              reads=hr, writes=[rcar])
                    if is_samp or c0 + w == cfg.seq:
                        kb.op("pool", lambda e, last=last, q=q: e.tensor_copy(out=osh[:, q, :].unsqueeze(2), in_=hf[:, :, last:last + 1]),
                              reads=hr, writes=[rosh])
                mixes = []
                for m in range(6):
                    if m == MI["v"] and False:
                        pass
                    mt, mr = mixp[m].next()
                    for c in range(KC):
                        kb.op("dve", lambda e, m=m, c=c, mt=mt: e.scalar_tensor_tensor(
                            out=mt[:, c, 0:w], in0=xx[:, c, 0:w], scalar=mu[:, m, c:c + 1], in1=hf[:, c, 0:w],
                            op0=ALU.mult, op1=ALU.add), reads=[xxr[0], hr[c], rW], writes=[mr[0]])
                    mixes.append((mt, mr[0]))
                xr_, xw_, xk_, xv_, xa_, xg_ = mixes

                def proj(out_ap, out_res, wt, col0, ncol, mix, kparts=P):
                    kb.mm(out_ap, out_res, [(wt[:, kc, col0:col0 + ncol], mix[0][:, kc, 0:w], [rW, mix[1]]) for kc in range(KC)])

                tw, twr = lorp["tw"].next()
                pt, pr = kb.ps()
                proj(pt[0:64, 0:w], pr, w1, 0, 64, xw_)
                kb.op("act", lambda e: e.activation(tw[:, 0:w], pt[0:64, 0:w], AF.Tanh), reads=[pr], writes=[twr[0]])
                ta, tar = lorp["ta"].next()
                pt, pr = kb.ps()
                proj(pt[0:64, 0:w], pr, a1, 0, 64, xa_)
                kb.op("act", lambda e, pt=pt: e.copy(ta[:, 0:w], pt[0:64, 0:w]), reads=[pr], writes=[tar[0]])
                tg0, tg0r = lorp["tg0"].next()
                pt, pr = kb.ps()
                proj(pt[:, 0:w], pr, g1, 0, 128, xg_)
                kb.op("act", lambda e, pt=pt: e.activation(tg0[:, 0:w], pt[:, 0:w], AF.Sigmoid), reads=[pr], writes=[tg0r[0]])
                tg1, tg1r = lorp["tg1"].next()
                pt, pr = kb.ps()
                proj(pt[0:32, 0:w], pr, g1, 128, 32, xg_)
                kb.op("act", lambda e, pt=pt: e.activation(tg1[:, 0:w], pt[0:32, 0:w], AF.Sigmoid), reads=[pr], writes=[tg1r[0]])
                if l > 0:
                    tv, tvr = lorp["tv"].next()
                    pt, pr = kb.ps()
                    proj(pt[0:32, 0:w], pr, v1, 0, 32, xv_)
                    kb.op("act", lambda e, pt=pt: e.copy(tv[:, 0:w], pt[0:32, 0:w]), reads=[pr], writes=[tvr[0]])

                v64t, v64r = v64p.next()
                for h in range(NH):
                    pv, prv = kb.ps()
                    proj(pv[0:64, 0:w], prv, wv, h * 64, 64, xv_)
                    if l == 0:
                        kb.op("act", lambda e, h=h, pv=pv: e.copy(v64t[:, h, 0:w], pv[0:64, 0:w]), reads=[prv], writes=[v64r[0]])
                    else:
                        pl, prl = kb.ps()
                        kb.mm(pl[0:64, 0:w], prl, [(v2[:, 0, h * 64:(h + 1) * 64], tv[:, 0:w], [rW, tvr[0]])])
                        vg, vgr = t64.next()
                        kb.op("act", lambda e, h=h, pl=pl, vg=vg: e.activation(vg[:, 0:w], pl[0:64, 0:w], AF.Sigmoid, bias=v0[:, h:h + 1], scale=1.0),
                              reads=[prl, rW], writes=[vgr[0]])
                        dd, ddr = t64.next()
                        kb.dma("sp", dd[:, 0:w], self.wk_d["vf64"][:, h, c0:c0 + w], reads=[self.wk_res["vf64"]], writes=[ddr[0]])
                        kb.op("dve", lambda e, h=h, pv=pv, dd=dd: e.tensor_tensor(out=dd[:, 0:w], in0=dd[:, 0:w], in1=pv[0:64, 0:w], op=ALU.subtract),
                              reads=[prv, ddr[0]], writes=[ddr[0]])
                        kb.op("pool", lambda e, dd=dd, vg=vg: e.tensor_tensor(out=dd[:, 0:w], in0=dd[:, 0:w], in1=vg[:, 0:w], op=ALU.mult),
                              reads=[ddr[0], vgr[0]], writes=[ddr[0]])
                        kb.op("dve", lambda e, h=h, pv=pv, dd=dd: e.tensor_tensor(out=v64t[:, h, 0:w], in0=dd[:, 0:w], in1=pv[0:64, 0:w], op=ALU.add),
                              reads=[prv, ddr[0]], writes=[v64r[0]])
                    pg, prg = kb.ps()
                    kb.mm(pg[0:64, 0:w], prg, [(g2a[:, 0, h * 64:(h + 1) * 64], tg0[:, 0:w], [rW, tg0r[0]]),
                                               (g2b[:, 0, h * 64:(h + 1) * 64], tg1[:, 0:w], [rW, tg1r[0]])])
                    gg, ggr = t64.next()
                    kb.op("act", lambda e, h=h, pg=pg, gg=gg: e.copy(gg[:, 0:w], pg[0:64, 0:w]), reads=[prg], writes=[ggr[0]])
                    kb.dma("pool", self.wk_d["g64"][:, h, c0:c0 + w], gg[:, 0:w], reads=[ggr[0]], writes=[self.wk_res["g64"]])
                kb.dma("pool", self.wk_d["v64"][:, :, c0:c0 + w], v64t[:, :, 0:w], reads=v64r, writes=[self.wk_res["v64"]])
                if l == 0:
                    kb.dma("pool", self.wk_d["vf64"][:, :, c0:c0 + w], v64t[:, :, 0:w], reads=v64r, writes=[self.wk_res["vf64"]])

                b64t, b64r = b64p.next()
                msk = mask32 if is_samp else mask64
                for c in range(KC):
                    cs = slice(c * 128, (c + 1) * 128)
                    p_r, r_r = kb.ps()
                    proj(p_r[:, 0:w], r_r, wr, c * 128, 128, xr_)
                    p_k, r_k = kb.ps()
                    proj(p_k[:, 0:w], r_k, wk, c * 128, 128, xk_)
                    p_w, r_w = kb.ps()
                    kb.mm(p_w[:, 0:w], r_w, [(w2[:, 0, cs], tw[:, 0:w], [rW, twr[0]])])
                    p_a, r_a = kb.ps()
                    kb.mm(p_a[:, 0:w], r_a, [(a2[:, 0, cs], ta[:, 0:w], [rW, tar[0]])])
                    T = {}

                    def tmp(name):
                        T[name] = t32.next()
                        return T[name][0]
                    a_ = tmp("a")
                    kb.op("act", lambda e: e.activation(a_[:, 0:w], p_a[:, 0:w], AF.Sigmoid, bias=rv[:, 1, c:c + 1], scale=1.0),
                          reads=[r_a, rW], writes=[T["a"][1][0]])
                    lw = tmp("lw")
                    kb.op("act", lambda e: e.activation(lw[:, 0:w], p_w[:, 0:w], AF.Sigmoid, bias=rv[:, 0, c:c + 1], scale=1.0),
                          reads=[r_w, rW], writes=[T["lw"][1][0]])
                    kb.op("dve", lambda e: e.tensor_scalar(out=lw[:, 0:w], in0=lw[:, 0:w], scalar1=-0.6065306597126334, scalar2=None, op0=ALU.mult),
                          reads=[T["lw"][1][0]], writes=[T["lw"][1][0]])
                    gc = tmp("gc")
                    kb.op("dve", lambda e: e.tensor_tensor_scan(out=gc[:, 0:w], data0=msk[:, 0:w], data1=lw[:, 0:w], initial=0.0,
                                                                 op0=ALU.mult, op1=ALU.add),
                          reads=[T["lw"][1][0], rW], writes=[T["gc"][1][0]])
                    st, sr = stg.next()
                    kb.op("act", lambda e: e.activation(st[:, 4, 0:w], gc[:, 0:w], AF.Exp), reads=[T["gc"][1][0]], writes=[sr[0]])
                    eng_ = tmp("eng")
                    kb.op("act", lambda e: e.activation(eng_[:, 0:w], gc[:, 0:w], AF.Exp, scale=-1.0), reads=[T["gc"][1][0]], writes=[T["eng"][1][0]])
                    egm = tmp("egm")
                    kb.op("pool", lambda e: e.tensor_tensor(out=egm[:, 0:w], in0=gc[:, 0:w], in1=lw[:, 0:w], op=ALU.subtract),
                          reads=[T["gc"][1][0], T["lw"][1][0]], writes=[T["egm"][1][0]])
                    kb.op("act", lambda e: e.activation(egm[:, 0:w], egm[:, 0:w], AF.Exp), reads=[T["egm"][1][0]], writes=[T["egm"][1][0]])
                    kkn = tmp("kkn")
                    kb.op("dve", lambda e: e.tensor_scalar(out=kkn[:, 0:w], in0=p_k[:, 0:w], scalar1=rv[:, 2, c:c + 1], scalar2=None, op0=ALU.mult),
                          reads=[r_k, rW], writes=[T["kkn"][1][0]])
                    k2, k2r = tb16.next()
                    kb.op("act", lambda e: e.activation(k2[:, 0:w], kkn[:, 0:w], AF.Square), reads=[T["kkn"][1][0]], writes=[k2r[0]])
                    p_s, r_s = kb.ps()
                    kb.mm(p_s[:, 0:w], r_s, [(bones[:], k2[:, 0:w], [rW, k2r[0]])])
                    rn = tmp("rn")
                    kb.op("dve", lambda e: e.tensor_scalar(out=rn[:, 0:w], in0=p_s[:, 0:w], scalar1=1e-24, scalar2=None, op0=ALU.max),
                          reads=[r_s], writes=[T["rn"][1][0]])
                    kb.op("act", lambda e: e.activation(rn[:, 0:w], rn[:, 0:w], AF.Sqrt), reads=[T["rn"][1][0]], writes=[T["rn"][1][0]])
                    kb.op("dve", lambda e: e.reciprocal(rn[:, 0:w], rn[:, 0:w]), reads=[T["rn"][1][0]], writes=[T["rn"][1][0]])
                    kb.op("pool", lambda e: e.tensor_tensor(out=kkn[:, 0:w], in0=kkn[:, 0:w], in1=rn[:, 0:w], op=ALU.mult),
                          reads=[T["rn"][1][0], T["kkn"][1][0]], writes=[T["kkn"][1][0]])
                    kp = tmp("kp")
                    kb.op("dve", lambda e: e.tensor_scalar(out=kp[:, 0:w], in0=a_[:, 0:w], scalar1=-1.0, scalar2=rv[:, 3, c:c + 1], op0=ALU.add, op1=ALU.mult),
                          reads=[T["a"][1][0], rW], writes=[T["kp"][1][0]])
                    kb.op("dve", lambda e: e.scalar_tensor_tensor(out=kp[:, 0:w], in0=kp[:, 0:w], scalar=1.0, in1=p_k[:, 0:w], op0=ALU.add, op1=ALU.mult),
                          reads=[T["kp"][1][0], r_k], writes=[T["kp"][1][0]])
                    kb.op("pool", lambda e: e.tensor_tensor(out=a_[:, 0:w], in0=a_[:, 0:w], in1=kkn[:, 0:w], op=ALU.mult),
                          reads=[T["a"][1][0], T["kkn"][1][0]], writes=[T["a"][1][0]])
                    pb_, pbr = tb16.next()
                    kb.op("dve", lambda e: e.scalar_tensor_tensor(out=pb_[:, 0:w], in0=p_r[:, 0:w], scalar=rv[:, 4, c:c + 1], in1=kp[:, 0:w], op0=ALU.mult, op1=ALU.mult),
                          reads=[r_r, T["kp"][1][0], rW], writes=[pbr[0]])
                    for half in range(2):
                        h = 2 * c + half
                        p_b, r_b = kb.ps()
                        kb.mm(p_b[0:64, 0:w], r_b, [(bones[:, half * 64:(half + 1) * 64], pb_[:, 0:w], [rW, pbr[0]])])
                        kb.op("dve", lambda e, h=h, p_b=p_b: e.tensor_tensor(out=b64t[:, h, 0:w], in0=v64t[:, h, 0:w], in1=p_b[0:64, 0:w], op=ALU.mult),
                              reads=[r_b, v64r[0]], writes=[b64r[0]])
                    kb.op("dve", lambda e: e.tensor_tensor(out=st[:, 0, 0:w], in0=st[:, 4, 0:w], in1=p_r[:, 0:w], op=ALU.mult),
                          reads=[r_r, sr[0]], writes=[sr[0]])
                    kb.op("pool", lambda e: e.tensor_tensor(out=st[:, 1, 0:w], in0=kp[:, 0:w], in1=eng_[:, 0:w], op=ALU.mult),
                          reads=[T["kp"][1][0], T["eng"][1][0]], writes=[sr[0]])
                    kb.op("pool", lambda e: e.tensor_tensor(out=st[:, 2, 0:w], in0=a_[:, 0:w], in1=eng_[:, 0:w], op=ALU.mult),
                          reads=[T["a"][1][0], T["eng"][1][0]], writes=[sr[0]])
                    kb.op("pool", lambda e: e.tensor_tensor(out=st[:, 3, 0:w], in0=kkn[:, 0:w], in1=egm[:, 0:w], op=ALU.mult),
                          reads=[T["kkn"][1][0], T["egm"][1][0]], writes=[sr[0]])
                    for i, n in enumerate(["rt", "kt", "at", "kkt", "eg"]):
                        kb.dma("sp" if i % 2 else "pool", self.wk_d[n][:, c, c0:c0 + w], st[:, i, 0:w], reads=[sr[0]], writes=[self.wk_res[n]])
                kb.dma("pool", self.wk_d["bn64"][:, :, c0:c0 + w], b64t[:, :, 0:w], reads=b64r, writes=[self.wk_res["bn64"]])
            kb.dma("pool", self.o_shift[:, l * NSEQ * KC:(l + 1) * NSEQ * KC], osh[:].rearrange("p a c -> p (a c)"), reads=[rosh])

    def _mod256(self, l, ti5, off, w, segs, xt, xr, hf, hr, pools):
        kb = self.kb
        rstd, rres = pools["rstd"].next()
        self.sumsq_rstd(lambda c: xt[:, c, off:off + w], lambda c: xr[c], w, pools["sq"], rstd, rres[0])
        for c in range(KC):
            for (s0, sw, q) in segs:
                tmp, tr = pools["tmp"].next()
                kb.op("dve", lambda e, c=c, s0=s0, sw=sw, tmp=tmp, q=q: e.scalar_tensor_tensor(
                    out=tmp[:, 0:sw], in0=xt[:, c, off + s0:off + s0 + sw], scalar=self.vA(l, 1, c, q), in1=rstd[:, s0:s0 + sw],
                    op0=ALU.mult, op1=ALU.mult), reads=[xr[c], rres[0], self.r_mod], writes=[tr[0]])
                kb.op("act", lambda e, c=c, s0=s0, sw=sw, tmp=tmp, q=q: e.activation(
                    hf[:, c, s0:s0 + sw], tmp[:, 0:sw], AF.Identity, bias=self.vB(l, 1, c, q), scale=1.0),
                    reads=[tr[0], self.r_mod], writes=[hr[c]])

    def rwkv_wkv(self, l):
        cfg, kb = self.cfg, self.kb
        HG = 8
        dbg = cfg.debug or ""
        with self.scope():
            rC = Res("wkvc")
            ones = kb.sb([P, P], F32, "ones32")
            ident = kb.sb([P, P], F32, "ident")
            mask = kb.sb([P, 5, P], F32, "mask")
            kb.op("pool", lambda e: e.memset(ones[:], 1.0), writes=[rC])
            kb.op("pool", lambda e: e.affine_select(out=ident[:], in_=ones[:], pattern=[[1, P]], compare_op=ALU.is_equal,
                                                    fill=0.0, base=0, channel_multiplier=-1), reads=[rC], writes=[rC])
            for i, (op_, cm, pat) in enumerate([(ALU.is_gt, -1, 1), (ALU.is_gt, 1, -1), (ALU.is_gt, -1, 1), (ALU.is_ge, -1, 1), (ALU.is_ge, -1, 1)]):
                kb.op("pool", lambda e, i=i, op_=op_, cm=cm, pat=pat: e.affine_select(
                    out=mask[:, i, :], in_=ones[:], pattern=[[pat, P]], compare_op=op_, fill=0.0, base=0, channel_multiplier=cm),
                    reads=[rC], writes=[rC])
            kb.op("pool", lambda e: e.memset(mask[0:64, :, 64:128], 0.0), reads=[rC], writes=[rC])
            kb.op("pool", lambda e: e.memset(mask[64:128, :, 0:64], 0.0), reads=[rC], writes=[rC])
            kb.op("pool", lambda e: e.tensor_scalar(out=mask[:, 4, :], in0=mask[:, 4, :], scalar1=-1.0, scalar2=None, op0=ALU.mult),
                  reads=[rC], writes=[rC])
            Hst = [kb.sb([P, KC, HD], F32, f"H{q}") for q in range(NSEQ)]
            Hres = [[Res(f"H{q}_{h}") for h in range(NH)] for q in range(NSEQ)]
            kb.op("pool", lambda e: e.memset(Hst[0][:], 0.0), writes=Hres[0])
            for q in range(1, NSEQ):
                kb.dma("sp", Hst[q][:].rearrange("p c i -> p (c i)"), self.hst0[l, q - 1], writes=Hres[q])

            def zrot(n, shape, name):
                r = Rot(kb, n, shape, F32, name)
                for t, rr in r.tiles:
                    kb.op("pool", lambda e, t=t: e.memset(t[:], 0.0), writes=rr)
                return r
            fpool = {n: Rot(kb, 2, [P, KC, P], F32, "f" + n) for n in ("kt", "at", "eg")}
            fmask = {(n, hf): zrot(2, [P, KC, P], f"fm{n}{hf}") for n in ("rt", "kkt") for hf in range(2)}
            vpool = Rot(kb, 1, [HD, NH, P], F32, "v64b")
            tkp = {(n, ch): zrot(1, [P, KC, P], f"{n}{ch}") for n in ("kTk", "naTk") for ch in range(2)}
            vtp = Rot(kb, 2, [P, NH, HD], F32, "vTk")
            osp = Rot(kb, 1, [HD, NH, P], F32, "ost")
            Gp = Rot(kb, HG + 2, [P, 5, P], F32, "G")
            XYp = Rot(kb, 4 * HG + 2, [P, P], F32, "XY")
            Pp = Rot(kb, HG + 2, [P, P], F32, "Pm")
            WUp = zrot(2 * (HG + 2), [P, HD], "WU")
            hgp = Rot(kb, HG + 2, [P, HD], F32, "hg")

            kb.psrot = list(range(6))
            zt, zr = osp.next()
            kb.op("pool", lambda e: e.memset(zt[:], 0.0), writes=zr)
            kb.dma("pool", self.wk_d["o64"][:, :, cfg.seq:cfg.seq + NS * SLOT], zt[:, :, 0:NS * SLOT], reads=zr, writes=[self.wk_res["o64"]])
            blocks = [(0, b * 128, 128, 64) for b in range(cfg.seq // 128)]
            blocks += [(q, cfg.seq + (q - 1) * SLOT, DEC, DEC) for q in range(1, NSEQ)]
            if "nosamp" in dbg:
                blocks = blocks[:cfg.seq // 128]
            if "noprompt" in dbg:
                blocks = blocks[cfg.seq // 128:]
            for (q, col0, BS, CL) in blocks:
                H = Hst[q]
                nch = BS // CL
                es = [2, 4, 8, 16, 32] if CL == 64 else [2, 4]
                F = {}
                for n in fpool:
                    t, r = fpool[n].next()
                    kb.dma("sp", t[:, :, 0:BS], self.wk_d[n][:, :, col0:col0 + BS], reads=[self.wk_res[n]], writes=r)
                    F[n] = (t, r[0])
                FM = {}
                for (n, hf) in fmask:
                    t, r = fmask[(n, hf)].next()
                    hs = slice(hf * 64, hf * 64 + 64)
                    kb.dma("sp", t[hs, :, 0:BS], self.wk_d[n][hs, :, col0:col0 + BS], reads=[self.wk_res[n]], writes=r)
                    FM[(n, hf)] = (t, r[0])
                vb, vbr = vpool.next()
                kb.dma("sp", vb[:, :, 0:BS], self.wk_d["v64"][:, :, col0:col0 + BS], reads=[self.wk_res["v64"]], writes=vbr)
                TK = {k_: tkp[k_].next() for k_ in tkp}
                vTk, vTr = vtp.next()
                for half4 in range(2):
                    for (src, nm, neg) in ((F["kt"], "kTk", False), (F["at"], "naTk", True)):
                        pt, pr = kb.ps()
                        for cc in range(4):
                            c = half4 * 4 + cc
                            kb.mm(pt[0:BS, cc * 128:(cc + 1) * 128], pr, [(src[0][:, c, 0:BS], ident[:], [src[1], rC])], transpose=True)
                        for ch in range(nch):
                            rows = slice(ch * CL, (ch + 1) * CL)
                            dst, dres = TK[(nm, ch)]
                            dview = dst[rows, half4 * 4:half4 * 4 + 4, :].rearrange("p a b -> p (a b)")
                            if neg:
                                kb.op("act", lambda e, pt=pt, dview=dview, rows=rows: e.mul(dview, pt[rows, :], -1.0), reads=[pr], writes=[dres[0]])
                            else:
                                kb.op("dve", lambda e, pt=pt, dview=dview, rows=rows: e.tensor_copy(out=dview, in_=pt[rows, :]), reads=[pr], writes=[dres[0]])
                    pt, pr = kb.ps()
                    for hh in range(8):
                        h = half4 * 8 + hh
                        kb.mm(pt[0:BS, hh * 64:(hh + 1) * 64], pr, [(vb[:, h, 0:BS], ident[0:64, 0:64], [vbr[0], rC])], transpose=True)
                    kb.op("act", lambda e, pt=pt: e.copy(vTk[0:BS, half4 * 8:half4 * 8 + 8, :].rearrange("p a b -> p (a b)"), pt[0:BS, :]),
                          reads=[pr], writes=[vTr[0]])
                ost, ostr = osp.next()
                if "tr" in dbg:
                    continue
                for hg0 in range(0, NH, HG):
                    heads = list(range(hg0, hg0 + HG))
                    S = {}
                    for h in heads:
                        c, hf = h // 2, h % 2
                        fk_, fa_ = F["kt"][0][:, c, 0:BS], F["at"][0][:, c, 0:BS]
                        fr_, fkk_ = FM[("rt", hf)][0][:, c, 0:BS], FM[("kkt", hf)][0][:, c, 0:BS]
                        rds = [F["kt"][1], F["at"][1], FM[("rt", hf)][1], FM[("kkt", hf)][1]]
                        pa, pra = kb.ps()
                        pb2, prb = kb.ps()
                        kb.mm(pa[0:BS, 0:BS], pra, [(fa_, fkk_, rds)])
                        kb.mm(pa[0:BS, 128:128 + BS], pra, [(fkk_, fa_, rds)])
                        kb.mm(pa[0:BS, 256:256 + BS], pra, [(fk_, fkk_, rds)])
                        kb.mm(pa[0:BS, 384:384 + BS], pra, [(fk_, fr_, rds)])
                        kb.mm(pb2[0:BS, 0:BS], prb, [(fa_, fr_, rds)])
                        G, Gr = Gp.next()
                        kb.op("dve", lambda e, G=G, pa=pa: e.tensor_tensor(
                            out=G[0:BS, 0:4, 0:BS], in0=pa[0:BS, :].rearrange("p (a b) -> p a b", b=128)[:, :, 0:BS],
                            in1=mask[0:BS, 0:4, 0:BS], op=ALU.mult), reads=[pra, rC], writes=[Gr[0]])
                        kb.op("dve", lambda e, G=G, pb2=pb2: e.tensor_tensor(
                            out=G[0:BS, 4, 0:BS], in0=pb2[0:BS, 0:BS], in1=mask[0:BS, 4, 0:BS], op=ALU.mult),
                            reads=[prb, rC], writes=[Gr[0]])
                        Pm, Pr = Pp.next()
                        kb.op("pool", lambda e, G=G, Pm=Pm: e.tensor_tensor(out=Pm[0:BS, 0:BS], in0=ident[0:BS, 0:BS], in1=G[0:BS, 0, 0:BS], op=ALU.subtract),
                              reads=[Gr[0], rC], writes=[Pr[0]])
                        S[h] = dict(G=G, Gr=Gr[0], Pm=Pm, Pr=Pr[0], X=G[0:BS, 0, 0:BS], Y=G[0:BS, 1, 0:BS], Xr=Gr[0], Yr=Gr[0])
                    if "gram" in dbg:
                        continue
                    for li, e_ in enumerate(es):
                        last = (li == len(es) - 1)
                        for h in heads:
                            s = S[h]
                            py, pyr = kb.ps()
                            kb.mm(py[0:BS, 0:BS], pyr, [(s["X"], s["Y"], [s["Xr"], s["Yr"]])])
                            Y2, Y2r = XYp.next()
                            kb.op("act", lambda e, py=py, Y2=Y2: e.copy(Y2[0:BS, 0:BS], py[0:BS, 0:BS]), reads=[pyr], writes=[Y2r[0]])
                            if not last:
                                px, pxr = kb.ps()
                                kb.mm(px[0:BS, 0:BS], pxr, [(s["Y"], s["X"], [s["Xr"], s["Yr"]])])
                                X2, X2r = XYp.next()
                                kb.op("dve", lambda e, px=px, X2=X2: e.tensor_copy(out=X2[0:BS, 0:BS], in_=px[0:BS, 0:BS]), reads=[pxr], writes=[X2r[0]])
                                s["X"], s["Xr"] = X2[0:BS, 0:BS], X2r[0]
                            s["Y"], s["Yr"] = Y2[0:BS, 0:BS], Y2r[0]
                        for h in heads:
                            s = S[h]
                            pp_, ppr = kb.ps()
                            kb.mm(pp_[0:BS, 0:BS], ppr, [(s["Y"], s["Pm"][0:BS, 0:BS], [s["Yr"], s["Pr"]])])
                            kb.op("dve", lambda e, s=s, pp_=pp_: e.tensor_tensor(out=s["Pm"][0:BS, 0:BS], in0=s["Pm"][0:BS, 0:BS], in1=pp_[0:BS, 0:BS], op=ALU.add),
                                  reads=[ppr, s["Pr"]], writes=[s["Pr"]])
                    if "inv" in dbg:
                        continue
                    po = {}
                    for hi, h in enumerate(heads):
                        bt, br = kb.psum[6 + hi // 4]
                        po[h] = (bt[:, (hi % 4) * 128:(hi % 4 + 1) * 128], br)
                        Wt, Wr_ = WUp.next()
                        Ut, Ur_ = WUp.next()
                        S[h].update(W=Wt, Wr=Wr_[0], U=Ut, Ur=Ur_[0])
                    for ch in range(nch):
                        rows = slice(ch * CL, (ch + 1) * CL)
                        for h in heads:
                            s = S[h]
                            c, hf = h // 2, h % 2
                            ps_ = slice(hf * 64, hf * 64 + 64)
                            gam = F["eg"][0][ps_, c, ch * CL + CL - 1:ch * CL + CL]
                            hg, hgr = hgp.next()
                            kb.op("pool", lambda e, hg=hg, gam=gam, c=c, ps_=ps_: e.tensor_scalar(out=hg[ps_, :], in0=H[ps_, c, :], scalar1=gam, scalar2=None, op0=ALU.mult),
                                  reads=[Hres[q][h], F["eg"][1]], writes=[hgr[0]])
                            s["hg"], s["hgr"], s["gam"] = hg, hgr[0], gam
                            pw, pwr = kb.ps()
                            kb.mm(pw[0:BS, 0:64], pwr, [(FM[("kkt", hf)][0][:, c, 0:BS], H[:, c, :], [FM[("kkt", hf)][1], Hres[q][h]]),
                                                          (s["G"][0:BS, 2, 0:BS], vTk[0:BS, h, :], [s["Gr"], vTr[0]])])
                            kb.op("act", lambda e, pw=pw, s=s: e.copy(s["W"][rows, :], pw[rows, 0:64]), reads=[pwr], writes=[s["Wr"]])
                        for h in heads:
                            s = S[h]
                            pu, pur = kb.ps()
                            kb.mm(pu[0:BS, 0:64], pur, [(s["Pm"][0:BS, 0:BS], s["W"][0:BS, :], [s["Pr"], s["Wr"]])])
                            kb.op("dve", lambda e, pu=pu, s=s: e.tensor_copy(out=s["U"][rows, :], in_=pu[rows, 0:64]), reads=[pur], writes=[s["Ur"]])
                        for h in heads:
                            s = S[h]
                            c, hf = h // 2, h % 2
                            ps_ = slice(hf * 64, hf * 64 + 64)
                            pt, pr = po[h]
                            kb.mm(pt[0:64, ch * CL:(ch + 1) * CL], pr,
                                  [(H[:, c, :], FM[("rt", hf)][0][:, c, rows], [Hres[q][h], FM[("rt", hf)][1]]),
                                   (vTk[0:BS, h, :], s["G"][0:BS, 3, rows], [vTr[0], s["Gr"]]),
                                   (s["U"][0:BS, :], s["G"][0:BS, 4, rows], [s["Ur"], s["Gr"]])])
                            ph, phr = kb.ps()
                            kT, kTr = TK[("kTk", ch)]
                            nT, nTr = TK[("naTk", ch)]
                            kb.mm(ph[:, 0:64], phr, [(kT[0:BS, c, :], vTk[0:BS, h, :], [kTr[0], vTr[0]]),
                                                      (nT[0:BS, c, :], s["U"][0:BS, :], [nTr[0], s["Ur"]])])
                            kb.op("dve", lambda e, ph=ph, s=s, c=c, ps_=ps_: e.scalar_tensor_tensor(
                                out=H[ps_, c, :], in0=ph[ps_, 0:64], scalar=s["gam"], in1=s["hg"][ps_, :], op0=ALU.mult, op1=ALU.add),
                                reads=[phr, s["hgr"], F["eg"][1]], writes=[Hres[q][h]])
                    for h in heads:
                        pt, pr = po[h]
                        kb.op("act", lambda e, pt=pt, h=h: e.copy(ost[:, h, 0:BS], pt[0:64, 0:BS]), reads=[pr], writes=[ostr[0]])
                if not dbg:
                    kb.dma("pool", self.wk_d["o64"][:, :, col0:col0 + BS], ost[:, :, 0:BS], reads=ostr, writes=[self.wk_res["o64"]])
            for q in range(NSEQ):
                kb.dma("pool", self.o_wkv[l, q], Hst[q][:].rearrange("p c i -> p (c i)"), reads=Hres[q])
            kb.psrot = list(range(8))

    def rwkv_out(self, l):
        cfg, kb = self.cfg, self.kb
        W = 256
        with self.scope():
            rW = Res("r3w")
            wo = kb.sb([HD, NH, D], BF16, "wo64")
            self.load_w(wo, rW, self.w_o64[l], 4)
            ln = kb.sb([HD, 2, NH], F32, "ln")
            kb.dma("sp", ln[:].rearrange("p a h -> p (a h)"), self.lnx64[:, l * 2 * NH:(l + 1) * 2 * NH], writes=[rW])
            ones = kb.sb([HD, HD], F32, "ones64")
            kb.op("pool", lambda e: e.memset(ones[:], 1.0), writes=[rW])
            geps = kb.sb([HD, 1], F32, "geps")
            kb.op("pool", lambda e: e.memset(geps[:], GN_EPS), writes=[rW])
            pools = {"rstd": Rot(kb, 2, [P, W], F32, "rstd"), "sq": Rot(kb, 1, [P, KC, W], BF16, "sq", nres=KC),
                     "tmp": Rot(kb, 3, [P, W], F32, "tmp")}
            xpool = Rot(kb, 2, [P, KC, W], F32, "xt", nres=KC)
            ypool = Rot(kb, 1, [P, KC, W], F32, "y", nres=KC)
            inp_ = {n: Rot(kb, 1, [HD, NH, W], F32, "i" + n) for n in ("o64", "g64", "bn64")}
            ogp = Rot(kb, 2, [HD, NH, W], BF16, "og")
            t64 = Rot(kb, 24, [HD, W], F32, "t64")
            for tile in self.tiles256:
                c0, w, segs, ti = tile
                I = {}
                for n in inp_:
                    t, r = inp_[n].next()
                    kb.dma("sp", t[:, :, 0:w], self.wk_d[n][:, :, c0:c0 + w], reads=[self.wk_res[n]], writes=r)
                    I[n] = (t, r[0])
                og, ogr = ogp.next()

                def head_gen(h, I=I, og=og, ogr=ogr, w=w):
                    o_ = I["o64"][0][:, h, 0:w]
                    osq, osqr = t64.next()
                    kb.op("act", lambda e: e.activation(osq[:, 0:w], o_, AF.Square), reads=[I["o64"][1]], writes=[osqr[0]])
                    yield
                    p1, r1 = kb.ps()
                    kb.mm(p1[0:64, 0:w], r1, [(ones[:], o_, [rW, I["o64"][1]])])
                    yield
                    p2, r2 = kb.ps()
                    kb.mm(p2[0:64, 0:w], r2, [(ones[:], osq[:, 0:w], [rW, osqr[0]])])
                    yield
                    nm, nmr = t64.next()
                    kb.op("act", lambda e: e.mul(nm[:, 0:w], p1[0:64, 0:w], -1.0 / HD), reads=[r1], writes=[nmr[0]])
                    yield
                    msq, msqr = t64.next()
                    kb.op("pool", lambda e: e.tensor_tensor(out=msq[:, 0:w], in0=nm[:, 0:w], in1=nm[:, 0:w], op=ALU.mult), reads=[nmr[0]], writes=[msqr[0]])
                    yield
                    var, varr = t64.next()
                    kb.op("dve", lambda e: e.scalar_tensor_tensor(out=var[:, 0:w], in0=p2[0:64, 0:w], scalar=1.0 / HD, in1=msq[:, 0:w], op0=ALU.mult, op1=ALU.subtract),
                          reads=[r2, msqr[0]], writes=[varr[0]])
                    yield
                    kb.op("act", lambda e: e.activation(var[:, 0:w], var[:, 0:w], AF.Sqrt, bias=geps[:, 0:1], scale=1.0), reads=[varr[0], rW], writes=[varr[0]])
                    yield
                    kb.op("dve", lambda e: e.reciprocal(var[:, 0:w], var[:, 0:w]), reads=[varr[0]], writes=[varr[0]])
                    yield
                    d, dr = t64.next()
                    kb.op("pool", lambda e: e.tensor_tensor(out=d[:, 0:w], in0=o_, in1=nm[:, 0:w], op=ALU.add), reads=[I["o64"][1], nmr[0]], writes=[dr[0]])
                    yield
                    kb.op("pool", lambda e: e.tensor_tensor(out=d[:, 0:w], in0=d[:, 0:w], in1=var[:, 0:w], op=ALU.mult), reads=[dr[0], varr[0]], writes=[dr[0]])
                    yield
                    kb.op("act", lambda e: e.activation(d[:, 0:w], d[:, 0:w], AF.Identity, bias=ln[:, 1, h:h + 1], scale=ln[:, 0, h:h + 1]),
                          reads=[dr[0], rW], writes=[dr[0]])
                    yield
                    kb.op("pool", lambda e: e.tensor_tensor(out=d[:, 0:w], in0=d[:, 0:w], in1=I["bn64"][0][:, h, 0:w], op=ALU.add),
                          reads=[dr[0], I["bn64"][1]], writes=[dr[0]])
                    yield
                    kb.op("dve", lambda e: e.tensor_tensor(out=og[:, h, 0:w], in0=d[:, 0:w], in1=I["g64"][0][:, h, 0:w], op=ALU.mult),
                          reads=[dr[0], I["g64"][1]], writes=[ogr[0]])
                    yield
                interleave([head_gen(h) for h in range(NH)], 4)
                xt, xr = self.load_x(xpool, tile)
                yt, yr = ypool.next()
                for oc in range(KC):
                    py, ry = kb.ps()
                    kb.mm(py[:, 0:w], ry, [(wo[:, h, oc * 128:(oc + 1) * 128], og[:, h, 0:w], [rW, ogr[0]]) for h in range(NH)])
                    kb.op("act", lambda e, oc=oc, py=py: e.copy(yt[:, oc, 0:w], py[:, 0:w]), reads=[ry], writes=[yr[oc]])
                self.post_residual(l, 1, tile, lambda c: yt[:, c, 0:w], lambda c: yr[c], xt, xr, pools)

    def rwkv_layer(self, l):
        self.rwkv_proj(l)
        self.rwkv_wkv(l)
        self.rwkv_out(l)


def _kmaj(w):
    Kd, N = w.shape
    return np.ascontiguousarray(w.reshape(Kd // P, P, N).transpose(1, 0, 2))


def _vec(v):
    sh = v.shape[:-1]
    return np.ascontiguousarray(np.moveaxis(v.reshape(sh + (KC, P)), -1, 0))


def _v64(v):
    sh = v.shape[:-1]
    return np.ascontiguousarray(np.moveaxis(v.reshape(sh + (NH, HD)), -1, 0))


def pack_shared(cfg, I):
    L, na = cfg.depth, cfg.n_a
    f = lambda a: np.ascontiguousarray(a, dtype=np.float32)
    out = {
        "w_mod": np.stack([_kmaj(f(I["w_mod"][l])) for l in range(L)]),
        "b_mod": f(I["b_mod"]).reshape(L, 72, P).transpose(2, 0, 1).reshape(P, -1).copy(),
        "g_norm": f(I["g_norm"]).reshape(L * 6, KC, P).transpose(2, 0, 1).reshape(P, -1).copy(),
        "w_in": np.stack([np.stack([_kmaj(f(I["w_ffn_in"][l, s])) for s in range(2)]) for l in range(L)]),
        "w_out": np.stack([np.stack([_kmaj(f(I["w_ffn_out"][l, s])) for s in range(2)]) for l in range(L)]),
        "mu": _vec(f(I["mu_a"])).reshape(P, -1).copy(),
        "rvec": _vec(np.stack([f(I["w0_a"]), f(I["a0_a"]), f(I["kk_a"]), f(I["ka_a"]),
                               f(I["rk_a"]).reshape(na, D)], axis=1)).reshape(P, -1).copy(),
        "v0_64": _v64(f(I["v0_a"])).reshape(HD, -1).copy(),
        "lnx64": _v64(np.stack([f(I["lnx_w_a"]), f(I["lnx_b_a"])], axis=1)).reshape(HD, -1).copy(),
        "w_rkv": np.stack([np.stack([_kmaj(f(I["w_rkv_a"][l, i])) for i in range(3)]) for l in range(na)]),
        "w_o64": np.stack([f(I["w_o_a"][l]).reshape(NH, HD, D).transpose(1, 0, 2).copy() for l in range(na)]),
        "w1": np.stack([_kmaj(f(I["w1_a"][l])) for l in range(na)]),
        "a1": np.stack([_kmaj(f(I["a1_a"][l])) for l in range(na)]),
        "v1": np.stack([_kmaj(f(I["v1_a"][l])) for l in range(na - 1)]),
        "g1": np.stack([_kmaj(f(I["g1_a"][l])) for l in range(na)]),
        "w2": f(I["w2_a"])[:, :, None, :].copy(),
        "a2": f(I["a2_a"])[:, :, None, :].copy(),
        "v2": f(I["v2_a"])[:, :, None, :].copy(),
        "g2": f(I["g2_a"])[:, :, None, :].copy(),
    }
    return out


def pack_core(cfg, I, core):
    na = cfg.n_a
    f = lambda a: np.ascontiguousarray(a, dtype=np.float32)
    b = core % I["x_prompt"].shape[0]
    sl = slice(core * NS, (core + 1) * NS)
    xall = np.zeros((cfg.ntok, D), np.float32)
    xall[:cfg.seq] = I["x_prompt"][b]
    for s in range(NS):
        xall[cfg.seq + s * SLOT: cfg.seq + s * SLOT + DEC] = I["x_sample"][core * NS + s]
    call = np.concatenate([f(I["c_prompt"][b:b + 1]), f(I["c_sample"][sl])], axis=0)
    out = {
        "xin": np.ascontiguousarray(xall.reshape(cfg.ntok, KC, P).transpose(2, 1, 0)),
        "cT": np.ascontiguousarray(call.reshape(NSEQ, KC, P).transpose(2, 1, 0)).reshape(P, -1),
        "sh0": _vec(f(I["state_shift"][:, sl])).reshape(P, -1).copy(),
        "hst0": np.ascontiguousarray(f(I["state_wkv"][:, sl]).reshape(na, NS, KC, 2, HD, HD).transpose(0, 1, 3, 5, 2, 4)).reshape(na, NS, P, KC * HD),
    }
    return out


def unpack_wkv(o_wkv):
    na = o_wkv.shape[0]
    a = o_wkv.reshape(na, NSEQ, 2, HD, KC, HD)
    return np.ascontiguousarray(a.transpose(0, 1, 4, 2, 5, 3)).reshape(na, NSEQ, NH, HD, HD)


def unpack_vec(o):
    return np.ascontiguousarray(np.moveaxis(o, 0, -1)).reshape(o.shape[1], D)


def _fox_setup(self):
    cfg, kb = self.cfg, self.kb
    NT = cfg.ntok
    nb = cfg.depth - cfg.n_a
    self.g_kv = self.inp("g_kv", [P, KC])
    self.w_kv = self.inp("w_kv", [P, KC, 2 * D])
    self.w_f = self.inp("w_f", [P, KC, NH])
    self.b_f = self.inp("b_f", [P, NH])
    self.w_q = self.inp("w_q", [nb, P, KC, D])
    self.w_o = self.inp("w_o", [nb, P, KC, D])
    self.cache_k = self.inp("cache_k", [cfg.npool * PAGE, D])
    self.cache_v = self.inp("cache_v", [cfg.npool * PAGE, D])
    self.cache_lf = self.inp("cache_lf", [cfg.npool * PAGE, NH])
    self.ptab = self.inp("ptab", [1, NS * cfg.npages], I32)
    self.o_k = self.outp("o_k", [P, KC, NT])
    self.o_v = self.outp("o_v", [NT, D])
    self.o_lf = self.outp("o_lf", [NT, NH])
    self.kbf = kb.dram("kbf", [P, KC, NT], BF16)
    self.vaug = kb.dram("vaug", [NT, NH * P], BF16)
    self.ckd = kb.dram("ckd", [NT, NH], F32)
    self.bnp = kb.dram("bnp", [NS, P, cfg.npages * NH], F32)
    self.r_kv = Res("kv")
    self.pidx = kb.sb([P, NS * cfg.npages], I32, "pidx")
    self.r_pidx = Res("pidx")
    with self.scope():
        iot = kb.sb([P, 1], I32, "iota")
        iof = kb.sb([P, 1], F32, "iotaf")
        idf = kb.sb([P, NS * cfg.npages], F32, "idf")
        kb.dma("sp", self.pidx[:], self.ptab[0:1, :].partition_broadcast(P), writes=[self.r_pidx])
        kb.op("pool", lambda e: e.iota(iot[:], pattern=[[0, 1]], base=0, channel_multiplier=1), writes=[self.r_pidx])
        kb.op("dve", lambda e: e.tensor_copy(out=iof[:], in_=iot[:]), reads=[self.r_pidx], writes=[self.r_pidx])
        kb.op("dve", lambda e: e.tensor_copy(out=idf[:], in_=self.pidx[:]), reads=[self.r_pidx], writes=[self.r_pidx])
        kb.op("dve", lambda e: e.tensor_scalar(out=idf[:], in0=idf[:], scalar1=float(PAGE), scalar2=iof[:, 0:1], op0=ALU.mult, op1=ALU.add),
              reads=[self.r_pidx], writes=[self.r_pidx])
        kb.op("dve", lambda e: e.tensor_copy(out=self.pidx[:], in_=idf[:]), reads=[self.r_pidx], writes=[self.r_pidx])


def _tri_consts(self, rC):
    kb = self.kb
    ones = kb.sb([P, P], F32, "ones32")
    ltri = kb.sb([P, P], F32, "ltri")
    e127 = kb.sb([P, P], F32, "e127")
    ltri32 = kb.sb([P, P], F32, "ltri32")
    kb.op("pool", lambda e: e.memset(ones[:], 1.0), writes=[rC])
    kb.op("pool", lambda e: e.affine_select(out=ltri[:], in_=ones[:], pattern=[[1, P]], compare_op=ALU.is_ge, fill=0.0, base=0, channel_multiplier=-1),
          reads=[rC], writes=[rC])
    kb.op("pool", lambda e: e.affine_select(out=e127[:], in_=ones[:], pattern=[[0, P]], compare_op=ALU.is_equal, fill=0.0, base=-127, channel_multiplier=1),
          reads=[rC], writes=[rC])
    kb.op("pool", lambda e: e.tensor_copy(out=ltri32[:], in_=ltri[:]), reads=[rC], writes=[rC])
    for a in range(4):
        for b in range(4):
            if a != b:
                kb.op("pool", lambda e, a=a, b=b: e.memset(ltri32[a * 32:(a + 1) * 32, b * 32:(b + 1) * 32], 0.0), reads=[rC], writes=[rC])
    return ones, ltri, e127, ltri32


def _shared_kv(self):
    cfg, kb = self.cfg, self.kb
    with self.scope():
        rW = Res("kvw")
        wkv = kb.sb([P, KC, 2 * D], BF16, "wkv")
        self.load_w(wkv, rW, self.w_kv, 4)
        wf = kb.sb([P, KC, NH], BF16, "wf")
        self.load_w(wf, rW, self.w_f, 1)
        gk = kb.sb([P, KC], F32, "gk")
        bf = kb.sb([P, NH], F32, "bf")
        kb.dma("sp", gk[:], self.g_kv[:, :], writes=[rW])
        kb.dma("sp", bf[:], self.b_f[:, :], writes=[rW])
        ones, ltri, e127, ltri32 = _tri_consts(self, rW)
        pools = self.std_pools()
        xpool = Rot(kb, 2, [P, KC, 512], F32, "xt", nres=KC)
        spool = Rot(kb, 2, [P, KC, 512], BF16, "st", nres=KC)
        kfp = Rot(kb, 2, [P, 512], F32, "kf")
        kbp = Rot(kb, 2, [P, 512], BF16, "kb")
        vfp = Rot(kb, 2, [P, D], F32, "vf")
        vap = Rot(kb, 2, [P, NH, P], BF16, "va")
        for t, r in vap.tiles:
            kb.op("pool", lambda e, t=t: e.memset(t[:], 0.0), writes=r)
            v4 = t[:].rearrange("p (a two) c -> p a two c", two=2)
            kb.op("pool", lambda e, v4=v4: e.memset(v4[:, :, 0, 64:65], 1.0), reads=r, writes=r)
            kb.op("pool", lambda e, v4=v4: e.memset(v4[:, :, 1, 0:1], 1.0), reads=r, writes=r)
        lfp = Rot(kb, 3, [P, NH], F32, "lf")
        ckp = Rot(kb, 3, [P, NH], F32, "ck")
        ck_prev = None
        for tile in cfg.tiles:
            c0, w, segs, ti = tile
            is_samp = c0 >= cfg.seq
            xt, xr = self.load_x(xpool, tile)
            st, sr = spool.next()
            self.modulate(0, 0, tile, xt, xr, lambda c, s0, sw: st[:, c, s0:s0 + sw], sr, pools, ident=lambda c: gk[:, c:c + 1])
            dbg = cfg.debug or ""
            if "kcut1" in dbg:
                continue
            for oc in range(KC):
                pk, rk = kb.ps()
                kb.mm(pk[:, 0:w], rk, [(wkv[:, kc, oc * 128:(oc + 1) * 128], st[:, kc, 0:w], [rW, sr[kc]]) for kc in range(KC)])
                kf, kfr = kfp.next()
                kb.op("act", lambda e, pk=pk, kf=kf: e.copy(kf[:, 0:w], pk[:, 0:w]), reads=[rk], writes=[kfr[0]])
                kbt, kbr = kbp.next()
                kb.op("dve", lambda e, kf=kf, kbt=kbt: e.tensor_copy(out=kbt[:, 0:w], in_=kf[:, 0:w]), reads=[kfr[0]], writes=[kbr[0]])
                kb.dma("pool", self.o_k[:, oc, c0:c0 + w], kf[:, 0:w], reads=[kfr[0]])
                kb.dma("pool", self.kbf[:, oc, c0:c0 + w], kbt[:, 0:w], reads=[kbr[0]], writes=[self.r_kv])
            if "kcut2" in dbg:
                continue
            for tb in range(w // 128):
                t0 = c0 + tb * 128
                ts_ = slice(tb * 128, (tb + 1) * 128)
                vf, vfr = vfp.next()
                va, var_ = vap.next()
                va4 = va[:].rearrange("p (a two) c -> p a two c", two=2)
                for half in range(2):
                    pv, rv_ = kb.ps()
                    kb.mm(pv[:, :], rv_, [(st[:, kc, ts_], wkv[:, kc, D + half * 512:D + (half + 1) * 512], [rW, sr[kc]]) for kc in range(KC)])
                    kb.op("act", lambda e, pv=pv, half=half: e.copy(vf[:, half * 512:(half + 1) * 512], pv[:, :]), reads=[rv_], writes=[vfr[0]])
                    pv4 = vf[:, half * 512:(half + 1) * 512].rearrange("p (a two c) -> p a two c", two=2, c=64)
                    kb.op("dve", lambda e, pv4=pv4, half=half: e.tensor_copy(out=va4[:, half * 4:half * 4 + 4, 0, 0:64], in_=pv4[:, :, 0, :]),
                          reads=[vfr[0]], writes=[var_[0]])
                    kb.op("pool", lambda e, pv4=pv4, half=half: e.tensor_copy(out=va4[:, half * 4:half * 4 + 4, 1, 64:128], in_=pv4[:, :, 1, :]),
                          reads=[vfr[0]], writes=[var_[0]])
                kb.dma("pool", self.o_v[t0:t0 + 128, :], vf[:], reads=[vfr[0]])
                kb.dma("pool", self.vaug[t0:t0 + 128, :], va[:].rearrange("p a c -> p (a c)"), reads=[var_[0]], writes=[self.r_kv])
                if "kcut3" in dbg:
                    continue
                pl, rl = kb.ps()
                kb.mm(pl[:, 0:NH], rl, [(st[:, kc, ts_], wf[:, kc, :], [rW, sr[kc]]) for kc in range(KC)])
                lf, lfr = lfp.next()
                kb.op("dve", lambda e, pl=pl, lf=lf: e.tensor_tensor(out=lf[:], in0=pl[:, 0:NH], in1=bf[:], op=ALU.add), reads=[rl, rW], writes=[lfr[0]])
                kb.op("act", lambda e, lf=lf: e.activation(lf[:], lf[:], AF.Exp, scale=-1.0), reads=[lfr[0]], writes=[lfr[0]])
                kb.op("act", lambda e, lf=lf: e.activation(lf[:], lf[:], AF.Ln, bias=1.0, scale=1.0), reads=[lfr[0]], writes=[lfr[0]])
                kb.op("dve", lambda e, lf=lf: e.tensor_scalar(out=lf[:], in0=lf[:], scalar1=-1.0, scalar2=None, op0=ALU.mult), reads=[lfr[0]], writes=[lfr[0]])
                kb.dma("pool", self.o_lf[t0:t0 + 128, :], lf[:], reads=[lfr[0]])
                if "kcut4" in dbg:
                    continue
                pc, rc = kb.ps()
                ck, ckr = ckp.next()
                if not is_samp:
                    items = [(ltri[:], lf[:], [rW, lfr[0]])]
                    if ck_prev is not None:
                        items.append((e127[:], ck_prev[0][:], [rW, ck_prev[1]]))
                    kb.mm(pc[:, 0:NH], rc, items)
                    kb.op("act", lambda e, pc=pc, ck=ck: e.copy(ck[:], pc[:, 0:NH]), reads=[rc], writes=[ckr[0]])
                    ck_prev = (ck, ckr[0])
                else:
                    kb.mm(pc[:, 0:NH], rc, [(ltri32[:], lf[:], [rW, lfr[0]])])
                    kb.op("act", lambda e, pc=pc, ck=ck: e.mul(ck[:], pc[:, 0:NH], -1.0), reads=[rc], writes=[ckr[0]])
                kb.dma("pool", self.ckd[t0:t0 + 128, :], ck[:], reads=[ckr[0]], writes=[self.r_kv])
        if "nopast" in (cfg.debug or ""):
            return
        npg = cfg.npages
        lpp = Rot(kb, 1, [P, npg, NH], F32, "lpast")
        cpp = Rot(kb, 2, [P, npg, NH], F32, "cpast")
        for s in range(NS):
            lp, lpr = lpp.next()
            for pg in range(npg):
                col = s * npg + pg
                kb.dma("pool", reads=[self.r_pidx], writes=[lpr[0]], fn=lambda e, pg=pg, col=col: e.indirect_dma_start(
                    out=lp[:, pg, :], out_offset=None, in_=self.cache_lf[:, :],
                    in_offset=bass.IndirectOffsetOnAxis(ap=self.pidx[:, col:col + 1], axis=0)))
            cp, cpr = cpp.next()
            for pg in range(npg):
                pc, rc = kb.ps()
                items = [(ltri[:], lp[:, pg, :], [rW, lpr[0]])]
                if pg > 0:
                    items.append((e127[:], cp[:, pg - 1, :], [rW, cpr[0]]))
                kb.mm(pc[:, 0:NH], rc, items)
                kb.op("act", lambda e, pc=pc, pg=pg: e.copy(cp[:, pg, :], pc[:, 0:NH]), reads=[rc], writes=[cpr[0]])
            pt_, rt_ = kb.ps()
            kb.mm(pt_[:, 0:NH], rt_, [(e127[:], cp[:, npg - 1, :], [rW, cpr[0]])])
            tot, totr = ckp.next()
            kb.op("act", lambda e, pt_=pt_, tot=tot: e.copy(tot[:], pt_[:, 0:NH]), reads=[rt_], writes=[totr[0]])
            kb.op("dve", lambda e, tot=tot: e.tensor_tensor(out=cp[:], in0=tot[:].unsqueeze(1).to_broadcast([P, npg, NH]), in1=cp[:], op=ALU.subtract),
                  reads=[totr[0], cpr[0]], writes=[cpr[0]])
            kb.dma("pool", self.bnp[s], cp[:].rearrange("p a h -> p (a h)"), reads=[cpr[0]], writes=[self.r_kv])


Prog.fox_setup = _fox_setup
Prog.shared_kv = _shared_kv


def _fox_layer(self, j):
    cfg, kb = self.cfg, self.kb
    l = cfg.n_a + j
    npg = cfg.npages
    with self.scope():
        rW = Res("aw")
        wq = kb.sb([P, KC, D], BF16, "wq")
        wo = kb.sb([P, KC, D], BF16, "wo")
        self.load_w(wq, rW, self.w_q[j], 2)
        self.load_w(wo, rW, self.w_o[j], 2)
        ones = kb.sb([P, P], F32, "ones32")
        onesb = kb.sb([P, 512], BF16, "onesb")
        sel = [kb.sb([P, P], F32, f"sel{i}") for i in range(2)]
        identb = kb.sb([P, P], BF16, "identb")
        kb.op("pool", lambda e: e.memset(ones[:], 1.0), writes=[rW])
        kb.op("pool", lambda e: e.memset(onesb[:], 1.0), writes=[rW])
        kb.op("pool", lambda e: e.affine_select(out=sel[0][:], in_=ones[:], pattern=[[0, P]], compare_op=ALU.is_equal, fill=0.0, base=-64, channel_multiplier=1),
              reads=[rW], writes=[rW])
        kb.op("pool", lambda e: e.affine_select(out=sel[1][:], in_=ones[:], pattern=[[0, P]], compare_op=ALU.is_equal, fill=0.0, base=0, channel_multiplier=1),
              reads=[rW], writes=[rW])
        kb.op("pool", lambda e: e.affine_select(out=identb[:], in_=onesb[:, 0:P], pattern=[[1, P]], compare_op=ALU.is_equal, fill=0.0, base=0, channel_multiplier=-1),
              reads=[rW], writes=[rW])
        dmask = kb.sb([P, 4, 512], BF16, "dmask")
        for dd in range(4):
            kb.op("pool", lambda e, dd=dd: e.affine_select(out=dmask[:, dd, :], in_=onesb[:], pattern=[[1, 512]], compare_op=ALU.is_ge, fill=0.0,
                                                           base=-dd * 128, channel_multiplier=-1), reads=[rW], writes=[rW])
        mask8 = kb.sb([DEC, DEC], BF16, "mask8")
        kb.op("pool", lambda e: e.affine_select(out=mask8[:], in_=onesb[0:DEC, 0:DEC], pattern=[[1, DEC]], compare_op=ALU.is_ge, fill=0.0,
                                                base=0, channel_multiplier=-1), reads=[rW], writes=[rW])
        pools = self.std_pools()
        xpool = Rot(kb, 1, [P, KC, 512], F32, "xt", nres=KC)
        hpool = Rot(kb, 1, [P, KC, 512], BF16, "ht", nres=KC)
        ypool = Rot(kb, 1, [P, KC, 512], F32, "y", nres=KC)
        qtm = kb.sb([P, NH, 512], BF16, "qtm")
        rq = Res("qtm")
        kb.op("pool", lambda e: e.memset(qtm[:], 0.0), writes=[rq])
        attnT = kb.sb([P, KC, 512], BF16, "attnT")
        ra = Res("attnT")
        kb.op("pool", lambda e: e.memset(attnT[:], 0.0), writes=[ra])
        rzp = [Rot(kb, 2, [P, 512], F32, f"rz{i}") for i in range(2)]
        for rp in rzp:
            for t, r in rp.tiles:
                kb.op("pool", lambda e, t=t: e.memset(t[:], 0.0), writes=r)
        osbp = Rot(kb, 2, [P, 512], F32, "osb")
        ktp = Rot(kb, 2, [P, 2, 512], BF16, "kt")
        vap = Rot(kb, 2, [P, 4, 512], BF16, "va")
        ckbp = Rot(kb, 2, [P, 4, NH], F32, "ckb")
        bnp_ = Rot(kb, 2, [P, 4, NH], F32, "bn")
        crp = Rot(kb, 2, [P, NH], F32, "cref")
        ptp = Rot(kb, 4, [P, 512], BF16, "pt")
        kb.psrot = [0, 1, 2, 3]

        def qproj(ht, hr, w):
            for oc in range(KC):
                pq, rq_ = kb.ps()
                kb.mm(pq[:, 0:w], rq_, [(wq[:, kc, oc * 128:(oc + 1) * 128], ht[:, kc, 0:w], [rW, hr[kc]]) for kc in range(KC)])
                kb.op("act", lambda e, pq=pq, oc=oc: e.mul(qtm[0:64, 2 * oc, 0:w], pq[0:64, 0:w], HD ** -0.5), reads=[rq_], writes=[rq])
                kb.op("act", lambda e, pq=pq, oc=oc: e.mul(qtm[64:128, 2 * oc + 1, 0:w], pq[64:128, 0:w], HD ** -0.5), reads=[rq_], writes=[rq])

        def epilogue(h, acc_ap, acc_res, w, out_cols):
            par = h % 2
            lr = 64 if par == 0 else 0
            rows = slice(0, 64) if par == 0 else slice(64, 128)
            rz, rzr = rzp[par].next()
            osb, osr = osbp.next()
            kb.op("act", lambda e: e.copy(osb[:, 0:w], acc_ap[:, 0:w]), reads=[acc_res], writes=[osr[0]])
            kb.op("dve", lambda e: e.reciprocal(rz[lr:lr + 1, 0:w], osb[lr:lr + 1, 0:w]), reads=[osr[0]], writes=[rzr[0]])
            pb_, pbr = kb.ps()
            kb.mm(pb_[:, 0:w], pbr, [(sel[par][:], rz[:, 0:w], [rW, rzr[0]])])
            kb.op("dve", lambda e: e.tensor_tensor(out=attnT[rows, h // 2, out_cols], in0=osb[rows, 0:w], in1=pb_[rows, 0:w], op=ALU.mult),
                  reads=[osr[0], pbr], writes=[ra])

        def out_proj(tile, xt, xr):
            c0, w, segs, ti = tile
            yt, yr = ypool.next()
            for oc in range(KC):
                py, ry = kb.ps()
                kb.mm(py[:, 0:w], ry, [(wo[:, kc, oc * 128:(oc + 1) * 128], attnT[:, kc, 0:w], [rW, ra]) for kc in range(KC)])
                kb.op("act", lambda e, oc=oc, py=py: e.copy(yt[:, oc, 0:w], py[:, 0:w]), reads=[ry], writes=[yr[oc]])
            self.post_residual(l, 1, tile, lambda c: yt[:, c, 0:w], lambda c: yr[c], xt, xr, pools)

        for qi, tile in enumerate(cfg.tiles[:-1]):
            c0, w, segs, ti = tile
            xt, xr = self.load_x(xpool, tile)
            ht, hr = hpool.next()
            self.modulate(l, 1, tile, xt, xr, lambda c, s0, sw: ht[:, c, s0:s0 + sw], hr, pools)
            qproj(ht, hr, w)
            cref, crr = crp.next()
            kb.dma("sp", cref[:], self.ckd[c0:c0 + 1, :].partition_broadcast(P), reads=[self.r_kv], writes=[crr[0]])
            nkb = 4 * (qi + 1)
            for hg in range(4):
                accs = [(kb.psum[4 + hi][0], kb.psum[4 + hi][1]) for hi in range(4)]
                for kg in range(qi + 1):
                    kt, ktr = ktp.next()
                    kb.dma("sp", kt[:], self.kbf[:, 2 * hg:2 * hg + 2, kg * 512:(kg + 1) * 512], reads=[self.r_kv], writes=[ktr[0]])
                    va, var_ = vap.next()
                    kb.dma("sp", va[:], self.vaug[kg * 512:(kg + 1) * 512, hg * 512:(hg + 1) * 512].rearrange("(b p) c -> p b c", p=P),
                           reads=[self.r_kv], writes=[var_[0]])
                    ckb, ckr = ckbp.next()
                    kb.dma("sp", ckb[:], self.ckd[kg * 512:(kg + 1) * 512, :].rearrange("(b p) h -> p b h", p=P), reads=[self.r_kv], writes=[ckr[0]])
                    bn, bnr = bnp_.next()
                    kb.op("dve", lambda e: e.tensor_tensor(out=bn[:], in0=cref[:].unsqueeze(1).to_broadcast([P, 4, NH]), in1=ckb[:], op=ALU.subtract),
                          reads=[crr[0], ckr[0]], writes=[bnr[0]])
                    for b in range(4):
                        kbi = kg * 4 + b
                        for hi in range(4):
                            h = 4 * hg + hi
                            ps_, pr_ = kb.ps()
                            kb.mm(ps_[:, 0:w], pr_, [(kt[:, hi // 2, b * 128:(b + 1) * 128], qtm[:, h, 0:w], [ktr[0], rq])])
                            pt, ptr = ptp.next()
                            kb.op("act", lambda e: e.activation(pt[:, 0:w], ps_[:, 0:w], AF.Exp, bias=bn[:, b, h:h + 1], scale=1.0),
                                  reads=[pr_, bnr[0]], writes=[ptr[0]])
                            if kbi >= 4 * qi:
                                dd = kbi - 4 * qi
                                kb.op("pool", lambda e: e.tensor_tensor(out=pt[:, 0:w], in0=pt[:, 0:w], in1=dmask[:, dd, 0:w], op=ALU.mult),
                                      reads=[ptr[0], rW], writes=[ptr[0]])
                            kb.mm(accs[hi][0][:, 0:w], accs[hi][1], [(va[:, b, hi * 128:(hi + 1) * 128], pt[:, 0:w], [var_[0], ptr[0]])],
                                  start=(kbi == 0), stop=(kbi == nkb - 1))
                for hi in range(4):
                    epilogue(4 * hg + hi, accs[hi][0], accs[hi][1], w, slice(0, w))
            out_proj(tile, xt, xr)

        tile = cfg.tiles[-1]
        c0, w, segs, ti = tile
        xt, xr = self.load_x(xpool, tile)
        ht, hr = hpool.next()
        self.modulate(l, 1, tile, xt, xr, lambda c, s0, sw: ht[:, c, s0:s0 + sw], hr, pools)
        qproj(ht, hr, w)
        kpp = Rot(kb, 2, [P, D], F32, "kpage")
        vpp = Rot(kb, 2, [P, D], F32, "vpage")
        k16p = Rot(kb, 2, [P, D], BF16, "k16")
        kTp = Rot(kb, 2, [P, KC, P], BF16, "kpT")
        vAp = Rot(kb, 2, [P, NH, P], BF16, "vaP")
        for t, r in vAp.tiles:
            kb.op("pool", lambda e, t=t: e.memset(t[:], 0.0), writes=r)
            v4 = t[:].rearrange("p (a two) c -> p a two c", two=2)
            kb.op("pool", lambda e, v4=v4: e.memset(v4[:, :, 0, 64:65], 1.0), reads=r, writes=r)
            kb.op("pool", lambda e, v4=v4: e.memset(v4[:, :, 1, 0:1], 1.0), reads=r, writes=r)
        bpp = Rot(kb, 1, [P, npg, NH], F32, "bpast")
        bnew = Rot(kb, 2, [DEC, NH], F32, "bnew")
        sbp = Rot(kb, 2, [P, NH, DEC], F32, "sb")
        pTp = Rot(kb, 2, [P, NH, DEC], BF16, "pT")
        acc_t, acc_r = kb.psum[4]
        for s in range(NS):
            qc = slice(s * SLOT, s * SLOT + DEC)
            bp, bpr = bpp.next()
            kb.dma("sp", bp[:].rearrange("p a h -> p (a h)"), self.bnp[s], reads=[self.r_kv], writes=[bpr[0]])
            for pg in range(npg + 1):
                new = (pg == npg)
                kT, kTr = kTp.next()
                vA, vAr = vAp.next()
                if not new:
                    col = s * npg + pg
                    kp, kpr = kpp.next()
                    vp, vpr = vpp.next()
                    kb.dma("pool", reads=[self.r_pidx], writes=[kpr[0]], fn=lambda e: e.indirect_dma_start(
                        out=kp[:], out_offset=None, in_=self.cache_k[:, :], in_offset=bass.IndirectOffsetOnAxis(ap=self.pidx[:, col:col + 1], axis=0)))
                    kb.dma("pool", reads=[self.r_pidx], writes=[vpr[0]], fn=lambda e: e.indirect_dma_start(
                        out=vp[:], out_offset=None, in_=self.cache_v[:, :], in_offset=bass.IndirectOffsetOnAxis(ap=self.pidx[:, col:col + 1], axis=0)))
                    k16, k16r = k16p.next()
                    kb.op("pool", lambda e: e.tensor_copy(out=k16[:], in_=kp[:]), reads=[kpr[0]], writes=[k16r[0]])
                    ptb, ptbr = kb.ps()
                    ptb16 = ptb[:, :].bitcast(BF16)
                    for c in range(KC):
                        kb.mm(ptb16[:, c * 128:(c + 1) * 128], ptbr, [(k16[:, c * 128:(c + 1) * 128], identb[:], [k16r[0], rW])], transpose=True)
                    kb.op("act", lambda e: e.copy(kT[:].rearrange("p c k -> p (c k)"), ptb16[:, :]), reads=[ptbr], writes=[kTr[0]])
                    vA4 = vA[:].rearrange("p (a two) c -> p a two c", two=2)
                    vp4 = vp[:].rearrange("p (a two c) -> p a two c", two=2, c=64)
                    kb.op("dve", lambda e: e.tensor_copy(out=vA4[:, :, 0, 0:64], in_=vp4[:, :, 0, :]), reads=[vpr[0]], writes=[vAr[0]])
                    kb.op("pool", lambda e: e.tensor_copy(out=vA4[:, :, 1, 64:128], in_=vp4[:, :, 1, :]), reads=[vpr[0]], writes=[vAr[0]])
                    nk = P
                    bias_ap = bp[:, pg, :]
                    bias_res = bpr[0]
                else:
                    t0 = c0 + s * SLOT
                    kb.dma("sp", kT[:, :, 0:DEC], self.kbf[:, :, t0:t0 + DEC], reads=[self.r_kv], writes=[kTr[0]])
                    kb.dma("sp", vA[0:DEC].rearrange("p a c -> p (a c)"), self.vaug[t0:t0 + DEC, :], reads=[self.r_kv], writes=[vAr[0]])
                    bnw, bnwr = bnew.next()
                    kb.dma("sp", bnw[:], self.ckd[t0:t0 + DEC, :], reads=[self.r_kv], writes=[bnwr[0]])
                    nk = DEC
                    bias_ap = bnw[:, :]
                    bias_res = bnwr[0]
                ps_, pr_ = kb.ps()
                for h in range(NH):
                    kb.mm(ps_[0:nk, h * DEC:(h + 1) * DEC], pr_, [(kT[:, h // 2, 0:nk], qtm[:, h, qc], [kTr[0], rq])])
                sb, sbr = sbp.next()
                kb.op("dve", lambda e: e.tensor_tensor(out=sb[0:nk], in0=ps_[0:nk, 0:NH * DEC].rearrange("p (h q) -> p h q", q=DEC),
                                                       in1=bias_ap.unsqueeze(2).to_broadcast([nk, NH, DEC]), op=ALU.add),
                      reads=[pr_, bias_res], writes=[sbr[0]])
                pT, pTr = pTp.next()
                kb.op("act", lambda e: e.activation(pT[0:nk].rearrange("p h q -> p (h q)"), sb[0:nk].rearrange("p h q -> p (h q)"), AF.Exp),
                      reads=[sbr[0]], writes=[pTr[0]])
                if new:
                    kb.op("pool", lambda e: e.tensor_tensor(out=pT[0:nk], in0=pT[0:nk], in1=mask8[:].unsqueeze(1).to_broadcast([DEC, NH, DEC]), op=ALU.mult),
                          reads=[pTr[0], rW], writes=[pTr[0]])
                for h in range(NH):
                    kb.mm(acc_t[:, h * DEC:(h + 1) * DEC], acc_r, [(vA[0:nk, h, :], pT[0:nk, h, :], [vAr[0], pTr[0]])],
                          start=(pg == 0), stop=(pg == npg))
            for h in range(NH):
                epilogue(h, acc_t[:, h * DEC:(h + 1) * DEC], acc_r, DEC, qc)
        out_proj(tile, xt, xr)
        kb.psrot = list(range(8))


def _final_out(self):
    kb = self.kb
    self.o_y = self.outp("o_y", [P, KC, self.cfg.ntok])
    for tile in self.cfg.tiles:
        c0, w, segs, ti = tile
        kb.dma("sp", self.o_y[:, :, c0:c0 + w], self.xs[:, :, c0:c0 + w], reads=[self.xs_res[ti]])


Prog.fox_layer = _fox_layer
Prog.final_out = _final_out


def build_program(cfg, stop=None):
    pg = Prog(cfg)
    pg.setup()
    pg.setup_rwkv()
    steps = []
    for l in range(cfg.depth):
        if l == cfg.n_a:
            steps.append(("fox_setup", pg.fox_setup))
            steps.append(("kv", pg.shared_kv))
        steps.append((f"ffn{l}a", lambda l=l: pg.ffn(l, 0)))
        if l < cfg.n_a:
            steps.append((f"rwkv{l}", lambda l=l: pg.rwkv_layer(l)))
        else:
            steps.append((f"fox{l}", lambda l=l: pg.fox_layer(l - cfg.n_a)))
        steps.append((f"ffn{l}b", lambda l=l: pg.ffn(l, 2)))
    for i, (name, fn) in enumerate(steps):
        if stop is not None and not isinstance(stop, int):
            if name not in stop:
                continue
        elif stop is not None and i >= stop:
            break
        fn()
    pg.final_out()
    pg.kb.finish()
    return pg


def pack_fox(cfg, I, core):
    f = lambda a: np.ascontiguousarray(a, dtype=np.float32)
    nb = cfg.depth - cfg.n_a
    shared = {
        "g_kv": _vec(f(I["g_kv"])[None])[:, 0, :].copy(),
        "w_kv": _kmaj(f(I["w_kv"])),
        "w_f": _kmaj(f(I["w_f"])),
        "b_f": np.ascontiguousarray(np.broadcast_to(f(I["b_f"])[None, :], (P, NH))),
        "w_q": np.stack([_kmaj(f(I["w_q_b"][j])) for j in range(nb)]),
        "w_o": np.stack([_kmaj(f(I["w_o_b"][j])) for j in range(nb)]),
        "cache_k": f(I["cache_k"]).reshape(-1, D),
        "cache_v": f(I["cache_v"]).reshape(-1, D),
        "cache_lf": f(I["cache_logf"]).reshape(-1, NH),
    }
    return shared


_STOP = None
_DEBUG = None


def kernel(**I):
    n_cores = 8
    seq = I["x_prompt"].shape[1]
    npages = I["page_table"].shape[1]
    npool = I["cache_k"].shape[0]
    depth = I["w_mod"].shape[0]
    cfg = Cfg(seq=seq, npages=npages, npool=npool, depth=depth, debug=_DEBUG)
    pg = build_program(cfg, stop=_STOP)
    shared = pack_shared(cfg, I)
    shared.update(pack_fox(cfg, I, 0))
    in_maps = []
    for core in range(n_cores):
        d = dict(shared)
        d.update(pack_core(cfg, I, core))
        d["ptab"] = np.ascontiguousarray(I["page_table"][core * NS:(core + 1) * NS].reshape(1, -1).astype(np.int32))
        in_maps.append({k: d[k] for k in pg.inputs})
    res = run_bass_kernel_spmd(pg.nc, in_maps, core_ids=list(range(n_cores)))
    R = res.results
    if _STOP is not None:
        for r in R:
            for k, shp in (("o_k", (P, KC, cfg.ntok)), ("o_v", (cfg.ntok, D)), ("o_lf", (cfg.ntok, NH))):
                r.setdefault(k, np.zeros(shp, np.float32))
    B = I["x_prompt"].shape[0]
    DB = I["x_sample"].shape[0]
    na = cfg.n_a

    def tokmaj(a):
        return np.ascontiguousarray(a.transpose(2, 1, 0)).reshape(a.shape[2], D)

    def samp_rows(a):
        return np.stack([a[seq + s * SLOT: seq + s * SLOT + DEC] for s in range(NS)])
    y_p = np.stack([tokmaj(R[b]["o_y"][:, :, :seq]) for b in range(B)])
    y_s = np.concatenate([samp_rows(tokmaj(R[c]["o_y"])) for c in range(n_cores)])[:DB]
    wk = [unpack_wkv(R[c]["o_wkv"]) for c in range(n_cores)]
    wkv_p = np.stack([wk[b][:, 0] for b in range(B)], axis=1)
    wkv_s = np.concatenate([wk[c][:, 1:] for c in range(n_cores)], axis=1)[:, :DB]
    sh = [np.stack([unpack_vec(R[c]["o_shift"].reshape(P, na, NSEQ, KC)[:, l]) for l in range(na)]) for c in range(n_cores)]
    sh_p = np.stack([sh[b][:, 0] for b in range(B)], axis=1)
    sh_s = np.concatenate([sh[c][:, 1:] for c in range(n_cores)], axis=1)[:, :DB]
    k_all = [tokmaj(R[c]["o_k"]) for c in range(n_cores)]
    k_p = np.stack([k_all[b][:seq] for b in range(B)]).reshape(B, seq, NH, HD)
    k_s = np.concatenate([samp_rows(k_all[c]) for c in range(n_cores)])[:DB].reshape(DB, DEC, NH, HD)
    v_p = np.stack([R[b]["o_v"][:seq] for b in range(B)]).reshape(B, seq, NH, HD)
    v_s = np.concatenate([samp_rows(R[c]["o_v"]) for c in range(n_cores)])[:DB].reshape(DB, DEC, NH, HD)
    lf_p = np.stack([R[b]["o_lf"][:seq] for b in range(B)])
    lf_s = np.concatenate([samp_rows(R[c]["o_lf"]) for c in range(n_cores)])[:DB]
    f32 = lambda a: np.ascontiguousarray(a, dtype=np.float32)
    return (f32(y_p), f32(y_s), f32(wkv_p), f32(sh_p), f32(k_p), f32(v_p), f32(lf_p),
            f32(wkv_s), f32(sh_s), f32(k_s), f32(v_s), f32(lf_s))
```

```python
import contextlib
import numpy as np
import concourse.bass as bass
import concourse.mybir as mybir
from concourse.bass_utils import run_bass_kernel_spmd

F32 = mybir.dt.float32
BF16 = mybir.dt.bfloat16
I32 = mybir.dt.int32
AF = mybir.ActivationFunctionType
ALU = mybir.AluOpType

P = 128
D = 1024
KC = 8
DFF = 2816
FC = 22
NH = 16
HD = 64
NSEQ = 5
NS = 4
SLOT = 32
DEC = 8
NORM_EPS = 1e-6
GN_EPS = 64e-5
PAGE = 128


class Res:
    __slots__ = ("w", "r", "name")

    def __init__(self, name=""):
        self.w = {}
        self.r = {}
        self.name = name


class KB:
    def __init__(self, nc, nring=20, same_sync=True):
        self.nc = nc
        self.es = contextlib.ExitStack()
        self.same = same_sync
        self.eng = {"pe": nc.tensor, "act": nc.scalar, "dve": nc.vector, "pool": nc.gpsimd, "sp": nc.sync}
        self.csem = {e: self.es.enter_context(nc.semaphore(f"c_{e}")) for e in ("pe", "act", "dve", "pool")}
        self.cnt = {e: 0 for e in self.csem}
        self.ring = {q: [self.es.enter_context(nc.semaphore(f"r_{q}{i}")) for i in range(nring)] for q in ("sp", "pool")}
        self.ruse = {q: [0] * nring for q in self.ring}
        self.rnext = {q: 0 for q in self.ring}
        self.seen = {e: {} for e in self.eng}
        self.nbuf = 0
        self.psum = []
        self.psn = 0
        self.psrot = list(range(8))

    def sb(self, shape, dt, name=None):
        self.nbuf += 1
        return self.es.enter_context(self.nc.sbuf_tensor(f"{name or 'sb'}_{self.nbuf}", list(shape), dt))

    def init_psum(self):
        for i in range(8):
            t = self.es.enter_context(self.nc.psum_tensor(f"ps{i}", [P, 512], F32))
            self.psum.append((t, Res(f"ps{i}")))

    def ps(self):
        t = self.psum[self.psrot[self.psn % len(self.psrot)]]
        self.psn += 1
        return t

    def dram(self, name, shape, dt):
        return self.nc.dram_tensor(name, list(shape), dt).ap()

    def _collect(self, e, reads, writes):
        waits = {}

        def add(d):
            for k, (sem, v) in d.items():
                if k == e and (e == "pe" or not self.same):
                    continue
                if k not in waits or waits[k][1] < v:
                    waits[k] = (sem, v)
        for r in reads:
            add(r.w)
        for w in writes:
            add(w.w)
            add(w.r)
        return waits

    def _wait(self, e, waits):
        sn = self.seen[e]
        for k, (sem, v) in waits.items():
            if sn.get(k, 0) >= v:
                continue
            self.eng[e].wait_ge(sem, v)
            sn[k] = v

    @staticmethod
    def _register(key, ev, reads, writes):
        for r in reads:
            r.r[key] = ev
        for w in writes:
            w.w = {key: ev}
            w.r = {}

    def op(self, e, fn, reads=(), writes=()):
        self._wait(e, self._collect(e, reads, writes))
        ins = fn(self.eng[e])
        self.cnt[e] += 1
        ins.then_inc(self.csem[e], 1)
        self._register(e, (self.csem[e], self.cnt[e]), reads, writes)

    def mm(self, out_ap, out_res, items, transpose=False, start=True, stop=True):
        n = len(items)
        allreads = []
        ins = None
        for i, (l, r, rd) in enumerate(items):
            self._wait("pe", self._collect("pe", rd, [out_res] if i == 0 else []))
            if transpose:
                ins = self.nc.tensor.transpose(out_ap, l, r)
            else:
                ins = self.nc.tensor.matmul(out_ap, lhsT=l, rhs=r, start=(start and i == 0), stop=(stop and i == n - 1))
            allreads.extend(rd)
        self.cnt["pe"] += 1
        ins.then_inc(self.csem["pe"], 1)
        self._register("pe", (self.csem["pe"], self.cnt["pe"]), allreads, [out_res])

    def dma(self, q, out_ap=None, in_ap=None, reads=(), writes=(), fn=None):
        waits = self._collect(q, reads, writes)
        ring = self.ring[q]
        i = self.rnext[q]
        self.rnext[q] = (i + 1) % len(ring)
        key = f"{q}{i}"
        if self.ruse[q][i] > 0:
            v = 16 * self.ruse[q][i]
            if key not in waits or waits[key][1] < v:
                waits[key] = (ring[i], v)
        self._wait(q, waits)
        if fn is None:
            ins = self.eng[q].dma_start(out=out_ap, in_=in_ap)
        else:
            ins = fn(self.eng[q])
        self.ruse[q][i] += 1
        ins.then_inc(ring[i], 16)
        self._register(key, (ring[i], 16 * self.ruse[q][i]), reads, writes)

    def finish(self):
        for q in self.ring:
            for i, sem in enumerate(self.ring[q]):
                if self.ruse[q][i] > 0:
                    self.nc.sync.wait_ge(sem, 16 * self.ruse[q][i])
        for e in self.csem:
            if self.cnt[e] > 0:
                self.nc.sync.wait_ge(self.csem[e], self.cnt[e])


class Rot:
    def __init__(self, kb, n, shape, dt, name, nres=1):
        self.tiles = [(kb.sb(shape, dt, name), [Res(f"{name}{i}_{j}") for j in range(nres)]) for i in range(n)]
        self.i = 0

    def next(self):
        t = self.tiles[self.i % len(self.tiles)]
        self.i += 1
        return t


def interleave(gens, width):
    gens = list(gens)
    active = []
    while gens or active:
        while gens and len(active) < width:
            active.append(gens.pop(0))
        for g in list(active):
            try:
                next(g)
            except StopIteration:
                active.remove(g)


class Cfg:
    def __init__(self, seq=8192, npages=64, npool=2560, debug=None, depth=4):
        self.seq = seq
        self.npages = npages
        self.npool = npool
        self.ntok = seq + NS * SLOT
        self.debug = debug
        self.depth = depth
        self.n_a = depth // 2
        self.tiles = []
        for t in range(seq // 512):
            self.tiles.append((t * 512, 512, [(0, 512, 0)], t))
        self.tiles.append((seq, NS * SLOT, [(s * SLOT, SLOT, 1 + s) for s in range(NS)], seq // 512))


class Prog:
    def __init__(self, cfg):
        self.cfg = cfg
        nc = self.nc = bass.Bass("TRN2", target_bir_lowering=False)
        self.kb = KB(nc)
        self.inputs = {}
        self.outputs = {}
        self.scope_stack = None

    def inp(self, name, shape, dt=F32):
        ap = self.nc.dram_tensor(name, list(shape), dt, kind="ExternalInput").ap()
        self.inputs[name] = ap
        return ap

    def outp(self, name, shape, dt=F32):
        ap = self.nc.dram_tensor(name, list(shape), dt, kind="ExternalOutput").ap()
        self.outputs[name] = ap
        return ap

    @contextlib.contextmanager
    def scope(self):
        kb = self.kb
        outer = kb.es
        inner = contextlib.ExitStack()
        kb.es = inner
        try:
            yield
            self.barrier()
        finally:
            kb.es = outer
            inner.close()

    def barrier(self):
        kb = self.kb
        evs = {}
        for e in kb.csem:
            if kb.cnt[e] > 0:
                evs[e] = (kb.csem[e], kb.cnt[e])
        for q in kb.ring:
            for i, sem in enumerate(kb.ring[q]):
                if kb.ruse[q][i] > 0:
                    evs[f"{q}{i}"] = (sem, 16 * kb.ruse[q][i])
        for e in kb.eng:
            for k, (sem, v) in evs.items():
                if kb.seen[e].get(k, 0) >= v:
                    continue
                kb.eng[e].wait_ge(sem, v)
                kb.seen[e][k] = v

    def load_w(self, dst_tile, dst_res, src_ap, nsplit):
        X = src_ap.shape[1]
        step = (X + nsplit - 1) // nsplit
        for x0 in range(0, X, step):
            x1 = min(X, x0 + step)
            self.kb.dma("pool", dst_tile[:, x0:x1, :], src_ap[:, x0:x1, :], writes=[dst_res])

    def consts(self):
        kb = self.kb
        self.ones_bf = kb.sb([P, P], BF16, "ones")
        self.r_const = Res("const")
        kb.op("dve", lambda e: e.memset(self.ones_bf[:], 1.0), writes=[self.r_const])
        self.eps_t = kb.sb([P, 1], F32, "eps")
        kb.op("dve", lambda e: e.memset(self.eps_t[:], NORM_EPS), writes=[self.r_const])

    def sumsq_rstd(self, src_chunks, src_res, w, sq_pool, rstd_tile, rstd_res, nchunk=KC, eps_ap=None, scale=1.0 / D):
        kb = self.kb
        sq, sqres = sq_pool.next()
        for c in range(nchunk):
            kb.op("act", lambda e, c=c: e.activation(sq[:, c, 0:w], src_chunks(c), AF.Square),
                  reads=[src_res(c)], writes=[sqres[c]])
        pt, pr = kb.ps()
        kb.mm(pt[:, 0:w], pr, [(self.ones_bf[:], sq[:, c, 0:w], [sqres[c], self.r_const]) for c in range(nchunk)])
        kb.op("act", lambda e: e.activation(rstd_tile[:, 0:w], pt[:, 0:w], AF.Sqrt,
                                            bias=(eps_ap if eps_ap is not None else self.eps_t[:, 0:1]), scale=scale),
              reads=[pr, self.r_const], writes=[rstd_res])
        kb.op("dve", lambda e: e.reciprocal(rstd_tile[:, 0:w], rstd_tile[:, 0:w]), reads=[rstd_res], writes=[rstd_res])

    def setup(self):
        cfg, kb, nc = self.cfg, self.kb, self.nc
        L = cfg.depth
        NT = cfg.ntok
        kb.init_psum()
        self.consts()
        self.xin = self.inp("xin", [P, KC, NT])
        self.cT = self.inp("cT", [P, KC * NSEQ])
        self.w_mod = self.inp("w_mod", [L, P, KC, 9 * D])
        self.b_mod = self.inp("b_mod", [P, L * 72])
        self.g_norm = self.inp("g_norm", [P, L * 6 * KC])
        self.w_in = self.inp("w_in", [L, 2, P, KC, 2 * DFF])
        self.w_out = self.inp("w_out", [L, 2, P, FC, D])
        self.xs = kb.dram("xs", [P, KC, NT], F32)
        self.hid = kb.dram("hid", [P, FC, NT], BF16)
        self.xs_res = [Res(f"xs{i}") for i in range(len(cfg.tiles))]
        self.hid_res = [Res(f"hid{i}") for i in range(len(cfg.tiles))]
        self.modv = kb.sb([P, L, 72, NSEQ], F32, "modv")
        self.AV = kb.sb([P, L * 3, KC, NSEQ], F32, "AV")
        self.CV = kb.sb([P, L * 3, KC, NSEQ], F32, "CV")
        self.gn = kb.sb([P, L * 6, KC], F32, "gn")
        self.r_mod = Res("mod")
        self.first_x = True
        self.compute_mods()

    def compute_mods(self):
        cfg, kb = self.cfg, self.kb
        L = cfg.depth
        with self.scope():
            cf = kb.sb([P, KC * NSEQ], F32, "cf")
            cb = kb.sb([P, KC, NSEQ], BF16, "cb")
            bm = kb.sb([P, L * 72], F32, "bm")
            rc = Res("c")
            kb.dma("sp", cf[:], self.cT[:, :], writes=[rc])
            kb.dma("sp", bm[:], self.b_mod[:, :], writes=[rc])
            kb.dma("sp", self.gn[:].rearrange("p a c -> p (a c)"), self.g_norm[:, :], writes=[self.r_mod])
            kb.op("act", lambda e: e.activation(cb[:].rearrange("p c s -> p (c s)"), cf[:], AF.Silu), reads=[rc], writes=[rc])
            wpool = Rot(kb, 2, [P, KC, 1152], BF16, "wm")
            for l in range(L):
                pt, pr = kb.ps()
                for blk in range(8):
                    wt, wr = wpool.next()
                    self.load_w(wt, wr[0], self.w_mod[l, :, :, blk * 1152:(blk + 1) * 1152], 2)
                    for o in range(9):
                        oc = blk * 9 + o
                        kb.mm(pt[:, oc * NSEQ:(oc + 1) * NSEQ], pr,
                              [(wt[:, kc, o * 128:(o + 1) * 128], cb[:, kc, :], [wr[0], rc]) for kc in range(KC)])
                kb.op("dve", lambda e, l=l, pt=pt: e.tensor_tensor(
                    out=self.modv[:, l, :, :], in0=pt[:, 0:72 * NSEQ].rearrange("p (a s) -> p a s", s=NSEQ),
                    in1=bm[:, l * 72:(l + 1) * 72].unsqueeze(2).to_broadcast([P, 72, NSEQ]), op=ALU.add),
                    reads=[pr, rc], writes=[self.r_mod])
                for s in range(3):
                    wres = 1.0 if s == 1 else 0.5
                    gpre = self.gn[:, (l * 3 + s) * 2 + 0, :].unsqueeze(2).to_broadcast([P, KC, NSEQ])
                    gpost = self.gn[:, (l * 3 + s) * 2 + 1, :].unsqueeze(2).to_broadcast([P, KC, NSEQ])
                    sc = self.modv[:, l, (s * 3 + 1) * KC:(s * 3 + 2) * KC, :]
                    gt = self.modv[:, l, (s * 3 + 2) * KC:(s * 3 + 3) * KC, :]
                    kb.op("dve", lambda e, sc=sc, gpre=gpre, l=l, s=s: e.scalar_tensor_tensor(
                        out=self.AV[:, l * 3 + s, :, :], in0=sc, scalar=1.0, in1=gpre, op0=ALU.add, op1=ALU.mult),
                        reads=[self.r_mod], writes=[self.r_mod])
                    kb.op("dve", lambda e, gt=gt, gpost=gpost, l=l, s=s, wres=wres: e.scalar_tensor_tensor(
                        out=self.CV[:, l * 3 + s, :, :], in0=gt, scalar=wres, in1=gpost, op0=ALU.mult, op1=ALU.mult),
                        reads=[self.r_mod], writes=[self.r_mod])

    def vA(self, l, s, c, q):
        return self.AV[:, l * 3 + s, c, q:q + 1]

    def vB(self, l, s, c, q):
        return self.modv[:, l, (s * 3) * KC + c, q:q + 1]

    def vC(self, l, s, c, q):
        return self.CV[:, l * 3 + s, c, q:q + 1]

    def x_src(self):
        if self.first_x:
            return self.xin
        return self.xs

    def load_x(self, pool, tile):
        c0, w, segs, ti = tile
        xt, xr = pool.next()
        self.kb.dma("sp", xt[:, :, 0:w], self.x_src()[:, :, c0:c0 + w], reads=[self.xs_res[ti]], writes=xr)
        return xt, xr

    def modulate(self, l, s, tile, xt, xr, out_fn, out_res, pools, ident=None):
        kb = self.kb
        c0, w, segs, ti = tile
        rstd, rres = pools["rstd"].next()
        self.sumsq_rstd(lambda c: xt[:, c, 0:w], lambda c: xr[c], w, pools["sq"], rstd, rres[0])
        for c in range(KC):
            for (s0, sw, q) in segs:
                tmp, tr = pools["tmp"].next()
                a_ap = self.vA(l, s, c, q) if ident is None else ident(c)
                kb.op("dve", lambda e, c=c, s0=s0, sw=sw, tmp=tmp, a_ap=a_ap: e.scalar_tensor_tensor(
                    out=tmp[:, 0:sw], in0=xt[:, c, s0:s0 + sw], scalar=a_ap, in1=rstd[:, s0:s0 + sw],
                    op0=ALU.mult, op1=ALU.mult), reads=[xr[c], rres[0], self.r_mod], writes=[tr[0]])
                if ident is None:
                    b_ap = self.vB(l, s, c, q)
                    kb.op("act", lambda e, c=c, s0=s0, sw=sw, tmp=tmp, b_ap=b_ap: e.activation(
                        out_fn(c, s0, sw), tmp[:, 0:sw], AF.Identity, bias=b_ap, scale=1.0),
                        reads=[tr[0], self.r_mod], writes=[out_res[c]])
                else:
                    kb.op("act", lambda e, c=c, s0=s0, sw=sw, tmp=tmp: e.activation(
                        out_fn(c, s0, sw), tmp[:, 0:sw], AF.Copy), reads=[tr[0]], writes=[out_res[c]])

    def post_residual(self, l, s, tile, y_fn, y_res, xt, xr, pools):
        kb = self.kb
        c0, w, segs, ti = tile
        rstd, rres = pools["rstd"].next()
        self.sumsq_rstd(y_fn, y_res, w, pools["sq"], rstd, rres[0])
        for c in range(KC):
            for (s0, sw, q) in segs:
                tmp, tr = pools["tmp"].next()
                kb.op("dve", lambda e, c=c, s0=s0, sw=sw, tmp=tmp, q=q: e.scalar_tensor_tensor(
                    out=tmp[:, 0:sw], in0=y_fn(c)[:, s0:s0 + sw], scalar=self.vC(l, s, c, q), in1=rstd[:, s0:s0 + sw],
                    op0=ALU.mult, op1=ALU.mult), reads=[y_res(c), rres[0], self.r_mod], writes=[tr[0]])
                kb.op("pool", lambda e, c=c, s0=s0, sw=sw, tmp=tmp: e.tensor_tensor(
                    out=xt[:, c, s0:s0 + sw], in0=xt[:, c, s0:s0 + sw], in1=tmp[:, 0:sw], op=ALU.add),
                    reads=[tr[0], xr[c]], writes=[xr[c]])
        kb.dma("pool", self.xs[:, :, c0:c0 + w], xt[:, :, 0:w], reads=xr, writes=[self.xs_res[ti]])

    def std_pools(self):
        kb = self.kb
        return {
            "rstd": Rot(kb, 2, [P, 512], F32, "rstd"),
            "sq": Rot(kb, 1, [P, KC, 512], BF16, "sq", nres=KC),
            "tmp": Rot(kb, 3, [P, 512], F32, "tmp"),
        }

    def ffn(self, l, s):
        cfg, kb = self.cfg, self.kb
        fi = 0 if s == 0 else 1
        ntile = len(cfg.tiles)
        with self.scope():
            win = kb.sb([P, KC, 2 * DFF], BF16, "win")
            rw = Res("win")
            self.load_w(win, rw, self.w_in[l, fi], 8)
            pools = self.std_pools()
            xpool = Rot(kb, 2, [P, KC, 512], F32, "xt", nres=KC)
            hpool = Rot(kb, 2, [P, KC, 512], BF16, "ht", nres=KC)
            spool = Rot(kb, 2, [P, 11, 512], BF16, "hs", nres=1)
            pools["tmp"] = Rot(kb, 4, [P, 512], F32, "tmpA")
            gpool = Rot(kb, 3, [P, 512], F32, "sg")
            def prep(tile):
                xt, xr = self.load_x(xpool, tile)
                ht, hr = hpool.next()
                self.modulate(l, s, tile, xt, xr, lambda c, s0, sw: ht[:, c, s0:s0 + sw], hr, pools)
                return ht, hr
            nxt = prep(cfg.tiles[0])
            for i, tile in enumerate(cfg.tiles):
                c0, w, segs, ti = tile
                ht, hr = nxt
                if i + 1 < ntile:
                    nxt = prep(cfg.tiles[i + 1])
                for half in range(2):
                    st, sr = spool.next()
                    for jj in range(11):
                        j = half * 11 + jj
                        pg, rg = kb.ps()
                        kb.mm(pg[:, 0:w], rg, [(win[:, kc, j * 128:(j + 1) * 128], ht[:, kc, 0:w], [rw, hr[kc]]) for kc in range(KC)])
                        pu, ru = kb.ps()
                        kb.mm(pu[:, 0:w], ru, [(win[:, kc, DFF + j * 128:DFF + (j + 1) * 128], ht[:, kc, 0:w], [rw, hr[kc]]) for kc in range(KC)])
                        sg, sgr = gpool.next()
                        kb.op("act", lambda e, sg=sg, pg=pg: e.activation(sg[:, 0:w], pg[:, 0:w], AF.Silu), reads=[rg], writes=[sgr[0]])
                        kb.op("dve", lambda e, sg=sg, pu=pu, st=st, jj=jj: e.tensor_tensor(
                            out=st[:, jj, 0:w], in0=sg[:, 0:w], in1=pu[:, 0:w], op=ALU.mult), reads=[sgr[0], ru], writes=[sr[0]])
                    kb.dma("pool", self.hid[:, half * 11:(half + 1) * 11, c0:c0 + w], st[:, :, 0:w], reads=[sr[0]],
                           writes=[self.hid_res[ti]])
        with self.scope():
            wout = kb.sb([P, FC, D], BF16, "wout")
            rw = Res("wout")
            self.load_w(wout, rw, self.w_out[l, fi], 4)
            pools = self.std_pools()
            xpool = Rot(kb, 2, [P, KC, 512], F32, "xt", nres=KC)
            ipool = Rot(kb, 2, [P, FC, 512], BF16, "hin", nres=1)
            ypool = Rot(kb, 2, [P, KC, 512], F32, "y", nres=KC)
            pend = None
            for tile in cfg.tiles:
                c0, w, segs, ti = tile
                hin, hir = ipool.next()
                kb.dma("sp", hin[:, :, 0:w], self.hid[:, :, c0:c0 + w], reads=[self.hid_res[ti]], writes=hir)
                xt, xr = self.load_x(xpool, tile)
                yt, yr = ypool.next()
                for oc in range(KC):
                    py, ry = kb.ps()
                    kb.mm(py[:, 0:w], ry, [(wout[:, kc, oc * 128:(oc + 1) * 128], hin[:, kc, 0:w], [rw, hir[0]]) for kc in range(FC)])
                    kb.op("act", lambda e, oc=oc, py=py, yt=yt: e.copy(yt[:, oc, 0:w], py[:, 0:w]), reads=[ry], writes=[yr[oc]])
                if pend is not None:
                    pend()
                pend = (lambda tile=tile, yt=yt, yr=yr, xt=xt, xr=xr, w=w: self.post_residual(
                    l, s, tile, lambda c: yt[:, c, 0:w], lambda c: yr[c], xt, xr, pools))
            pend()
        self.first_x = False

    def setup_rwkv(self):
        cfg, kb = self.cfg, self.kb
        na, NT = cfg.n_a, cfg.ntok
        self.mu = self.inp("mu", [P, na * 6 * KC])
        self.rvec = self.inp("rvec", [P, na * 5 * KC])
        self.v0_64 = self.inp("v0_64", [HD, max(1, na - 1) * NH])
        self.lnx64 = self.inp("lnx64", [HD, na * 2 * NH])
        self.w_rkv = self.inp("w_rkv", [na, 3, P, KC, D])
        self.w_o64 = self.inp("w_o64", [na, HD, NH, D])
        self.w1 = self.inp("w1", [na, P, KC, 64])
        self.a1 = self.inp("a1", [na, P, KC, 64])
        self.v1 = self.inp("v1", [max(1, na - 1), P, KC, 32])
        self.g1 = self.inp("g1", [na, P, KC, 160])
        self.w2 = self.inp("w2", [na, 64, 1, D])
        self.a2 = self.inp("a2", [na, 64, 1, D])
        self.v2 = self.inp("v2", [max(1, na - 1), 32, 1, D])
        self.g2 = self.inp("g2", [na, 160, 1, D])
        self.sh0 = self.inp("sh0", [P, na * NS * KC])
        self.hst0 = self.inp("hst0", [na, NS, P, KC * HD])
        self.o_shift = self.outp("o_shift", [P, na * NSEQ * KC])
        self.o_wkv = self.outp("o_wkv", [na, NSEQ, P, KC * HD])
        names = ["rt", "kt", "at", "kkt", "eg"]
        self.wk_d = {n: kb.dram("wk_" + n, [P, KC, NT], F32) for n in names}
        for n in ["v64", "g64", "bn64", "o64", "vf64"]:
            self.wk_d[n] = kb.dram("wk_" + n, [HD, NH, NT], F32)
        self.wk_res = {n: Res("wk_" + n) for n in self.wk_d}
        self.tiles256 = []
        for t in range(cfg.seq // 256):
            self.tiles256.append((t * 256, 256, [(0, 256, 0)], t // 2))
        self.tiles256.append((cfg.seq, NS * SLOT, [(s * SLOT, SLOT, 1 + s) for s in range(NS)], len(cfg.tiles) - 1))

    def rwkv_proj(self, l):
        cfg, kb = self.cfg, self.kb
        na = cfg.n_a
        W = 256
        with self.scope():
            rW = Res("rw")
            wr = kb.sb([P, KC, D], BF16, "wr")
            wk = kb.sb([P, KC, D], BF16, "wk")
            wv = kb.sb([P, KC, D], BF16, "wv")
            for t, i in ((wr, 0), (wk, 1), (wv, 2)):
                self.load_w(t, rW, self.w_rkv[l, i], 2)
            w1 = kb.sb([P, KC, 64], BF16, "w1")
            a1 = kb.sb([P, KC, 64], BF16, "a1")
            g1 = kb.sb([P, KC, 160], BF16, "g1")
            self.load_w(w1, rW, self.w1[l], 1)
            self.load_w(a1, rW, self.a1[l], 1)
            self.load_w(g1, rW, self.g1[l], 1)
            w2 = kb.sb([64, 1, D], BF16, "w2")
            a2 = kb.sb([64, 1, D], BF16, "a2")
            g2a = kb.sb([P, 1, D], BF16, "g2a")
            g2b = kb.sb([32, 1, D], BF16, "g2b")
            self.load_w(w2, rW, self.w2[l], 1)
            self.load_w(a2, rW, self.a2[l], 1)
            self.load_w(g2a, rW, self.g2[l, 0:128], 1)
            self.load_w(g2b, rW, self.g2[l, 128:160], 1)
            if l > 0:
                v1 = kb.sb([P, KC, 32], BF16, "v1")
                v2 = kb.sb([32, 1, D], BF16, "v2")
                self.load_w(v1, rW, self.v1[l - 1], 1)
                self.load_w(v2, rW, self.v2[l - 1], 1)
                v0 = kb.sb([HD, NH], F32, "v0")
                kb.dma("sp", v0[:], self.v0_64[:, (l - 1) * NH:l * NH], writes=[rW])
            mu = kb.sb([P, 6, KC], F32, "mu")
            kb.dma("sp", mu[:].rearrange("p a c -> p (a c)"), self.mu[:, l * 6 * KC:(l + 1) * 6 * KC], writes=[rW])
            rv = kb.sb([P, 5, KC], F32, "rv")
            kb.dma("sp", rv[:].rearrange("p a c -> p (a c)"), self.rvec[:, l * 5 * KC:(l + 1) * 5 * KC], writes=[rW])
            sh0 = kb.sb([P, NS, KC], F32, "sh0")
            kb.dma("sp", sh0[:].rearrange("p a c -> p (a c)"), self.sh0[:, l * NS * KC:(l + 1) * NS * KC], writes=[rW])
            bones = kb.sb([P, P], BF16, "bones")
            kb.op("pool", lambda e: e.memset(bones[:], 0.0), writes=[rW])
            kb.op("pool", lambda e: e.memset(bones[0:64, 0:64], 1.0), writes=[rW])
            kb.op("pool", lambda e: e.memset(bones[64:128, 64:128], 1.0), writes=[rW])
            mask64 = kb.sb([P, W], F32, "m64")
            mask32 = kb.sb([P, NS * SLOT], F32, "m32")
            kb.op("pool", lambda e: e.memset(mask64[:], 1.0), writes=[rW])
            kb.op("pool", lambda e: e.memset(mask64[:].rearrange("p (a b) -> p a b", b=64)[:, :, 0:1], 0.0), writes=[rW])
            kb.op("pool", lambda e: e.memset(mask32[:], 1.0), writes=[rW])
            kb.op("pool", lambda e: e.memset(mask32[:].rearrange("p (a b) -> p a b", b=SLOT)[:, :, 0:1], 0.0), writes=[rW])
            carry = kb.sb([P, KC, 2], F32, "carry")
            rcar = Res("carry")
            kb.op("pool", lambda e: e.memset(carry[:], 0.0), writes=[rcar])
            tiny = kb.sb([P, 1], F32, "tiny")
            kb.op("pool", lambda e: e.memset(tiny[:], 1e-24), writes=[rW])
            osh = kb.sb([P, NSEQ, KC], F32, "osh")
            rosh = Res("osh")

            pools = {"rstd": Rot(kb, 2, [P, W], F32, "rstd"), "sq": Rot(kb, 1, [P, KC, W], BF16, "sq", nres=KC),
                     "tmp": Rot(kb, 3, [P, W], F32, "tmp")}
            xpool = Rot(kb, 1, [P, KC, W], F32, "xt", nres=KC)
            hfp = Rot(kb, 1, [P, KC, W], F32, "hf", nres=KC)
            xxp = Rot(kb, 1, [P, KC, W], F32, "xx", nres=1)
            mixp = [Rot(kb, 1, [P, KC, W], BF16, f"mix{m}", nres=1) for m in range(6)]
            lorp = {n: Rot(kb, 1, [sz, W], BF16, n) for n, sz in (("tw", 64), ("ta", 64), ("tv", 32), ("tg0", 128), ("tg1", 32))}
            t32 = Rot(kb, 14, [P, W], F32, "t32")
            tb16 = Rot(kb, 3, [P, W], BF16, "tb16")
            stg = Rot(kb, 2, [P, 5, W], F32, "stg")
            v64p = Rot(kb, 1, [HD, NH, W], F32, "v64t")
            b64p = Rot(kb, 1, [HD, NH, W], F32, "b64t")
            t64 = Rot(kb, 8, [HD, W], F32, "t64")
            MI = {"r": 0, "w": 1, "k": 2, "v": 3, "a": 4, "g": 5}

            x512 = {}
            for (c0, w, segs, ti5) in self.tiles256:
                is_samp = (c0 >= cfg.seq)
                xt_full, xr = xpool.next()
                kb.dma("sp", xt_full[:, :, 0:w], self.x_src()[:, :, c0:c0 + w], reads=[self.xs_res[ti5]], writes=xr)
                off = 0
                hf, hr = hfp.next()
                self._mod256(l, ti5, off, w, segs, xt_full, xr, hf, hr, pools)
                xx, xxr = xxp.next()
                for (s0, sw, q) in segs:
                    kb.op("pool", lambda e, s0=s0, sw=sw: e.tensor_tensor(
                        out=xx[:, :, s0 + 1:s0 + sw], in0=hf[:, :, s0:s0 + sw - 1], in1=hf[:, :, s0 + 1:s0 + sw], op=ALU.subtract),
                        reads=hr, writes=[xxr[0]])
                    prev = carry[:, :, 0:1] if not is_samp else sh0[:, q - 1, :].unsqueeze(2)
                    kb.op("pool", lambda e, s0=s0, prev=prev: e.tensor_tensor(
                        out=xx[:, :, s0:s0 + 1], in0=prev, in1=hf[:, :, s0:s0 + 1], op=ALU.subtract),
                        reads=hr + [rcar, rW], writes=[xxr[0]])
                    last = s0 + sw - 1 if not is_samp else s0 + DEC - 1
                    if not is_samp:
                        kb.op("pool", lambda e, last=last: e.tensor_copy(out=carry[:, :, 0:1], in_=hf[:, :, last:last + 1]),
                              reads=hr, writes=[rcar])
                    if is_samp or c0 + w == cfg.seq:
                        kb.op("pool", lambda e, last=last, q=q: e.tensor_copy(out=osh[:, q, :].unsqueeze(2), in_=hf[:, :, last:last + 1]),
                              reads=hr, writes=[rosh])
                mixes = []
                for m in range(6):
                    if m == MI["v"] and False:
                        pass
                    mt, mr = mixp[m].next()
                    for c in range(KC):
                        kb.op("dve", lambda e, m=m, c=c, mt=mt: e.scalar_tensor_tensor(
                            out=mt[:, c, 0:w], in0=xx[:, c, 0:w], scalar=mu[:, m, c:c + 1], in1=hf[:, c, 0:w],
                            op0=ALU.mult, op1=ALU.add), reads=[xxr[0], hr[c], rW], writes=[mr[0]])
                    mixes.append((mt, mr[0]))
                xr_, xw_, xk_, xv_, xa_, xg_ = mixes

                def proj(out_ap, out_res, wt, col0, ncol, mix, kparts=P):
                    kb.mm(out_ap, out_res, [(wt[:, kc, col0:col0 + ncol], mix[0][:, kc, 0:w], [rW, mix[1]]) for kc in range(KC)])

                tw, twr = lorp["tw"].next()
                pt, pr = kb.ps()
                proj(pt[0:64, 0:w], pr, w1, 0, 64, xw_)
                kb.op("act", lambda e: e.activation(tw[:, 0:w], pt[0:64, 0:w], AF.Tanh), reads=[pr], writes=[twr[0]])
                ta, tar = lorp["ta"].next()
                pt, pr = kb.ps()
                proj(pt[0:64, 0:w], pr, a1, 0, 64, xa_)
                kb.op("act", lambda e, pt=pt: e.copy(ta[:, 0:w], pt[0:64, 0:w]), reads=[pr], writes=[tar[0]])
                tg0, tg0r = lorp["tg0"].next()
                pt, pr = kb.ps()
                proj(pt[:, 0:w], pr, g1, 0, 128, xg_)
                kb.op("act", lambda e, pt=pt: e.activation(tg0[:, 0:w], pt[:, 0:w], AF.Sigmoid), reads=[pr], writes=[tg0r[0]])
                tg1, tg1r = lorp["tg1"].next()
                pt, pr = kb.ps()
                proj(pt[0:32, 0:w], pr, g1, 128, 32, xg_)
                kb.op("act", lambda e, pt=pt: e.activation(tg1[:, 0:w], pt[0:32, 0:w], AF.Sigmoid), reads=[pr], writes=[tg1r[0]])
                if l > 0:
                    tv, tvr = lorp["tv"].next()
                    pt, pr = kb.ps()
                    proj(pt[0:32, 0:w], pr, v1, 0, 32, xv_)
                    kb.op("act", lambda e, pt=pt: e.copy(tv[:, 0:w], pt[0:32, 0:w]), reads=[pr], writes=[tvr[0]])

                v64t, v64r = v64p.next()
                for h in range(NH):
                    pv, prv = kb.ps()
                    proj(pv[0:64, 0:w], prv, wv, h * 64, 64, xv_)
                    if l == 0:
                        kb.op("act", lambda e, h=h, pv=pv: e.copy(v64t[:, h, 0:w], pv[0:64, 0:w]), reads=[prv], writes=[v64r[0]])
                    else:
                        pl, prl = kb.ps()
                        kb.mm(pl[0:64, 0:w], prl, [(v2[:, 0, h * 64:(h + 1) * 64], tv[:, 0:w], [rW, tvr[0]])])
                        vg, vgr = t64.next()
                        kb.op("act", lambda e, h=h, pl=pl, vg=vg: e.activation(vg[:, 0:w], pl[0:64, 0:w], AF.Sigmoid, bias=v0[:, h:h + 1], scale=1.0),
                              reads=[prl, rW], writes=[vgr[0]])
                        dd, ddr = t64.next()
                        kb.dma("sp", dd[:, 0:w], self.wk_d["vf64"][:, h, c0:c0 + w], reads=[self.wk_res["vf64"]], writes=[ddr[0]])
                        kb.op("dve", lambda e, h=h, pv=pv, dd=dd: e.tensor_tensor(out=dd[:, 0:w], in0=dd[:, 0:w], in1=pv[0:64, 0:w], op=ALU.subtract),
                              reads=[prv, ddr[0]], writes=[ddr[0]])
                        kb.op("pool", lambda e, dd=dd, vg=vg: e.tensor_tensor(out=dd[:, 0:w], in0=dd[:, 0:w], in1=vg[:, 0:w], op=ALU.mult),
                              reads=[ddr[0], vgr[0]], writes=[ddr[0]])
                        kb.op("dve", lambda e, h=h, pv=pv, dd=dd: e.tensor_tensor(out=v64t[:, h, 0:w], in0=dd[:, 0:w], in1=pv[0:64, 0:w], op=ALU.add),
                              reads=[prv, ddr[0]], writes=[v64r[0]])
                    pg, prg = kb.ps()
                    kb.mm(pg[0:64, 0:w], prg, [(g2a[:, 0, h * 64:(h + 1) * 64], tg0[:, 0:w], [rW, tg0r[0]]),
                                               (g2b[:, 0, h * 64:(h + 1) * 64], tg1[:, 0:w], [rW, tg1r[0]])])
                    gg, ggr = t64.next()
                    kb.op("act", lambda e, h=h, pg=pg, gg=gg: e.copy(gg[:, 0:w], pg[0:64, 0:w]), reads=[prg], writes=[ggr[0]])
                    kb.dma("pool", self.wk_d["g64"][:, h, c0:c0 + w], gg[:, 0:w], reads=[ggr[0]], writes=[self.wk_res["g64"]])
                kb.dma("pool", self.wk_d["v64"][:, :, c0:c0 + w], v64t[:, :, 0:w], reads=v64r, writes=[self.wk_res["v64"]])
                if l == 0:
                    kb.dma("pool", self.wk_d["vf64"][:, :, c0:c0 + w], v64t[:, :, 0:w], reads=v64r, writes=[self.wk_res["vf64"]])

                b64t, b64r = b64p.next()
                msk = mask32 if is_samp else mask64
                for c in range(KC):
                    cs = slice(c * 128, (c + 1) * 128)
                    p_r, r_r = kb.ps()
                    proj(p_r[:, 0:w], r_r, wr, c * 128, 128, xr_)
                    p_k, r_k = kb.ps()
                    proj(p_k[:, 0:w], r_k, wk, c * 128, 128, xk_)
                    p_w, r_w = kb.ps()
                    kb.mm(p_w[:, 0:w], r_w, [(w2[:, 0, cs], tw[:, 0:w], [rW, twr[0]])])
                    p_a, r_a = kb.ps()
                    kb.mm(p_a[:, 0:w], r_a, [(a2[:, 0, cs], ta[:, 0:w], [rW, tar[0]])])
                    T = {}

                    def tmp(name):
                        T[name] = t32.next()
                        return T[name][0]
                    a_ = tmp("a")
                    kb.op("act", lambda e: e.activation(a_[:, 0:w], p_a[:, 0:w], AF.Sigmoid, bias=rv[:, 1, c:c + 1], scale=1.0),
                          reads=[r_a, rW], writes=[T["a"][1][0]])
                    lw = tmp("lw")
                    kb.op("act", lambda e: e.activation(lw[:, 0:w], p_w[:, 0:w], AF.Sigmoid, bias=rv[:, 0, c:c + 1], scale=1.0),
                          reads=[r_w, rW], writes=[T["lw"][1][0]])
                    kb.op("dve", lambda e: e.tensor_scalar(out=lw[:, 0:w], in0=lw[:, 0:w], scalar1=-0.6065306597126334, scalar2=None, op0=ALU.mult),
                          reads=[T["lw"][1][0]], writes=[T["lw"][1][0]])
                    gc = tmp("gc")
                    kb.op("dve", lambda e: e.tensor_tensor_scan(out=gc[:, 0:w], data0=msk[:, 0:w], data1=lw[:, 0:w], initial=0.0,
                                                                 op0=ALU.mult, op1=ALU.add),
                          reads=[T["lw"][1][0], rW], writes=[T["gc"][1][0]])
                    st, sr = stg.next()
                    kb.op("act", lambda e: e.activation(st[:, 4, 0:w], gc[:, 0:w], AF.Exp), reads=[T["gc"][1][0]], writes=[sr[0]])
                    eng_ = tmp("eng")
                    kb.op("act", lambda e: e.activation(eng_[:, 0:w], gc[:, 0:w], AF.Exp, scale=-1.0), reads=[T["gc"][1][0]], writes=[T["eng"][1][0]])
                    egm = tmp("egm")
                    kb.op("pool", lambda e: e.tensor_tensor(out=egm[:, 0:w], in0=gc[:, 0:w], in1=lw[:, 0:w], op=ALU.subtract),
                          reads=[T["gc"][1][0], T["lw"][1][0]], writes=[T["egm"][1][0]])
                    kb.op("act", lambda e: e.activation(egm[:, 0:w], egm[:, 0:w], AF.Exp), reads=[T["egm"][1][0]], writes=[T["egm"][1][0]])
                    kkn = tmp("kkn")
                    kb.op("dve", lambda e: e.tensor_scalar(out=kkn[:, 0:w], in0=p_k[:, 0:w], scalar1=rv[:, 2, c:c + 1], scalar2=None, op0=ALU.mult),
                          reads=[r_k, rW], writes=[T["kkn"][1][0]])
                    k2, k2r = tb16.next()
                    kb.op("act", lambda e: e.activation(k2[:, 0:w], kkn[:, 0:w], AF.Square), reads=[T["kkn"][1][0]], writes=[k2r[0]])
                    p_s, r_s = kb.ps()
                    kb.mm(p_s[:, 0:w], r_s, [(bones[:], k2[:, 0:w], [rW, k2r[0]])])
                    rn = tmp("rn")
                    kb.op("dve", lambda e: e.tensor_scalar(out=rn[:, 0:w], in0=p_s[:, 0:w], scalar1=1e-24, scalar2=None, op0=ALU.max),
                          reads=[r_s], writes=[T["rn"][1][0]])
                    kb.op("act", lambda e: e.activation(rn[:, 0:w], rn[:, 0:w], AF.Sqrt), reads=[T["rn"][1][0]], writes=[T["rn"][1][0]])
                    kb.op("dve", lambda e: e.reciprocal(rn[:, 0:w], rn[:, 0:w]), reads=[T["rn"][1][0]], writes=[T["rn"][1][0]])
                    kb.op("pool", lambda e: e.tensor_tensor(out=kkn[:, 0:w], in0=kkn[:, 0:w], in1=rn[:, 0:w], op=ALU.mult),
                          reads=[T["rn"][1][0], T["kkn"][1][0]], writes=[T["kkn"][1][0]])
                    kp = tmp("kp")
                    kb.op("dve", lambda e: e.tensor_scalar(out=kp[:, 0:w], in0=a_[:, 0:w], scalar1=-1.0, scalar2=rv[:, 3, c:c + 1], op0=ALU.add, op1=ALU.mult),
                          reads=[T["a"][1][0], rW], writes=[T["kp"][1][0]])
                    kb.op("dve", lambda e: e.scalar_tensor_tensor(out=kp[:, 0:w], in0=kp[:, 0:w], scalar=1.0, in1=p_k[:, 0:w], op0=ALU.add, op1=ALU.mult),
                          reads=[T["kp"][1][0], r_k], writes=[T["kp"][1][0]])
                    kb.op("pool", lambda e: e.tensor_tensor(out=a_[:, 0:w], in0=a_[:, 0:w], in1=kkn[:, 0:w], op=ALU.mult),
                          reads=[T["a"][1][0], T["kkn"][1][0]], writes=[T["a"][1][0]])
                    pb_, pbr = tb16.next()
                    kb.op("dve", lambda e: e.scalar_tensor_tensor(out=pb_[:, 0:w], in0=p_r[:, 0:w], scalar=rv[:, 4, c:c + 1], in1=kp[:, 0:w], op0=ALU.mult, op1=ALU.mult),
                          reads=[r_r, T["kp"][1][0], rW], writes=[pbr[0]])
                    for half in range(2):
                        h = 2 * c + half
                        p_b, r_b = kb.ps()
                        kb.mm(p_b[0:64, 0:w], r_b, [(bones[:, half * 64:(half + 1) * 64], pb_[:, 0:w], [rW, pbr[0]])])
                        kb.op("dve", lambda e, h=h, p_b=p_b: e.tensor_tensor(out=b64t[:, h, 0:w], in0=v64t[:, h, 0:w], in1=p_b[0:64, 0:w], op=ALU.mult),
                              reads=[r_b, v64r[0]], writes=[b64r[0]])
                    kb.op("dve", lambda e: e.tensor_tensor(out=st[:, 0, 0:w], in0=st[:, 4, 0:w], in1=p_r[:, 0:w], op=ALU.mult),
                          reads=[r_r, sr[0]], writes=[sr[0]])
                    kb.op("pool", lambda e: e.tensor_tensor(out=st[:, 1, 0:w], in0=kp[:, 0:w], in1=eng_[:, 0:w], op=ALU.mult),
                          reads=[T["kp"][1][0], T["eng"][1][0]], writes=[sr[0]])
                    kb.op("pool", lambda e: e.tensor_tensor(out=st[:, 2, 0:w], in0=a_[:, 0:w], in1=eng_[:, 0:w], op=ALU.mult),
                          reads=[T["a"][1][0], T["eng"][1][0]], writes=[sr[0]])
                    kb.op("pool", lambda e: e.tensor_tensor(out=st[:, 3, 0:w], in0=kkn[:, 0:w], in1=egm[:, 0:w], op=ALU.mult),
                          reads=[T["kkn"][1][0], T["egm"][1][0]], writes=[sr[0]])
                    for i, n in enumerate(["rt", "kt", "at", "kkt", "eg"]):
                        kb.dma("sp" if i % 2 else "pool", self.wk_d[n][:, c, c0:c0 + w], st[:, i, 0:w], reads=[sr[0]], writes=[self.wk_res[n]])
                kb.dma("pool", self.wk_d["bn64"][:, :, c0:c0 + w], b64t[:, :, 0:w], reads=b64r, writes=[self.wk_res["bn64"]])
            kb.dma("pool", self.o_shift[:, l * NSEQ * KC:(l + 1) * NSEQ * KC], osh[:].rearrange("p a c -> p (a c)"), reads=[rosh])

    def _mod256(self, l, ti5, off, w, segs, xt, xr, hf, hr, pools):
        kb = self.kb
        rstd, rres = pools["rstd"].next()
        self.sumsq_rstd(lambda c: xt[:, c, off:off + w], lambda c: xr[c], w, pools["sq"], rstd, rres[0])
        for c in range(KC):
            for (s0, sw, q) in segs:
                tmp, tr = pools["tmp"].next()
                kb.op("dve", lambda e, c=c, s0=s0, sw=sw, tmp=tmp, q=q: e.scalar_tensor_tensor(
                    out=tmp[:, 0:sw], in0=xt[:, c, off + s0:off + s0 + sw], scalar=self.vA(l, 1, c, q), in1=rstd[:, s0:s0 + sw],
                    op0=ALU.mult, op1=ALU.mult), reads=[xr[c], rres[0], self.r_mod], writes=[tr[0]])
                kb.op("act", lambda e, c=c, s0=s0, sw=sw, tmp=tmp, q=q: e.activation(
                    hf[:, c, s0:s0 + sw], tmp[:, 0:sw], AF.Identity, bias=self.vB(l, 1, c, q), scale=1.0),
                    reads=[tr[0], self.r_mod], writes=[hr[c]])

    def rwkv_wkv(self, l):
        cfg, kb = self.cfg, self.kb
        HG = 8
        dbg = cfg.debug or ""
        with self.scope():
            rC = Res("wkvc")
            ones = kb.sb([P, P], F32, "ones32")
            ident = kb.sb([P, P], F32, "ident")
            mask = kb.sb([P, 5, P], F32, "mask")
            kb.op("pool", lambda e: e.memset(ones[:], 1.0), writes=[rC])
            kb.op("pool", lambda e: e.affine_select(out=ident[:], in_=ones[:], pattern=[[1, P]], compare_op=ALU.is_equal,
                                                    fill=0.0, base=0, channel_multiplier=-1), reads=[rC], writes=[rC])
            for i, (op_, cm, pat) in enumerate([(ALU.is_gt, -1, 1), (ALU.is_gt, 1, -1), (ALU.is_gt, -1, 1), (ALU.is_ge, -1, 1), (ALU.is_ge, -1, 1)]):
                kb.op("pool", lambda e, i=i, op_=op_, cm=cm, pat=pat: e.affine_select(
                    out=mask[:, i, :], in_=ones[:], pattern=[[pat, P]], compare_op=op_, fill=0.0, base=0, channel_multiplier=cm),
                    reads=[rC], writes=[rC])
            kb.op("pool", lambda e: e.memset(mask[0:64, :, 64:128], 0.0), reads=[rC], writes=[rC])
            kb.op("pool", lambda e: e.memset(mask[64:128, :, 0:64], 0.0), reads=[rC], writes=[rC])
            kb.op("pool", lambda e: e.tensor_scalar(out=mask[:, 4, :], in0=mask[:, 4, :], scalar1=-1.0, scalar2=None, op0=ALU.mult),
                  reads=[rC], writes=[rC])
            Hst = [kb.sb([P, KC, HD], F32, f"H{q}") for q in range(NSEQ)]
            Hres = [[Res(f"H{q}_{h}") for h in range(NH)] for q in range(NSEQ)]
            kb.op("pool", lambda e: e.memset(Hst[0][:], 0.0), writes=Hres[0])
            for q in range(1, NSEQ):
                kb.dma("sp", Hst[q][:].rearrange("p c i -> p (c i)"), self.hst0[l, q - 1], writes=Hres[q])

            def zrot(n, shape, name):
                r = Rot(kb, n, shape, F32, name)
                for t, rr in r.tiles:
                    kb.op("pool", lambda e, t=t: e.memset(t[:], 0.0), writes=rr)
                return r
            fpool = {n: Rot(kb, 2, [P, KC, P], F32, "f" + n) for n in ("kt", "at", "eg")}
            fmask = {(n, hf): zrot(2, [P, KC, P], f"fm{n}{hf}") for n in ("rt", "kkt") for hf in range(2)}
            vpool = Rot(kb, 1, [HD, NH, P], F32, "v64b")
            tkp = {(n, ch): zrot(1, [P, KC, P], f"{n}{ch}") for n in ("kTk", "naTk") for ch in range(2)}
            vtp = Rot(kb, 2, [P, NH, HD], F32, "vTk")
            osp = Rot(kb, 1, [HD, NH, P], F32, "ost")
            Gp = Rot(kb, HG + 2, [P, 5, P], F32, "G")
            XYp = Rot(kb, 4 * HG + 2, [P, P], F32, "XY")
            Pp = Rot(kb, HG + 2, [P, P], F32, "Pm")
            WUp = zrot(2 * (HG + 2), [P, HD], "WU")
            hgp = Rot(kb, HG + 2, [P, HD], F32, "hg")

            kb.psrot = list(range(6))
            zt, zr = osp.next()
            kb.op("pool", lambda e: e.memset(zt[:], 0.0), writes=zr)
            kb.dma("pool", self.wk_d["o64"][:, :, cfg.seq:cfg.seq + NS * SLOT], zt[:, :, 0:NS * SLOT], reads=zr, writes=[self.wk_res["o64"]])
            blocks = [(0, b * 128, 128, 64) for b in range(cfg.seq // 128)]
            blocks += [(q, cfg.seq + (q - 1) * SLOT, DEC, DEC) for q in range(1, NSEQ)]
            if "nosamp" in dbg:
                blocks = blocks[:cfg.seq // 128]
            if "noprompt" in dbg:
                blocks = blocks[cfg.seq // 128:]
            for (q, col0, BS, CL) in blocks:
                H = Hst[q]
                nch = BS // CL
                es = [2, 4, 8, 16, 32] if CL == 64 else [2, 4]
                F = {}
                for n in fpool:
                    t, r = fpool[n].next()
                    kb.dma("sp", t[:, :, 0:BS], self.wk_d[n][:, :, col0:col0 + BS], reads=[self.wk_res[n]], writes=r)
                    F[n] = (t, r[0])
                FM = {}
                for (n, hf) in fmask:
                    t, r = fmask[(n, hf)].next()
                    hs = slice(hf * 64, hf * 64 + 64)
                    kb.dma("sp", t[hs, :, 0:BS], self.wk_d[n][hs, :, col0:col0 + BS], reads=[self.wk_res[n]], writes=r)
                    FM[(n, hf)] = (t, r[0])
                vb, vbr = vpool.next()
                kb.dma("sp", vb[:, :, 0:BS], self.wk_d["v64"][:, :, col0:col0 + BS], reads=[self.wk_res["v64"]], writes=vbr)
                TK = {k_: tkp[k_].next() for k_ in tkp}
                vTk, vTr = vtp.next()
                for half4 in range(2):
                    for (src, nm, neg) in ((F["kt"], "kTk", False), (F["at"], "naTk", True)):
                        pt, pr = kb.ps()
                        for cc in range(4):
                            c = half4 * 4 + cc
                            kb.mm(pt[0:BS, cc * 128:(cc + 1) * 128], pr, [(src[0][:, c, 0:BS], ident[:], [src[1], rC])], transpose=True)
                        for ch in range(nch):
                            rows = slice(ch * CL, (ch + 1) * CL)
                            dst, dres = TK[(nm, ch)]
                            dview = dst[rows, half4 * 4:half4 * 4 + 4, :].rearrange("p a b -> p (a b)")
                            if neg:
                                kb.op("act", lambda e, pt=pt, dview=dview, rows=rows: e.mul(dview, pt[rows, :], -1.0), reads=[pr], writes=[dres[0]])
                            else:
                                kb.op("dve", lambda e, pt=pt, dview=dview, rows=rows: e.tensor_copy(out=dview, in_=pt[rows, :]), reads=[pr], writes=[dres[0]])
                    pt, pr = kb.ps()
                    for hh in range(8):
                        h = half4 * 8 + hh
                        kb.mm(pt[0:BS, hh * 64:(hh + 1) * 64], pr, [(vb[:, h, 0:BS], ident[0:64, 0:64], [vbr[0], rC])], transpose=True)
                    kb.op("act", lambda e, pt=pt: e.copy(vTk[0:BS, half4 * 8:half4 * 8 + 8, :].rearrange("p a b -> p (a b)"), pt[0:BS, :]),
                          reads=[pr], writes=[vTr[0]])
                ost, ostr = osp.next()
                if "tr" in dbg:
                    continue
                for hg0 in range(0, NH, HG):
                    heads = list(range(hg0, hg0 + HG))
                    S = {}
                    for h in heads:
                        c, hf = h // 2, h % 2
                        fk_, fa_ = F["kt"][0][:, c, 0:BS], F["at"][0][:, c, 0:BS]
                        fr_, fkk_ = FM[("rt", hf)][0][:, c, 0:BS], FM[("kkt", hf)][0][:, c, 0:BS]
                        rds = [F["kt"][1], F["at"][1], FM[("rt", hf)][1], FM[("kkt", hf)][1]]
                        pa, pra = kb.ps()
                        pb2, prb = kb.ps()
                        kb.mm(pa[0:BS, 0:BS], pra, [(fa_, fkk_, rds)])
                        kb.mm(pa[0:BS, 128:128 + BS], pra, [(fkk_, fa_, rds)])
                        kb.mm(pa[0:BS, 256:256 + BS], pra, [(fk_, fkk_, rds)])
                        kb.mm(pa[0:BS, 384:384 + BS], pra, [(fk_, fr_, rds)])
                        kb.mm(pb2[0:BS, 0:BS], prb, [(fa_, fr_, rds)])
                        G, Gr = Gp.next()
                        kb.op("dve", lambda e, G=G, pa=pa: e.tensor_tensor(
                            out=G[0:BS, 0:4, 0:BS], in0=pa[0:BS, :].rearrange("p (a b) -> p a b", b=128)[:, :, 0:BS],
                            in1=mask[0:BS, 0:4, 0:BS], op=ALU.mult), reads=[pra, rC], writes=[Gr[0]])
                        kb.op("dve", lambda e, G=G, pb2=pb2: e.tensor_tensor(
                            out=G[0:BS, 4, 0:BS], in0=pb2[0:BS, 0:BS], in1=mask[0:BS, 4, 0:BS], op=ALU.mult),
                            reads=[prb, rC], writes=[Gr[0]])
                        Pm, Pr = Pp.next()
                        kb.op("pool", lambda e, G=G, Pm=Pm: e.tensor_tensor(out=Pm[0:BS, 0:BS], in0=ident[0:BS, 0:BS], in1=G[0:BS, 0, 0:BS], op=ALU.subtract),
                              reads=[Gr[0], rC], writes=[Pr[0]])
                        S[h] = dict(G=G, Gr=Gr[0], Pm=Pm, Pr=Pr[0], X=G[0:BS, 0, 0:BS], Y=G[0:BS, 1, 0:BS], Xr=Gr[0], Yr=Gr[0])
                    if "gram" in dbg:
                        continue
                    for li, e_ in enumerate(es):
                        last = (li == len(es) - 1)
                        for h in heads:
                            s = S[h]
                            py, pyr = kb.ps()
                            kb.mm(py[0:BS, 0:BS], pyr, [(s["X"], s["Y"], [s["Xr"], s["Yr"]])])
                            Y2, Y2r = XYp.next()
                            kb.op("act", lambda e, py=py, Y2=Y2: e.copy(Y2[0:BS, 0:BS], py[0:BS, 0:BS]), reads=[pyr], writes=[Y2r[0]])
                            if not last:
                                px, pxr = kb.ps()
                                kb.mm(px[0:BS, 0:BS], pxr, [(s["Y"], s["X"], [s["Xr"], s["Yr"]])])
                                X2, X2r = XYp.next()
                                kb.op("dve", lambda e, px=px, X2=X2: e.tensor_copy(out=X2[0:BS, 0:BS], in_=px[0:BS, 0:BS]), reads=[pxr], writes=[X2r[0]])
                                s["X"], s["Xr"] = X2[0:BS, 0:BS], X2r[0]
                            s["Y"], s["Yr"] = Y2[0:BS, 0:BS], Y2r[0]
                        for h in heads:
                            s = S[h]
                            pp_, ppr = kb.ps()
                            kb.mm(pp_[0:BS, 0:BS], ppr, [(s["Y"], s["Pm"][0:BS, 0:BS], [s["Yr"], s["Pr"]])])
                            kb.op("dve", lambda e, s=s, pp_=pp_: e.tensor_tensor(out=s["Pm"][0:BS, 0:BS], in0=s["Pm"][0:BS, 0:BS], in1=pp_[0:BS, 0:BS], op=ALU.add),
                                  reads=[ppr, s["Pr"]], writes=[s["Pr"]])
                    if "inv" in dbg:
                        continue
                    po = {}
                    for hi, h in enumerate(heads):
                        bt, br = kb.psum[6 + hi // 4]
                        po[h] = (bt[:, (hi % 4) * 128:(hi % 4 + 1) * 128], br)
                        Wt, Wr_ = WUp.next()
                        Ut, Ur_ = WUp.next()
                        S[h].update(W=Wt, Wr=Wr_[0], U=Ut, Ur=Ur_[0])
                    for ch in range(nch):
                        rows = slice(ch * CL, (ch + 1) * CL)
                        for h in heads:
                            s = S[h]
                            c, hf = h // 2, h % 2
                            ps_ = slice(hf * 64, hf * 64 + 64)
                            gam = F["eg"][0][ps_, c, ch * CL + CL - 1:ch * CL + CL]
                            hg, hgr = hgp.next()
                            kb.op("pool", lambda e, hg=hg, gam=gam, c=c, ps_=ps_: e.tensor_scalar(out=hg[ps_, :], in0=H[ps_, c, :], scalar1=gam, scalar2=None, op0=ALU.mult),
                                  reads=[Hres[q][h], F["eg"][1]], writes=[hgr[0]])
                            s["hg"], s["hgr"], s["gam"] = hg, hgr[0], gam
                            pw, pwr = kb.ps()
                            kb.mm(pw[0:BS, 0:64], pwr, [(FM[("kkt", hf)][0][:, c, 0:BS], H[:, c, :], [FM[("kkt", hf)][1], Hres[q][h]]),
                                                          (s["G"][0:BS, 2, 0:BS], vTk[0:BS, h, :], [s["Gr"], vTr[0]])])
                            kb.op("act", lambda e, pw=pw, s=s: e.copy(s["W"][rows, :], pw[rows, 0:64]), reads=[pwr], writes=[s["Wr"]])
                        for h in heads:
                            s = S[h]
                            pu, pur = kb.ps()
                            kb.mm(pu[0:BS, 0:64], pur, [(s["Pm"][0:BS, 0:BS], s["W"][0:BS, :], [s["Pr"], s["Wr"]])])
                            kb.op("dve", lambda e, pu=pu, s=s: e.tensor_copy(out=s["U"][rows, :], in_=pu[rows, 0:64]), reads=[pur], writes=[s["Ur"]])
                        for h in heads:
                            s = S[h]
                            c, hf = h // 2, h % 2
                            ps_ = slice(hf * 64, hf * 64 + 64)
                            pt, pr = po[h]
                            kb.mm(pt[0:64, ch * CL:(ch + 1) * CL], pr,
                                  [(H[:, c, :], FM[("rt", hf)][0][:, c, rows], [Hres[q][h], FM[("rt", hf)][1]]),
                                   (vTk[0:BS, h, :], s["G"][0:BS, 3, rows], [vTr[0], s["Gr"]]),
                                   (s["U"][0:BS, :], s["G"][0:BS, 4, rows], [s["Ur"], s["Gr"]])])
                            ph, phr = kb.ps()
                            kT, kTr = TK[("kTk", ch)]
                            nT, nTr = TK[("naTk", ch)]
                            kb.mm(ph[:, 0:64], phr, [(kT[0:BS, c, :], vTk[0:BS, h, :], [kTr[0], vTr[0]]),
                                                      (nT[0:BS, c, :], s["U"][0:BS, :], [nTr[0], s["Ur"]])])
                            kb.op("dve", lambda e, ph=ph, s=s, c=c, ps_=ps_: e.scalar_tensor_tensor(
                                out=H[ps_, c, :], in0=ph[ps_, 0:64], scalar=s["gam"], in1=s["hg"][ps_, :], op0=ALU.mult, op1=ALU.add),
                                reads=[phr, s["hgr"], F["eg"][1]], writes=[Hres[q][h]])
                    for h in heads:
                        pt, pr = po[h]
                        kb.op("act", lambda e, pt=pt, h=h: e.copy(ost[:, h, 0:BS], pt[0:64, 0:BS]), reads=[pr], writes=[ostr[0]])
                if not dbg:
                    kb.dma("pool", self.wk_d["o64"][:, :, col0:col0 + BS], ost[:, :, 0:BS], reads=ostr, writes=[self.wk_res["o64"]])
            for q in range(NSEQ):
                kb.dma("pool", self.o_wkv[l, q], Hst[q][:].rearrange("p c i -> p (c i)"), reads=Hres[q])
            kb.psrot = list(range(8))

    def rwkv_out(self, l):
        cfg, kb = self.cfg, self.kb
        W = 256
        with self.scope():
            rW = Res("r3w")
            wo = kb.sb([HD, NH, D], BF16, "wo64")
            self.load_w(wo, rW, self.w_o64[l], 4)
            ln = kb.sb([HD, 2, NH], F32, "ln")
            kb.dma("sp", ln[:].rearrange("p a h -> p (a h)"), self.lnx64[:, l * 2 * NH:(l + 1) * 2 * NH], writes=[rW])
            ones = kb.sb([HD, HD], F32, "ones64")
            kb.op("pool", lambda e: e.memset(ones[:], 1.0), writes=[rW])
            geps = kb.sb([HD, 1], F32, "geps")
            kb.op("pool", lambda e: e.memset(geps[:], GN_EPS), writes=[rW])
            pools = {"rstd": Rot(kb, 2, [P, W], F32, "rstd"), "sq": Rot(kb, 1, [P, KC, W], BF16, "sq", nres=KC),
                     "tmp": Rot(kb, 3, [P, W], F32, "tmp")}
            xpool = Rot(kb, 2, [P, KC, W], F32, "xt", nres=KC)
            ypool = Rot(kb, 1, [P, KC, W], F32, "y", nres=KC)
            inp_ = {n: Rot(kb, 1, [HD, NH, W], F32, "i" + n) for n in ("o64", "g64", "bn64")}
            ogp = Rot(kb, 2, [HD, NH, W], BF16, "og")
            t64 = Rot(kb, 24, [HD, W], F32, "t64")
            for tile in self.tiles256:
                c0, w, segs, ti = tile
                I = {}
                for n in inp_:
                    t, r = inp_[n].next()
                    kb.dma("sp", t[:, :, 0:w], self.wk_d[n][:, :, c0:c0 + w], reads=[self.wk_res[n]], writes=r)
                    I[n] = (t, r[0])
                og, ogr = ogp.next()

                def head_gen(h, I=I, og=og, ogr=ogr, w=w):
                    o_ = I["o64"][0][:, h, 0:w]
                    osq, osqr = t64.next()
                    kb.op("act", lambda e: e.activation(osq[:, 0:w], o_, AF.Square), reads=[I["o64"][1]], writes=[osqr[0]])
                    yield
                    p1, r1 = kb.ps()
                    kb.mm(p1[0:64, 0:w], r1, [(ones[:], o_, [rW, I["o64"][1]])])
                    yield
                    p2, r2 = kb.ps()
                    kb.mm(p2[0:64, 0:w], r2, [(ones[:], osq[:, 0:w], [rW, osqr[0]])])
                    yield
                    nm, nmr = t64.next()
                    kb.op("act", lambda e: e.mul(nm[:, 0:w], p1[0:64, 0:w], -1.0 / HD), reads=[r1], writes=[nmr[0]])
                    yield
                    msq, msqr = t64.next()
                    kb.op("pool", lambda e: e.tensor_tensor(out=msq[:, 0:w], in0=nm[:, 0:w], in1=nm[:, 0:w], op=ALU.mult), reads=[nmr[0]], writes=[msqr[0]])
                    yield
                    var, varr = t64.next()
                    kb.op("dve", lambda e: e.scalar_tensor_tensor(out=var[:, 0:w], in0=p2[0:64, 0:w], scalar=1.0 / HD, in1=msq[:, 0:w], op0=ALU.mult, op1=ALU.subtract),
                          reads=[r2, msqr[0]], writes=[varr[0]])
                    yield
                    kb.op("act", lambda e: e.activation(var[:, 0:w], var[:, 0:w], AF.Sqrt, bias=geps[:, 0:1], scale=1.0), reads=[varr[0], rW], writes=[varr[0]])
                    yield
                    kb.op("dve", lambda e: e.reciprocal(var[:, 0:w], var[:, 0:w]), reads=[varr[0]], writes=[varr[0]])
                    yield
                    d, dr = t64.next()
                    kb.op("pool", lambda e: e.tensor_tensor(out=d[:, 0:w], in0=o_, in1=nm[:, 0:w], op=ALU.add), reads=[I["o64"][1], nmr[0]], writes=[dr[0]])
                    yield
                    kb.op("pool", lambda e: e.tensor_tensor(out=d[:, 0:w], in0=d[:, 0:w], in1=var[:, 0:w], op=ALU.mult), reads=[dr[0], varr[0]], writes=[dr[0]])
                    yield
                    kb.op("act", lambda e: e.activation(d[:, 0:w], d[:, 0:w], AF.Identity, bias=ln[:, 1, h:h + 1], scale=ln[:, 0, h:h + 1]),
                          reads=[dr[0], rW], writes=[dr[0]])
                    yield
                    kb.op("pool", lambda e: e.tensor_tensor(out=d[:, 0:w], in0=d[:, 0:w], in1=I["bn64"][0][:, h, 0:w], op=ALU.add),
                          reads=[dr[0], I["bn64"][1]], writes=[dr[0]])
                    yield
                    kb.op("dve", lambda e: e.tensor_tensor(out=og[:, h, 0:w], in0=d[:, 0:w], in1=I["g64"][0][:, h, 0:w], op=ALU.mult),
                          reads=[dr[0], I["g64"][1]], writes=[ogr[0]])
                    yield
                interleave([head_gen(h) for h in range(NH)], 4)
                xt, xr = self.load_x(xpool, tile)
                yt, yr = ypool.next()
                for oc in range(KC):
                    py, ry = kb.ps()
                    kb.mm(py[:, 0:w], ry, [(wo[:, h, oc * 128:(oc + 1) * 128], og[:, h, 0:w], [rW, ogr[0]]) for h in range(NH)])
                    kb.op("act", lambda e, oc=oc, py=py: e.copy(yt[:, oc, 0:w], py[:, 0:w]), reads=[ry], writes=[yr[oc]])
                self.post_residual(l, 1, tile, lambda c: yt[:, c, 0:w], lambda c: yr[c], xt, xr, pools)

    def rwkv_layer(self, l):
        self.rwkv_proj(l)
        self.rwkv_wkv(l)
        self.rwkv_out(l)


def _kmaj(w):
    Kd, N = w.shape
    return np.ascontiguousarray(w.reshape(Kd // P, P, N).transpose(1, 0, 2))


def _vec(v):
    sh = v.shape[:-1]
    return np.ascontiguousarray(np.moveaxis(v.reshape(sh + (KC, P)), -1, 0))


def _v64(v):
    sh = v.shape[:-1]
    return np.ascontiguousarray(np.moveaxis(v.reshape(sh + (NH, HD)), -1, 0))


def pack_shared(cfg, I):
    L, na = cfg.depth, cfg.n_a
    f = lambda a: np.ascontiguousarray(a, dtype=np.float32)
    out = {
        "w_mod": np.stack([_kmaj(f(I["w_mod"][l])) for l in range(L)]),
        "b_mod": f(I["b_mod"]).reshape(L, 72, P).transpose(2, 0, 1).reshape(P, -1).copy(),
        "g_norm": f(I["g_norm"]).reshape(L * 6, KC, P).transpose(2, 0, 1).reshape(P, -1).copy(),
        "w_in": np.stack([np.stack([_kmaj(f(I["w_ffn_in"][l, s])) for s in range(2)]) for l in range(L)]),
        "w_out": np.stack([np.stack([_kmaj(f(I["w_ffn_out"][l, s])) for s in range(2)]) for l in range(L)]),
        "mu": _vec(f(I["mu_a"])).reshape(P, -1).copy(),
        "rvec": _vec(np.stack([f(I["w0_a"]), f(I["a0_a"]), f(I["kk_a"]), f(I["ka_a"]),
                               f(I["rk_a"]).reshape(na, D)], axis=1)).reshape(P, -1).copy(),
        "v0_64": _v64(f(I["v0_a"])).reshape(HD, -1).copy(),
        "lnx64": _v64(np.stack([f(I["lnx_w_a"]), f(I["lnx_b_a"])], axis=1)).reshape(HD, -1).copy(),
        "w_rkv": np.stack([np.stack([_kmaj(f(I["w_rkv_a"][l, i])) for i in range(3)]) for l in range(na)]),
        "w_o64": np.stack([f(I["w_o_a"][l]).reshape(NH, HD, D).transpose(1, 0, 2).copy() for l in range(na)]),
        "w1": np.stack([_kmaj(f(I["w1_a"][l])) for l in range(na)]),
        "a1": np.stack([_kmaj(f(I["a1_a"][l])) for l in range(na)]),
        "v1": np.stack([_kmaj(f(I["v1_a"][l])) for l in range(na - 1)]),
        "g1": np.stack([_kmaj(f(I["g1_a"][l])) for l in range(na)]),
        "w2": f(I["w2_a"])[:, :, None, :].copy(),
        "a2": f(I["a2_a"])[:, :, None, :].copy(),
        "v2": f(I["v2_a"])[:, :, None, :].copy(),
        "g2": f(I["g2_a"])[:, :, None, :].copy(),
    }
    return out


def pack_core(cfg, I, core):
    na = cfg.n_a
    f = lambda a: np.ascontiguousarray(a, dtype=np.float32)
    b = core % I["x_prompt"].shape[0]
    sl = slice(core * NS, (core + 1) * NS)
    xall = np.zeros((cfg.ntok, D), np.float32)
    xall[:cfg.seq] = I["x_prompt"][b]
    for s in range(NS):
        xall[cfg.seq + s * SLOT: cfg.seq + s * SLOT + DEC] = I["x_sample"][core * NS + s]
    call = np.concatenate([f(I["c_prompt"][b:b + 1]), f(I["c_sample"][sl])], axis=0)
    out = {
        "xin": np.ascontiguousarray(xall.reshape(cfg.ntok, KC, P).transpose(2, 1, 0)),
        "cT": np.ascontiguousarray(call.reshape(NSEQ, KC, P).transpose(2, 1, 0)).reshape(P, -1),
        "sh0": _vec(f(I["state_shift"][:, sl])).reshape(P, -1).copy(),
        "hst0": np.ascontiguousarray(f(I["state_wkv"][:, sl]).reshape(na, NS, KC, 2, HD, HD).transpose(0, 1, 3, 5, 2, 4)).reshape(na, NS, P, KC * HD),
    }
    return out


def unpack_wkv(o_wkv):
    na = o_wkv.shape[0]
    a = o_wkv.reshape(na, NSEQ, 2, HD, KC, HD)
    return np.ascontiguousarray(a.transpose(0, 1, 4, 2, 5, 3)).reshape(na, NSEQ, NH, HD, HD)


def unpack_vec(o):
    return np.ascontiguousarray(np.moveaxis(o, 0, -1)).reshape(o.shape[1], D)


def _fox_setup(self):
    cfg, kb = self.cfg, self.kb
    NT = cfg.ntok
    nb = cfg.depth - cfg.n_a
    self.g_kv = self.inp("g_kv", [P, KC])
    self.w_kv = self.inp("w_kv", [P, KC, 2 * D])
    self.w_f = self.inp("w_f", [P, KC, NH])
    self.b_f = self.inp("b_f", [P, NH])
    self.w_q = self.inp("w_q", [nb, P, KC, D])
    self.w_o = self.inp("w_o", [nb, P, KC, D])
    self.cache_k = self.inp("cache_k", [cfg.npool * PAGE, D])
    self.cache_v = self.inp("cache_v", [cfg.npool * PAGE, D])
    self.cache_lf = self.inp("cache_lf", [cfg.npool * PAGE, NH])
    self.ptab = self.inp("ptab", [1, NS * cfg.npages], I32)
    self.o_k = self.outp("o_k", [P, KC, NT])
    self.o_v = self.outp("o_v", [NT, D])
    self.o_lf = self.outp("o_lf", [NT, NH])
    self.kbf = kb.dram("kbf", [P, KC, NT], BF16)
    self.vaug = kb.dram("vaug", [NT, NH * P], BF16)
    self.ckd = kb.dram("ckd", [NT, NH], F32)
    self.bnp = kb.dram("bnp", [NS, P, cfg.npages * NH], F32)
    self.r_kv = Res("kv")
    self.pidx = kb.sb([P, NS * cfg.npages], I32, "pidx")
    self.r_pidx = Res("pidx")
    with self.scope():
        iot = kb.sb([P, 1], I32, "iota")
        iof = kb.sb([P, 1], F32, "iotaf")
        idf = kb.sb([P, NS * cfg.npages], F32, "idf")
        kb.dma("sp", self.pidx[:], self.ptab[0:1, :].partition_broadcast(P), writes=[self.r_pidx])
        kb.op("pool", lambda e: e.iota(iot[:], pattern=[[0, 1]], base=0, channel_multiplier=1), writes=[self.r_pidx])
        kb.op("dve", lambda e: e.tensor_copy(out=iof[:], in_=iot[:]), reads=[self.r_pidx], writes=[self.r_pidx])
        kb.op("dve", lambda e: e.tensor_copy(out=idf[:], in_=self.pidx[:]), reads=[self.r_pidx], writes=[self.r_pidx])
        kb.op("dve", lambda e: e.tensor_scalar(out=idf[:], in0=idf[:], scalar1=float(PAGE), scalar2=iof[:, 0:1], op0=ALU.mult, op1=ALU.add),
              reads=[self.r_pidx], writes=[self.r_pidx])
        kb.op("dve", lambda e: e.tensor_copy(out=self.pidx[:], in_=idf[:]), reads=[self.r_pidx], writes=[self.r_pidx])


def _tri_consts(self, rC):
    kb = self.kb
    ones = kb.sb([P, P], F32, "ones32")
    ltri = kb.sb([P, P], F32, "ltri")
    e127 = kb.sb([P, P], F32, "e127")
    ltri32 = kb.sb([P, P], F32, "ltri32")
    kb.op("pool", lambda e: e.memset(ones[:], 1.0), writes=[rC])
    kb.op("pool", lambda e: e.affine_select(out=ltri[:], in_=ones[:], pattern=[[1, P]], compare_op=ALU.is_ge, fill=0.0, base=0, channel_multiplier=-1),
          reads=[rC], writes=[rC])
    kb.op("pool", lambda e: e.affine_select(out=e127[:], in_=ones[:], pattern=[[0, P]], compare_op=ALU.is_equal, fill=0.0, base=-127, channel_multiplier=1),
          reads=[rC], writes=[rC])
    kb.op("pool", lambda e: e.tensor_copy(out=ltri32[:], in_=ltri[:]), reads=[rC], writes=[rC])
    for a in range(4):
        for b in range(4):
            if a != b:
                kb.op("pool", lambda e, a=a, b=b: e.memset(ltri32[a * 32:(a + 1) * 32, b * 32:(b + 1) * 32], 0.0), reads=[rC], writes=[rC])
    return ones, ltri, e127, ltri32


def _shared_kv(self):
    cfg, kb = self.cfg, self.kb
    with self.scope():
        rW = Res("kvw")
        wkv = kb.sb([P, KC, 2 * D], BF16, "wkv")
        self.load_w(wkv, rW, self.w_kv, 4)
        wf = kb.sb([P, KC, NH], BF16, "wf")
        self.load_w(wf, rW, self.w_f, 1)
        gk = kb.sb([P, KC], F32, "gk")
        bf = kb.sb([P, NH], F32, "bf")
        kb.dma("sp", gk[:], self.g_kv[:, :], writes=[rW])
        kb.dma("sp", bf[:], self.b_f[:, :], writes=[rW])
        ones, ltri, e127, ltri32 = _tri_consts(self, rW)
        pools = self.std_pools()
        xpool = Rot(kb, 2, [P, KC, 512], F32, "xt", nres=KC)
        spool = Rot(kb, 2, [P, KC, 512], BF16, "st", nres=KC)
        kfp = Rot(kb, 2, [P, 512], F32, "kf")
        kbp = Rot(kb, 2, [P, 512], BF16, "kb")
        vfp = Rot(kb, 2, [P, D], F32, "vf")
        vap = Rot(kb, 2, [P, NH, P], BF16, "va")
        for t, r in vap.tiles:
            kb.op("pool", lambda e, t=t: e.memset(t[:], 0.0), writes=r)
            v4 = t[:].rearrange("p (a two) c -> p a two c", two=2)
            kb.op("pool", lambda e, v4=v4: e.memset(v4[:, :, 0, 64:65], 1.0), reads=r, writes=r)
            kb.op("pool", lambda e, v4=v4: e.memset(v4[:, :, 1, 0:1], 1.0), reads=r, writes=r)
        lfp = Rot(kb, 3, [P, NH], F32, "lf")
        ckp = Rot(kb, 3, [P, NH], F32, "ck")
        ck_prev = None
        for tile in cfg.tiles:
            c0, w, segs, ti = tile
            is_samp = c0 >= cfg.seq
            xt, xr = self.load_x(xpool, tile)
            st, sr = spool.next()
            self.modulate(0, 0, tile, xt, xr, lambda c, s0, sw: st[:, c, s0:s0 + sw], sr, pools, ident=lambda c: gk[:, c:c + 1])
            dbg = cfg.debug or ""
            if "kcut1" in dbg:
                continue
            for oc in range(KC):
                pk, rk = kb.ps()
                kb.mm(pk[:, 0:w], rk, [(wkv[:, kc, oc * 128:(oc + 1) * 128], st[:, kc, 0:w], [rW, sr[kc]]) for kc in range(KC)])
                kf, kfr = kfp.next()
                kb.op("act", lambda e, pk=pk, kf=kf: e.copy(kf[:, 0:w], pk[:, 0:w]), reads=[rk], writes=[kfr[0]])
                kbt, kbr = kbp.next()
                kb.op("dve", lambda e, kf=kf, kbt=kbt: e.tensor_copy(out=kbt[:, 0:w], in_=kf[:, 0:w]), reads=[kfr[0]], writes=[kbr[0]])
                kb.dma("pool", self.o_k[:, oc, c0:c0 + w], kf[:, 0:w], reads=[kfr[0]])
                kb.dma("pool", self.kbf[:, oc, c0:c0 + w], kbt[:, 0:w], reads=[kbr[0]], writes=[self.r_kv])
            if "kcut2" in dbg:
                continue
            for tb in range(w // 128):
                t0 = c0 + tb * 128
                ts_ = slice(tb * 128, (tb + 1) * 128)
                vf, vfr = vfp.next()
                va, var_ = vap.next()
                va4 = va[:].rearrange("p (a two) c -> p a two c", two=2)
                for half in range(2):
                    pv, rv_ = kb.ps()
                    kb.mm(pv[:, :], rv_, [(st[:, kc, ts_], wkv[:, kc, D + half * 512:D + (half + 1) * 512], [rW, sr[kc]]) for kc in range(KC)])
                    kb.op("act", lambda e, pv=pv, half=half: e.copy(vf[:, half * 512:(half + 1) * 512], pv[:, :]), reads=[rv_], writes=[vfr[0]])
                    pv4 = vf[:, half * 512:(half + 1) * 512].rearrange("p (a two c) -> p a two c", two=2, c=64)
                    kb.op("dve", lambda e, pv4=pv4, half=half: e.tensor_copy(out=va4[:, half * 4:half * 4 + 4, 0, 0:64], in_=pv4[:, :, 0, :]),
                          reads=[vfr[0]], writes=[var_[0]])
                    kb.op("pool", lambda e, pv4=pv4, half=half: e.tensor_copy(out=va4[:, half * 4:half * 4 + 4, 1, 64:128], in_=pv4[:, :, 1, :]),
                          reads=[vfr[0]], writes=[var_[0]])
                kb.dma("pool", self.o_v[t0:t0 + 128, :], vf[:], reads=[vfr[0]])
                kb.dma("pool", self.vaug[t0:t0 + 128, :], va[:].rearrange("p a c -> p (a c)"), reads=[var_[0]], writes=[self.r_kv])
                if "kcut3" in dbg:
                    continue
                pl, rl = kb.ps()
                kb.mm(pl[:, 0:NH], rl, [(st[:, kc, ts_], wf[:, kc, :], [rW, sr[kc]]) for kc in range(KC)])
                lf, lfr = lfp.next()
                kb.op("dve", lambda e, pl=pl, lf=lf: e.tensor_tensor(out=lf[:], in0=pl[:, 0:NH], in1=bf[:], op=ALU.add), reads=[rl, rW], writes=[lfr[0]])
                kb.op("act", lambda e, lf=lf: e.activation(lf[:], lf[:], AF.Exp, scale=-1.0), reads=[lfr[0]], writes=[lfr[0]])
                kb.op("act", lambda e, lf=lf: e.activation(lf[:], lf[:], AF.Ln, bias=1.0, scale=1.0), reads=[lfr[0]], writes=[lfr[0]])
                kb.op("dve", lambda e, lf=lf: e.tensor_scalar(out=lf[:], in0=lf[:], scalar1=-1.0, scalar2=None, op0=ALU.mult), reads=[lfr[0]], writes=[lfr[0]])
                kb.dma("pool", self.o_lf[t0:t0 + 128, :], lf[:], reads=[lfr[0]])
                if "kcut4" in dbg:
                    continue
                pc, rc = kb.ps()
                ck, ckr = ckp.next()
                if not is_samp:
                    items = [(ltri[:], lf[:], [rW, lfr[0]])]
                    if ck_prev is not None:
                        items.append((e127[:], ck_prev[0][:], [rW, ck_prev[1]]))
                    kb.mm(pc[:, 0:NH], rc, items)
                    kb.op("act", lambda e, pc=pc, ck=ck: e.copy(ck[:], pc[:, 0:NH]), reads=[rc], writes=[ckr[0]])
                    ck_prev = (ck, ckr[0])
                else:
                    kb.mm(pc[:, 0:NH], rc, [(ltri32[:], lf[:], [rW, lfr[0]])])
                    kb.op("act", lambda e, pc=pc, ck=ck: e.mul(ck[:], pc[:, 0:NH], -1.0), reads=[rc], writes=[ckr[0]])
                kb.dma("pool", self.ckd[t0:t0 + 128, :], ck[:], reads=[ckr[0]], writes=[self.r_kv])
        if "nopast" in (cfg.debug or ""):
            return
        npg = cfg.npages
        lpp = Rot(kb, 1, [P, npg, NH], F32, "lpast")
        cpp = Rot(kb, 2, [P, npg, NH], F32, "cpast")
        for s in range(NS):
            lp, lpr = lpp.next()
            for pg in range(npg):
                col = s * npg + pg
                kb.dma("pool", reads=[self.r_pidx], writes=[lpr[0]], fn=lambda e, pg=pg, col=col: e.indirect_dma_start(
                    out=lp[:, pg, :], out_offset=None, in_=self.cache_lf[:, :],
                    in_offset=bass.IndirectOffsetOnAxis(ap=self.pidx[:, col:col + 1], axis=0)))
            cp, cpr = cpp.next()
            for pg in range(npg):
                pc, rc = kb.ps()
                items = [(ltri[:], lp[:, pg, :], [rW, lpr[0]])]
                if pg > 0:
                    items.append((e127[:], cp[:, pg - 1, :], [rW, cpr[0]]))
                kb.mm(pc[:, 0:NH], rc, items)
                kb.op("act", lambda e, pc=pc, pg=pg: e.copy(cp[:, pg, :], pc[:, 0:NH]), reads=[rc], writes=[cpr[0]])
            pt_, rt_ = kb.ps()
            kb.mm(pt_[:, 0:NH], rt_, [(e127[:], cp[:, npg - 1, :], [rW, cpr[0]])])
            tot, totr = ckp.next()
            kb.op("act", lambda e, pt_=pt_, tot=tot: e.copy(tot[:], pt_[:, 0:NH]), reads=[rt_], writes=[totr[0]])
            kb.op("dve", lambda e, tot=tot: e.tensor_tensor(out=cp[:], in0=tot[:].unsqueeze(1).to_broadcast([P, npg, NH]), in1=cp[:], op=ALU.subtract),
                  reads=[totr[0], cpr[0]], writes=[cpr[0]])
            kb.dma("pool", self.bnp[s], cp[:].rearrange("p a h -> p (a h)"), reads=[cpr[0]], writes=[self.r_kv])


Prog.fox_setup = _fox_setup
Prog.shared_kv = _shared_kv


def _fox_layer(self, j):
    cfg, kb = self.cfg, self.kb
    l = cfg.n_a + j
    npg = cfg.npages
    with self.scope():
        rW = Res("aw")
        wq = kb.sb([P, KC, D], BF16, "wq")
        wo = kb.sb([P, KC, D], BF16, "wo")
        self.load_w(wq, rW, self.w_q[j], 2)
        self.load_w(wo, rW, self.w_o[j], 2)
        ones = kb.sb([P, P], F32, "ones32")
        onesb = kb.sb([P, 512], BF16, "onesb")
        sel = [kb.sb([P, P], F32, f"sel{i}") for i in range(2)]
        identb = kb.sb([P, P], BF16, "identb")
        kb.op("pool", lambda e: e.memset(ones[:], 1.0), writes=[rW])
        kb.op("pool", lambda e: e.memset(onesb[:], 1.0), writes=[rW])
        kb.op("pool", lambda e: e.affine_select(out=sel[0][:], in_=ones[:], pattern=[[0, P]], compare_op=ALU.is_equal, fill=0.0, base=-64, channel_multiplier=1),
              reads=[rW], writes=[rW])
        kb.op("pool", lambda e: e.affine_select(out=sel[1][:], in_=ones[:], pattern=[[0, P]], compare_op=ALU.is_equal, fill=0.0, base=0, channel_multiplier=1),
              reads=[rW], writes=[rW])
        kb.op("pool", lambda e: e.affine_select(out=identb[:], in_=onesb[:, 0:P], pattern=[[1, P]], compare_op=ALU.is_equal, fill=0.0, base=0, channel_multiplier=-1),
              reads=[rW], writes=[rW])
        dmask = kb.sb([P, 4, 512], BF16, "dmask")
        for dd in range(4):
            kb.op("pool", lambda e, dd=dd: e.affine_select(out=dmask[:, dd, :], in_=onesb[:], pattern=[[1, 512]], compare_op=ALU.is_ge, fill=0.0,
                                                           base=-dd * 128, channel_multiplier=-1), reads=[rW], writes=[rW])
        mask8 = kb.sb([DEC, DEC], BF16, "mask8")
        kb.op("pool", lambda e: e.affine_select(out=mask8[:], in_=onesb[0:DEC, 0:DEC], pattern=[[1, DEC]], compare_op=ALU.is_ge, fill=0.0,
                                                base=0, channel_multiplier=-1), reads=[rW], writes=[rW])
        pools = self.std_pools()
        xpool = Rot(kb, 1, [P, KC, 512], F32, "xt", nres=KC)
        hpool = Rot(kb, 1, [P, KC, 512], BF16, "ht", nres=KC)
        ypool = Rot(kb, 1, [P, KC, 512], F32, "y", nres=KC)
        qtm = kb.sb([P, NH, 512], BF16, "qtm")
        rq = Res("qtm")
        kb.op("pool", lambda e: e.memset(qtm[:], 0.0), writes=[rq])
        attnT = kb.sb([P, KC, 512], BF16, "attnT")
        ra = Res("attnT")
        kb.op("pool", lambda e: e.memset(attnT[:], 0.0), writes=[ra])
        rzp = [Rot(kb, 2, [P, 512], F32, f"rz{i}") for i in range(2)]
        for rp in rzp:
            for t, r in rp.tiles:
                kb.op("pool", lambda e, t=t: e.memset(t[:], 0.0), writes=r)
        osbp = Rot(kb, 2, [P, 512], F32, "osb")
        ktp = Rot(kb, 2, [P, 2, 512], BF16, "kt")
        vap = Rot(kb, 2, [P, 4, 512], BF16, "va")
        ckbp = Rot(kb, 2, [P, 4, NH], F32, "ckb")
        bnp_ = Rot(kb, 2, [P, 4, NH], F32, "bn")
        crp = Rot(kb, 2, [P, NH], F32, "cref")
        ptp = Rot(kb, 4, [P, 512], BF16, "pt")
        kb.psrot = [0, 1, 2, 3]

        def qproj(ht, hr, w):
            for oc in range(KC):
                pq, rq_ = kb.ps()
                kb.mm(pq[:, 0:w], rq_, [(wq[:, kc, oc * 128:(oc + 1) * 128], ht[:, kc, 0:w], [rW, hr[kc]]) for kc in range(KC)])
                kb.op("act", lambda e, pq=pq, oc=oc: e.mul(qtm[0:64, 2 * oc, 0:w], pq[0:64, 0:w], HD ** -0.5), reads=[rq_], writes=[rq])
                kb.op("act", lambda e, pq=pq, oc=oc: e.mul(qtm[64:128, 2 * oc + 1, 0:w], pq[64:128, 0:w], HD ** -0.5), reads=[rq_], writes=[rq])

        def epilogue(h, acc_ap, acc_res, w, out_cols):
            par = h % 2
            lr = 64 if par == 0 else 0
            rows = slice(0, 64) if par == 0 else slice(64, 128)
            rz, rzr = rzp[par].next()
            osb, osr = osbp.next()
            kb.op("act", lambda e: e.copy(osb[:, 0:w], acc_ap[:, 0:w]), reads=[acc_res], writes=[osr[0]])
            kb.op("dve", lambda e: e.reciprocal(rz[lr:lr + 1, 0:w], osb[lr:lr + 1, 0:w]), reads=[osr[0]], writes=[rzr[0]])
            pb_, pbr = kb.ps()
            kb.mm(pb_[:, 0:w], pbr, [(sel[par][:], rz[:, 0:w], [rW, rzr[0]])])
            kb.op("dve", lambda e: e.tensor_tensor(out=attnT[rows, h // 2, out_cols], in0=osb[rows, 0:w], in1=pb_[rows, 0:w], op=ALU.mult),
                  reads=[osr[0], pbr], writes=[ra])

        def out_proj(tile, xt, xr):
            c0, w, segs, ti = tile
            yt, yr = ypool.next()
            for oc in range(KC):
                py, ry = kb.ps()
                kb.mm(py[:, 0:w], ry, [(wo[:, kc, oc * 128:(oc + 1) * 128], attnT[:, kc, 0:w], [rW, ra]) for kc in range(KC)])
                kb.op("act", lambda e, oc=oc, py=py: e.copy(yt[:, oc, 0:w], py[:, 0:w]), reads=[ry], writes=[yr[oc]])
            self.post_residual(l, 1, tile, lambda c: yt[:, c, 0:w], lambda c: yr[c], xt, xr, pools)

        for qi, tile in enumerate(cfg.tiles[:-1]):
            c0, w, segs, ti = tile
            xt, xr = self.load_x(xpool, tile)
            ht, hr = hpool.next()
            self.modulate(l, 1, tile, xt, xr, lambda c, s0, sw: ht[:, c, s0:s0 + sw], hr, pools)
            qproj(ht, hr, w)
            cref, crr = crp.next()
            kb.dma("sp", cref[:], self.ckd[c0:c0 + 1, :].partition_broadcast(P), reads=[self.r_kv], writes=[crr[0]])
            nkb = 4 * (qi + 1)
            for hg in range(4):
                accs = [(kb.psum[4 + hi][0], kb.psum[4 + hi][1]) for hi in range(4)]
                for kg in range(qi + 1):
                    kt, ktr = ktp.next()
                    kb.dma("sp", kt[:], self.kbf[:, 2 * hg:2 * hg + 2, kg * 512:(kg + 1) * 512], reads=[self.r_kv], writes=[ktr[0]])
                    va, var_ = vap.next()
                    kb.dma("sp", va[:], self.vaug[kg * 512:(kg + 1) * 512, hg * 512:(hg + 1) * 512].rearrange("(b p) c -> p b c", p=P),
                           reads=[self.r_kv], writes=[var_[0]])
                    ckb, ckr = ckbp.next()
                    kb.dma("sp", ckb[:], self.ckd[kg * 512:(kg + 1) * 512, :].rearrange("(b p) h -> p b h", p=P), reads=[self.r_kv], writes=[ckr[0]])
                    bn, bnr = bnp_.next()
                    kb.op("dve", lambda e: e.tensor_tensor(out=bn[:], in0=cref[:].unsqueeze(1).to_broadcast([P, 4, NH]), in1=ckb[:], op=ALU.subtract),
                          reads=[crr[0], ckr[0]], writes=[bnr[0]])
                    for b in range(4):
                        kbi = kg * 4 + b
                        for hi in range(4):
                            h = 4 * hg + hi
                            ps_, pr_ = kb.ps()
                            kb.mm(ps_[:, 0:w], pr_, [(kt[:, hi // 2, b * 128:(b + 1) * 128], qtm[:, h, 0:w], [ktr[0], rq])])
                            pt, ptr = ptp.next()
                            kb.op("act", lambda e: e.activation(pt[:, 0:w], ps_[:, 0:w], AF.Exp, bias=bn[:, b, h:h + 1], scale=1.0),
                                  reads=[pr_, bnr[0]], writes=[ptr[0]])
                            if kbi >= 4 * qi:
                                dd = kbi - 4 * qi
                                kb.op("pool", lambda e: e.tensor_tensor(out=pt[:, 0:w], in0=pt[:, 0:w], in1=dmask[:, dd, 0:w], op=ALU.mult),
                                      reads=[ptr[0], rW], writes=[ptr[0]])
                            kb.mm(accs[hi][0][:, 0:w], accs[hi][1], [(va[:, b, hi * 128:(hi + 1) * 128], pt[:, 0:w], [var_[0], ptr[0]])],
                                  start=(kbi == 0), stop=(kbi == nkb - 1))
                for hi in range(4):
                    epilogue(4 * hg + hi, accs[hi][0], accs[hi][1], w, slice(0, w))
            out_proj(tile, xt, xr)

        tile = cfg.tiles[-1]
        c0, w, segs, ti = tile
        xt, xr = self.load_x(xpool, tile)
        ht, hr = hpool.next()
        self.modulate(l, 1, tile, xt, xr, lambda c, s0, sw: ht[:, c, s0:s0 + sw], hr, pools)
        qproj(ht, hr, w)
        kpp = Rot(kb, 2, [P, D], F32, "kpage")
        vpp = Rot(kb, 2, [P, D], F32, "vpage")
        k16p = Rot(kb, 2, [P, D], BF16, "k16")
        kTp = Rot(kb, 2, [P, KC, P], BF16, "kpT")
        vAp = Rot(kb, 2, [P, NH, P], BF16, "vaP")
        for t, r in vAp.tiles:
            kb.op("pool", lambda e, t=t: e.memset(t[:], 0.0), writes=r)
            v4 = t[:].rearrange("p (a two) c -> p a two c", two=2)
            kb.op("pool", lambda e, v4=v4: e.memset(v4[:, :, 0, 64:65], 1.0), reads=r, writes=r)
            kb.op("pool", lambda e, v4=v4: e.memset(v4[:, :, 1, 0:1], 1.0), reads=r, writes=r)
        bpp = Rot(kb, 1, [P, npg, NH], F32, "bpast")
        bnew = Rot(kb, 2, [DEC, NH], F32, "bnew")
        sbp = Rot(kb, 2, [P, NH, DEC], F32, "sb")
        pTp = Rot(kb, 2, [P, NH, DEC], BF16, "pT")
        acc_t, acc_r = kb.psum[4]
        for s in range(NS):
            qc = slice(s * SLOT, s * SLOT + DEC)
            bp, bpr = bpp.next()
            kb.dma("sp", bp[:].rearrange("p a h -> p (a h)"), self.bnp[s], reads=[self.r_kv], writes=[bpr[0]])
            for pg in range(npg + 1):
                new = (pg == npg)
                kT, kTr = kTp.next()
                vA, vAr = vAp.next()
                if not new:
                    col = s * npg + pg
                    kp, kpr = kpp.next()
                    vp, vpr = vpp.next()
                    kb.dma("pool", reads=[self.r_pidx], writes=[kpr[0]], fn=lambda e: e.indirect_dma_start(
                        out=kp[:], out_offset=None, in_=self.cache_k[:, :], in_offset=bass.IndirectOffsetOnAxis(ap=self.pidx[:, col:col + 1], axis=0)))
                    kb.dma("pool", reads=[self.r_pidx], writes=[vpr[0]], fn=lambda e: e.indirect_dma_start(
                        out=vp[:], out_offset=None, in_=self.cache_v[:, :], in_offset=bass.IndirectOffsetOnAxis(ap=self.pidx[:, col:col + 1], axis=0)))
                    k16, k16r = k16p.next()
                    kb.op("dve", lambda e: e.tensor_copy(out=k16[:], in_=kp[:]), reads=[kpr[0]], writes=[k16r[0]])
                    ptb, ptbr = kb.ps()
                    ptb16 = ptb[:, :].bitcast(BF16)
                    for c in range(KC):
                        kb.mm(ptb16[:, c * 128:(c + 1) * 128], ptbr, [(k16[:, c * 128:(c + 1) * 128], identb[:], [k16r[0], rW])], transpose=True)
                    kb.op("act", lambda e: e.copy(kT[:].rearrange("p c k -> p (c k)"), ptb16[:, :]), reads=[ptbr], writes=[kTr[0]])
                    vA4 = vA[:].rearrange("p (a two) c -> p a two c", two=2)
                    vp4 = vp[:].rearrange("p (a two c) -> p a two c", two=2, c=64)
                    kb.op("dve", lambda e: e.tensor_copy(out=vA4[:, :, 0, 0:64], in_=vp4[:, :, 0, :]), reads=[vpr[0]], writes=[vAr[0]])
                    kb.op("act", lambda e: e.copy(vA4[:, :, 1, 64:128], vp4[:, :, 1, :]), reads=[vpr[0]], writes=[vAr[0]])
                    nk = P
                    bias_ap = bp[:, pg, :]
                    bias_res = bpr[0]
                else:
                    t0 = c0 + s * SLOT
                    kb.dma("sp", kT[:, :, 0:DEC], self.kbf[:, :, t0:t0 + DEC], reads=[self.r_kv], writes=[kTr[0]])
                    kb.dma("sp", vA[0:DEC].rearrange("p a c -> p (a c)"), self.vaug[t0:t0 + DEC, :], reads=[self.r_kv], writes=[vAr[0]])
                    bnw, bnwr = bnew.next()
                    kb.dma("sp", bnw[:], self.ckd[t0:t0 + DEC, :], reads=[self.r_kv], writes=[bnwr[0]])
                    nk = DEC
                    bias_ap = bnw[:, :]
                    bias_res = bnwr[0]
                ps_, pr_ = kb.ps()
                for h in range(NH):
                    kb.mm(ps_[0:nk, h * DEC:(h + 1) * DEC], pr_, [(kT[:, h // 2, 0:nk], qtm[:, h, qc], [kTr[0], rq])])
                sb, sbr = sbp.next()
                kb.op("dve", lambda e: e.tensor_tensor(out=sb[0:nk], in0=ps_[0:nk, 0:NH * DEC].rearrange("p (h q) -> p h q", q=DEC),
                                                       in1=bias_ap.unsqueeze(2).to_broadcast([nk, NH, DEC]), op=ALU.add),
                      reads=[pr_, bias_res], writes=[sbr[0]])
                pT, pTr = pTp.next()
                kb.op("act", lambda e: e.activation(pT[0:nk].rearrange("p h q -> p (h q)"), sb[0:nk].rearrange("p h q -> p (h q)"), AF.Exp),
                      reads=[sbr[0]], writes=[pTr[0]])
                if new:
                    kb.op("pool", lambda e: e.tensor_tensor(out=pT[0:nk], in0=pT[0:nk], in1=mask8[:].unsqueeze(1).to_broadcast([DEC, NH, DEC]), op=ALU.mult),
                          reads=[pTr[0], rW], writes=[pTr[0]])
                for h in range(NH):
                    kb.mm(acc_t[:, h * DEC:(h + 1) * DEC], acc_r, [(vA[0:nk, h, :], pT[0:nk, h, :], [vAr[0], pTr[0]])],
                          start=(pg == 0), stop=(pg == npg))
            for h in range(NH):
                epilogue(h, acc_t[:, h * DEC:(h + 1) * DEC], acc_r, DEC, qc)
        out_proj(tile, xt, xr)
        kb.psrot = list(range(8))


def _final_out(self):
    kb = self.kb
    self.o_y = self.outp("o_y", [P, KC, self.cfg.ntok])
    for tile in self.cfg.tiles:
        c0, w, segs, ti = tile
        kb.dma("sp", self.o_y[:, :, c0:c0 + w], self.xs[:, :, c0:c0 + w], reads=[self.xs_res[ti]])


Prog.fox_layer = _fox_layer
Prog.final_out = _final_out


def build_program(cfg, stop=None):
    pg = Prog(cfg)
    pg.setup()
    pg.setup_rwkv()
    steps = []
    for l in range(cfg.depth):
        if l == cfg.n_a:
            steps.append(("fox_setup", pg.fox_setup))
            steps.append(("kv", pg.shared_kv))
        steps.append((f"ffn{l}a", lambda l=l: pg.ffn(l, 0)))
        if l < cfg.n_a:
            steps.append((f"rwkv{l}", lambda l=l: pg.rwkv_layer(l)))
        else:
            steps.append((f"fox{l}", lambda l=l: pg.fox_layer(l - cfg.n_a)))
        steps.append((f"ffn{l}b", lambda l=l: pg.ffn(l, 2)))
    for i, (name, fn) in enumerate(steps):
        if stop is not None and not isinstance(stop, int):
            if name not in stop:
                continue
        elif stop is not None and i >= stop:
            break
        fn()
    pg.final_out()
    pg.kb.finish()
    return pg


def pack_fox(cfg, I, core):
    f = lambda a: np.ascontiguousarray(a, dtype=np.float32)
    nb = cfg.depth - cfg.n_a
    shared = {
        "g_kv": _vec(f(I["g_kv"])[None])[:, 0, :].copy(),
        "w_kv": _kmaj(f(I["w_kv"])),
        "w_f": _kmaj(f(I["w_f"])),
        "b_f": np.ascontiguousarray(np.broadcast_to(f(I["b_f"])[None, :], (P, NH))),
        "w_q": np.stack([_kmaj(f(I["w_q_b"][j])) for j in range(nb)]),
        "w_o": np.stack([_kmaj(f(I["w_o_b"][j])) for j in range(nb)]),
        "cache_k": f(I["cache_k"]).reshape(-1, D),
        "cache_v": f(I["cache_v"]).reshape(-1, D),
        "cache_lf": f(I["cache_logf"]).reshape(-1, NH),
    }
    return shared


_STOP = None
_DEBUG = None


def kernel(**I):
    n_cores = 8
    seq = I["x_prompt"].shape[1]
    npages = I["page_table"].shape[1]
    npool = I["cache_k"].shape[0]
    depth = I["w_mod"].shape[0]
    cfg = Cfg(seq=seq, npages=npages, npool=npool, depth=depth, debug=_DEBUG)
    pg = build_program(cfg, stop=_STOP)
    shared = pack_shared(cfg, I)
    shared.update(pack_fox(cfg, I, 0))
    in_maps = []
    for core in range(n_cores):
        d = dict(shared)
        d.update(pack_core(cfg, I, core))
        d["ptab"] = np.ascontiguousarray(I["page_table"][core * NS:(core + 1) * NS].reshape(1, -1).astype(np.int32))
        in_maps.append({k: d[k] for k in pg.inputs})
    res = run_bass_kernel_spmd(pg.nc, in_maps, core_ids=list(range(n_cores)))
    R = res.results
    if _STOP is not None:
        for r in R:
            for k, shp in (("o_k", (P, KC, cfg.ntok)), ("o_v", (cfg.ntok, D)), ("o_lf", (cfg.ntok, NH))):
                r.setdefault(k, np.zeros(shp, np.float32))
    B = I["x_prompt"].shape[0]
    DB = I["x_sample"].shape[0]
    na = cfg.n_a

    def tokmaj(a):
        return np.ascontiguousarray(a.transpose(2, 1, 0)).reshape(a.shape[2], D)

    def samp_rows(a):
        return np.stack([a[seq + s * SLOT: seq + s * SLOT + DEC] for s in range(NS)])
    y_p = np.stack([tokmaj(R[b]["o_y"][:, :, :seq]) for b in range(B)])
    y_s = np.concatenate([samp_rows(tokmaj(R[c]["o_y"])) for c in range(n_cores)])[:DB]
    wk = [unpack_wkv(R[c]["o_wkv"]) for c in range(n_cores)]
    wkv_p = np.stack([wk[b][:, 0] for b in range(B)], axis=1)
    wkv_s = np.concatenate([wk[c][:, 1:] for c in range(n_cores)], axis=1)[:, :DB]
    sh = [np.stack([unpack_vec(R[c]["o_shift"].reshape(P, na, NSEQ, KC)[:, l]) for l in range(na)]) for c in range(n_cores)]
    sh_p = np.stack([sh[b][:, 0] for b in range(B)], axis=1)
    sh_s = np.concatenate([sh[c][:, 1:] for c in range(n_cores)], axis=1)[:, :DB]
    k_all = [tokmaj(R[c]["o_k"]) for c in range(n_cores)]
    k_p = np.stack([k_all[b][:seq] for b in range(B)]).reshape(B, seq, NH, HD)
    k_s = np.concatenate([samp_rows(k_all[c]) for c in range(n_cores)])[:DB].reshape(DB, DEC, NH, HD)
    v_p = np.stack([R[b]["o_v"][:seq] for b in range(B)]).reshape(B, seq, NH, HD)
    v_s = np.concatenate([samp_rows(R[c]["o_v"]) for c in range(n_cores)])[:DB].reshape(DB, DEC, NH, HD)
    lf_p = np.stack([R[b]["o_lf"][:seq] for b in range(B)])
    lf_s = np.concatenate([samp_rows(R[c]["o_lf"]) for c in range(n_cores)])[:DB]
    f32 = lambda a: np.ascontiguousarray(a, dtype=np.float32)
    return (f32(y_p), f32(y_s), f32(wkv_p), f32(sh_p), f32(k_p), f32(v_p), f32(lf_p),
            f32(wkv_s), f32(sh_s), f32(k_s), f32(v_s), f32(lf_s))
```
